# Optimizing a Trainium2 kernel written in Bass

```python
import math
import jax, jax.numpy as jnp
from jax import lax
import numpy as np

D_MODEL = 2048
BATCH = 4
SEQ = 2048
DEPTH = 2
DEC_BATCH = 128
DEC_SEQ = 4
PAST_LEN = 8192
PAGE_SIZE = 128

N_A_LAYERS = DEPTH // 2
N_B_LAYERS = DEPTH - N_A_LAYERS
EPS = 1e-6
GDN_QK_HEADS = 16
GDN_V_HEADS = 32
GDN_DK = 128
GDN_DV = 128
GDN_KEY_DIM = GDN_QK_HEADS * GDN_DK
GDN_VAL_DIM = GDN_V_HEADS * GDN_DV
CONV_W = 4
CONV_DIM = 2 * GDN_KEY_DIM + GDN_VAL_DIM
GDN_IN_DIM = CONV_DIM + GDN_VAL_DIM + 2 * GDN_V_HEADS
GDN_CHUNK = 64
MLA_HEADS = D_MODEL // 128
Q_LORA = 512
KV_LORA = 512
QK_NOPE = 128
QK_ROPE = 64
V_HEAD = 128
ROPE_THETA = 10000.0
ATTN_BLOCK = 128
D_FF = -(-8 * D_MODEL // (3 * 256)) * 256

kernel_name = "yoco_gdn_mla_adaln_step"


def rmsnorm(x, g):
    xf = x.astype(jnp.float32)
    y = xf * lax.rsqrt(jnp.mean(xf * xf, axis=-1, keepdims=True) + EPS)
    return (y * g.astype(jnp.float32)).astype(x.dtype)


def l2norm(x):
    xf = x.astype(jnp.float32)
    return xf * lax.rsqrt(jnp.sum(xf * xf, axis=-1, keepdims=True) + EPS)


def ada_mods(c, w, b, n):
    m = jax.nn.silu(c) @ w + b
    return [t[:, None, :] for t in jnp.split(m, n, axis=-1)]


def modulate(x, g, shift, scale):
    return rmsnorm(x, g) * (1.0 + scale) + shift


def rope(x, pos):
    half = QK_ROPE // 2
    inv = ROPE_THETA ** (-jnp.arange(half, dtype=jnp.float32) / half)
    ang = pos.astype(jnp.float32)[:, None] * inv[None, :]
    shape = (1, pos.shape[0]) + (1,) * (x.ndim - 3) + (half,)
    cos = jnp.cos(ang).reshape(shape)
    sin = jnp.sin(ang).reshape(shape)
    xf = x.astype(jnp.float32)
    x1, x2 = xf[..., :half], xf[..., half:]
    return jnp.concatenate([x1 * cos - x2 * sin, x2 * cos + x1 * sin], axis=-1).astype(x.dtype)


def swiglu(h, w_gate_up, w_down):
    gate, up = jnp.split(h @ w_gate_up, 2, axis=-1)
    return (jax.nn.silu(gate) * up) @ w_down


def causal_conv(u, conv_prev, w):
    up = jnp.concatenate([conv_prev.astype(u.dtype), u], axis=1)
    L = u.shape[1]
    out = up[:, 0:L] * w[0]
    for j in range(1, CONV_W):
        out = out + up[:, j:j + L] * w[j]
    return jax.nn.silu(out), up[:, -(CONV_W - 1):]


def gated_delta_chunked(q, k, v, g, beta, s0):
    B, L, H, DK = q.shape
    C = GDN_CHUNK
    n = -(-L // C)
    pad = n * C - L

    def prep(t):
        t = jnp.pad(t, [(0, 0), (0, pad)] + [(0, 0)] * (t.ndim - 2))
        t = jnp.moveaxis(t, 2, 1)
        return t.reshape(t.shape[:2] + (n, C) + t.shape[3:])

    q, k, v, g, beta = [prep(t) for t in (q, k, v, g, beta)]
    q = q * (GDN_DK ** -0.5)
    gc = jnp.cumsum(g, axis=-1)
    idx = jnp.arange(C)
    causal = idx[:, None] >= idx[None, :]
    strict = idx[:, None] > idx[None, :]
    decay = jnp.exp(jnp.where(causal, gc[..., :, None] - gc[..., None, :], -jnp.inf))
    kb = k * beta[..., None]
    kk = jnp.einsum('bhncd,bhnsd->bhncs', kb, k)
    a_mat = jnp.eye(C, dtype=jnp.float32) + jnp.where(strict, kk * decay, 0.0)
    t_inv = lax.linalg.triangular_solve(a_mat, jnp.broadcast_to(jnp.eye(C, dtype=jnp.float32), a_mat.shape),
                                        left_side=True, lower=True, unit_diagonal=True)
    u = t_inv @ (v * beta[..., None])
    w = t_inv @ (kb * jnp.exp(gc)[..., None])
    qk = jnp.einsum('bhncd,bhnsd->bhncs', q, k) * decay

    def step(S, xs):
        q_c, k_c, u_c, w_c, g_c, qk_c = xs
        v_new = u_c - jnp.einsum('bhck,bhkv->bhcv', w_c, S)
        o = (jnp.einsum('bhck,bhkv->bhcv', q_c * jnp.exp(g_c)[..., None], S)
             + jnp.einsum('bhcs,bhsv->bhcv', qk_c, v_new))
        g_last = g_c[..., -1:]
        S = (S * jnp.exp(g_last)[..., None]
             + jnp.einsum('bhck,bhcv->bhkv', k_c * jnp.exp(g_last - g_c)[..., None], v_new))
        return S, o

    xs = tuple(jnp.moveaxis(t, 2, 0) for t in (q, k, u, w, gc, qk))
    s_fin, o = lax.scan(step, s0, xs)
    o = jnp.moveaxis(o, 0, 2).reshape(B, H, n * C, -1)
    return jnp.moveaxis(o, 1, 2)[:, :L], s_fin


def gdn_mixer(h, conv_prev, s0, w_in, w_conv, a_log, dt_bias, g_norm, w_out):
    B, L, _ = h.shape
    proj = h @ w_in
    qkv, z, b, a = jnp.split(proj, [CONV_DIM, CONV_DIM + GDN_VAL_DIM, CONV_DIM + GDN_VAL_DIM + GDN_V_HEADS], axis=-1)
    qkv, conv_new = causal_conv(qkv, conv_prev, w_conv)
    q, k, v = jnp.split(qkv, [GDN_KEY_DIM, 2 * GDN_KEY_DIM], axis=-1)
    rep = GDN_V_HEADS // GDN_QK_HEADS
    q = jnp.repeat(l2norm(q.reshape(B, L, GDN_QK_HEADS, GDN_DK)), rep, axis=2)
    k = jnp.repeat(l2norm(k.reshape(B, L, GDN_QK_HEADS, GDN_DK)), rep, axis=2)
    v = v.reshape(B, L, GDN_V_HEADS, GDN_DV).astype(jnp.float32)
    beta = jax.nn.sigmoid(b.astype(jnp.float32))
    g = -jnp.exp(a_log.astype(jnp.float32)) * jax.nn.softplus(a.astype(jnp.float32) + dt_bias.astype(jnp.float32))
    o, s_new = gated_delta_chunked(q, k, v, g, beta, s0.astype(jnp.float32))
    o = rmsnorm(o, g_norm) * jax.nn.silu(z.reshape(B, L, GDN_V_HEADS, GDN_DV).astype(jnp.float32))
    y = o.reshape(B, L, GDN_VAL_DIM).astype(h.dtype) @ w_out
    return y, conv_new, s_new.astype(s0.dtype)


def mla_shared_kv(x, c, pos, w_ada, b_ada, g_in, w_down, g_kv):
    shift, scale = ada_mods(c, w_ada, b_ada, 2)
    hn = modulate(x, g_in, shift, scale)
    ckv, kpe = jnp.split(hn @ w_down, [KV_LORA], axis=-1)
    return rmsnorm(ckv, g_kv), rope(kpe, pos)


def mla_attend(q_lat, q_pe, ckv_new, kpe_new, ckv_past, kpe_past):
    B, T, H, R = q_lat.shape
    scale = (QK_NOPE + QK_ROPE) ** -0.5
    blk = ATTN_BLOCK if T % ATTN_BLOCK == 0 else T
    nb = T // blk
    kpos = jnp.arange(T)

    def one_block(args):
        i0, ql, qp = args
        s = (jnp.einsum('bqhr,bkr->bhqk', ql, ckv_new)
             + jnp.einsum('bqhp,bkp->bhqk', qp, kpe_new)).astype(jnp.float32) * scale
        qpos = i0 + jnp.arange(blk)
        s = jnp.where(kpos[None, :] <= qpos[:, None], s, -jnp.inf)
        if ckv_past is None:
            p = jax.nn.softmax(s, axis=-1).astype(ckv_new.dtype)
            return jnp.einsum('bhqk,bkr->bqhr', p, ckv_new)
        sp = (jnp.einsum('bqhr,bkr->bhqk', ql, ckv_past)
              + jnp.einsum('bqhp,bkp->bhqk', qp, kpe_past)).astype(jnp.float32) * scale
        P = ckv_past.shape[1]
        p = jax.nn.softmax(jnp.concatenate([sp, s], axis=-1), axis=-1).astype(ckv_new.dtype)
        return (jnp.einsum('bhqk,bkr->bqhr', p[..., :P], ckv_past)
                + jnp.einsum('bhqk,bkr->bqhr', p[..., P:], ckv_new))

    qlb = jnp.moveaxis(q_lat.reshape(B, nb, blk, H, R), 1, 0)
    qpb = jnp.moveaxis(q_pe.reshape(B, nb, blk, H, QK_ROPE), 1, 0)
    o = lax.map(one_block, (jnp.arange(nb) * blk, qlb, qpb))
    return jnp.moveaxis(o, 0, 1).reshape(B, T, H, R)


def mla_mixer(h, pos, ckv, kpe, ckv_past, kpe_past, w_dq, g_q, w_uq, w_uk, w_uv, w_o):
    B, L, _ = h.shape
    cq = rmsnorm(h @ w_dq, g_q)
    qf = (cq @ w_uq).reshape(B, L, MLA_HEADS, QK_NOPE + QK_ROPE)
    q_nope = qf[..., :QK_NOPE]
    q_pe = rope(qf[..., QK_NOPE:], pos)
    q_lat = jnp.einsum('blhd,rhd->blhr', q_nope, w_uk.reshape(KV_LORA, MLA_HEADS, QK_NOPE))
    o_lat = mla_attend(q_lat, q_pe, ckv, kpe, ckv_past, kpe_past)
    o = jnp.einsum('blhr,rhv->blhv', o_lat, w_uv.reshape(KV_LORA, MLA_HEADS, V_HEAD))
    return o.reshape(B, L, MLA_HEADS * V_HEAD) @ w_o


def trunk(x, c, pos, conv_in, ssm_in, ckv_past, kpe_past, p):
    conv_out, ssm_out = [], []
    ckv = kpe = None
    for l in range(DEPTH):
        sh1, sc1, gt1, sh2, sc2, gt2 = ada_mods(c, p['w_ada'][l], p['b_ada'][l], 6)
        h = modulate(x, p['g_mix'][l], sh1, sc1)
        if l < N_A_LAYERS:
            y, cs, ss = gdn_mixer(h, conv_in[l], ssm_in[l], p['gdn_w_in'][l], p['gdn_w_conv'][l],
                                  p['gdn_a_log'][l], p['gdn_dt_bias'][l], p['gdn_g_norm'][l], p['gdn_w_out'][l])
            conv_out.append(cs)
            ssm_out.append(ss)
        else:
            j = l - N_A_LAYERS
            y = mla_mixer(h, pos, ckv, kpe, ckv_past, kpe_past, p['mla_w_dq'][j], p['mla_g_q'][j],
                          p['mla_w_uq'][j], p['kv_w_uk'], p['kv_w_uv'], p['mla_w_o'][j])
        x = x + gt1 * y
        h = modulate(x, p['g_ffn'][l], sh2, sc2)
        x = x + gt2 * swiglu(h, p['w_gate_up'][l], p['w_down'][l])
        if l == N_A_LAYERS - 1:
            ckv, kpe = mla_shared_kv(x, c, pos, p['kv_w_ada'], p['kv_b_ada'], p['kv_g_in'],
                                     p['kv_w_down'], p['kv_g_norm'])
    shf, scf = ada_mods(c, p['final_w_ada'], p['final_b_ada'], 2)
    y = modulate(x, p['final_g'], shf, scf)
    return y, jnp.stack(conv_out), jnp.stack(ssm_out), ckv, kpe


def setup_inputs(seed: int = 0) -> dict:
    key = jax.random.key(seed)
    ks = iter(jax.random.split(key, 48))

    def nrm(shape, s):
        return s * jax.random.normal(next(ks), shape, jnp.float32)

    def gain(shape):
        return 1.0 + nrm(shape, 0.05)

    D = D_MODEL
    n_pages = PAST_LEN // PAGE_SIZE
    n_used = DEC_BATCH * n_pages
    n_phys = n_used + -(-n_used // 4)
    page_table = jax.random.permutation(next(ks), n_phys)[:n_used].reshape(DEC_BATCH, n_pages).astype(jnp.int32)
    dt = jax.random.uniform(next(ks), (N_A_LAYERS, GDN_V_HEADS), jnp.float32, 0.001, 0.1)
    a_log = jnp.log(jax.random.uniform(next(ks), (N_A_LAYERS, GDN_V_HEADS), jnp.float32, 1.0, 16.0))
    ada_s = 0.5 * D ** -0.5
    return {
        "x_prompt": nrm((BATCH, SEQ, D), 1.0),
        "x_sample": nrm((DEC_BATCH, DEC_SEQ, D), 1.0),
        "c_prompt": nrm((BATCH, D), 1.0),
        "c_sample": nrm((DEC_BATCH, D), 1.0),
        "cache_ckv": nrm((n_phys, PAGE_SIZE, KV_LORA), 1.0),
        "cache_kpe": nrm((n_phys, PAGE_SIZE, QK_ROPE), 1.0),
        "page_table": page_table,
        "state_ssm": nrm((N_A_LAYERS, DEC_BATCH, GDN_V_HEADS, GDN_DK, GDN_DV), GDN_DK ** -0.5),
        "state_conv": nrm((N_A_LAYERS, DEC_BATCH, CONV_W - 1, CONV_DIM), 1.0),
        "w_ada": nrm((DEPTH, D, 6 * D), ada_s),
        "b_ada": nrm((DEPTH, 6 * D), 0.01),
        "g_mix": gain((DEPTH, D)),
        "g_ffn": gain((DEPTH, D)),
        "w_gate_up": nrm((DEPTH, D, 2 * D_FF), D ** -0.5),
        "w_down": nrm((DEPTH, D_FF, D), D_FF ** -0.5),
        "gdn_w_in": nrm((N_A_LAYERS, D, GDN_IN_DIM), D ** -0.5),
        "gdn_w_conv": nrm((N_A_LAYERS, CONV_W, CONV_DIM), CONV_W ** -0.5),
        "gdn_a_log": a_log,
        "gdn_dt_bias": jnp.log(jnp.expm1(dt)),
        "gdn_g_norm": gain((N_A_LAYERS, GDN_DV)),
        "gdn_w_out": nrm((N_A_LAYERS, GDN_VAL_DIM, D), GDN_VAL_DIM ** -0.5),
        "kv_w_ada": nrm((D, 2 * D), ada_s),
        "kv_b_ada": nrm((2 * D,), 0.01),
        "kv_g_in": gain((D,)),
        "kv_w_down": nrm((D, KV_LORA + QK_ROPE), D ** -0.5),
        "kv_g_norm": gain((KV_LORA,)),
        "kv_w_uk": nrm((KV_LORA, MLA_HEADS * QK_NOPE), KV_LORA ** -0.5),
        "kv_w_uv": nrm((KV_LORA, MLA_HEADS * V_HEAD), KV_LORA ** -0.5),
        "mla_w_dq": nrm((N_B_LAYERS, D, Q_LORA), D ** -0.5),
        "mla_g_q": gain((N_B_LAYERS, Q_LORA)),
        "mla_w_uq": nrm((N_B_LAYERS, Q_LORA, MLA_HEADS * (QK_NOPE + QK_ROPE)), Q_LORA ** -0.5),
        "mla_w_o": nrm((N_B_LAYERS, MLA_HEADS * V_HEAD, D), (MLA_HEADS * V_HEAD) ** -0.5),
        "final_w_ada": nrm((D, 2 * D), ada_s),
        "final_b_ada": nrm((2 * D,), 0.01),
        "final_g": gain((D,)),
    }


def reference(x_prompt, x_sample, c_prompt, c_sample, cache_ckv, cache_kpe, page_table, state_ssm, state_conv,
              w_ada, b_ada, g_mix, g_ffn, w_gate_up, w_down,
              gdn_w_in, gdn_w_conv, gdn_a_log, gdn_dt_bias, gdn_g_norm, gdn_w_out,
              kv_w_ada, kv_b_ada, kv_g_in, kv_w_down, kv_g_norm, kv_w_uk, kv_w_uv,
              mla_w_dq, mla_g_q, mla_w_uq, mla_w_o, final_w_ada, final_b_ada, final_g):
    p = dict(w_ada=w_ada, b_ada=b_ada, g_mix=g_mix, g_ffn=g_ffn, w_gate_up=w_gate_up, w_down=w_down,
             gdn_w_in=gdn_w_in, gdn_w_conv=gdn_w_conv, gdn_a_log=gdn_a_log, gdn_dt_bias=gdn_dt_bias,
             gdn_g_norm=gdn_g_norm, gdn_w_out=gdn_w_out, kv_w_ada=kv_w_ada, kv_b_ada=kv_b_ada,
             kv_g_in=kv_g_in, kv_w_down=kv_w_down, kv_g_norm=kv_g_norm, kv_w_uk=kv_w_uk, kv_w_uv=kv_w_uv,
             mla_w_dq=mla_w_dq, mla_g_q=mla_g_q, mla_w_uq=mla_w_uq, mla_w_o=mla_w_o,
             final_w_ada=final_w_ada, final_b_ada=final_b_ada, final_g=final_g)
    B, S, _ = x_prompt.shape
    conv0 = jnp.zeros((N_A_LAYERS, B, CONV_W - 1, CONV_DIM), x_prompt.dtype)
    ssm0 = jnp.zeros((N_A_LAYERS, B, GDN_V_HEADS, GDN_DK, GDN_DV), state_ssm.dtype)
    y_prompt, conv_prompt, ssm_prompt, ckv_prompt, kpe_prompt = trunk(
        x_prompt, c_prompt, jnp.arange(S), conv0, ssm0, None, None, p)
    Bd, T, _ = x_sample.shape
    past_len = page_table.shape[1] * cache_ckv.shape[1]
    ckv_past = cache_ckv[page_table].reshape(Bd, past_len, KV_LORA)
    kpe_past = cache_kpe[page_table].reshape(Bd, past_len, QK_ROPE)
    y_sample, conv_sample, ssm_sample, ckv_sample, kpe_sample = trunk(
        x_sample, c_sample, past_len + jnp.arange(T), state_conv, state_ssm, ckv_past, kpe_past, p)
    return (y_prompt, y_sample, ssm_prompt, conv_prompt, ckv_prompt, kpe_prompt,
            ssm_sample, conv_sample, ckv_sample, kpe_sample)
```

```python
import os
import numpy as np
import concourse.bass as bass
import concourse.mybir as mybir

F32 = mybir.dt.float32
BF16 = mybir.dt.bfloat16
I32 = mybir.dt.int32
U32 = mybir.dt.uint32
AF = mybir.ActivationFunctionType
ALU = mybir.AluOpType
AX = mybir.AxisListType
DTSZ = {F32: 4, BF16: 2, I32: 4, U32: 4}


class Buf:
    __slots__ = ("w", "r", "name")

    def __init__(self, name=""):
        self.w = {}
        self.r = {}
        self.name = name


class V:
    __slots__ = ("ap", "bufs")

    def __init__(self, ap, bufs):
        self.ap = ap
        self.bufs = bufs

    def __getitem__(self, key):
        return V(self.ap[key], self.bufs)

    def bc(self, shape):
        return V(self.ap.to_broadcast(list(shape)), self.bufs)

    def un(self, axis):
        return V(self.ap.unsqueeze(axis), self.bufs)

    def re(self, s, **kw):
        return V(self.ap.rearrange(s, **kw), self.bufs)

    def bitcast(self, dt):
        return V(self.ap.bitcast(dt), self.bufs)

    @property
    def shape(self):
        return tuple(self.ap.shape)


class Tens:
    def __init__(self, handle, shape, nsplit=1, name=""):
        self.h = handle
        self.shape = tuple(shape)
        self.nsplit = nsplit
        self.bufs = [Buf(f"{name}.{i}") for i in range(nsplit)]
        self.name = name

    def all(self):
        return V(self.h.ap() if hasattr(self.h, "ap") and callable(getattr(self.h, "ap")) else self.h[:], list(self.bufs))

    def __getitem__(self, key):
        ap = self.h[key]
        if self.nsplit == 1:
            return V(ap, self.bufs)
        k1 = key[1] if isinstance(key, tuple) and len(key) > 1 else slice(None)
        if isinstance(k1, int):
            per = self.shape[1] // self.nsplit
            return V(ap, [self.bufs[k1 // per]])
        if isinstance(k1, slice):
            per = self.shape[1] // self.nsplit
            a = 0 if k1.start is None else k1.start
            b = self.shape[1] if k1.stop is None else k1.stop
            return V(ap, self.bufs[a // per:(b - 1) // per + 1])
        return V(ap, list(self.bufs))


class K:
    def __init__(self, needed=None, same_engine_sync=True):
        import bisect
        self._bisect = bisect
        self.needed = needed
        self.used = {e: set() for e in ("pe", "dve", "act", "pool")}
        self.nc = bass.Bass("TRN2", target_bir_lowering=False)
        nc = self.nc
        self.E = {"pe": nc.tensor, "dve": nc.vector, "act": nc.scalar, "pool": nc.gpsimd, "sp": nc.sync}
        self.esem = {e: nc.alloc_semaphore(f"es_{e}") for e in ("pe", "dve", "act", "pool")}
        self.ecnt = {e: 0 for e in self.esem}
        self.seen = {e: {} for e in self.E}
        self.same = same_engine_sync
        self.dpool = {}
        for q, n in (("sp", 24), ("pool", 24), ("act", 8)):
            self.dpool[q] = [[nc.alloc_semaphore(f"ds_{q}{i}"), 0] for i in range(n)]
        self.dnext = {q: 0 for q in self.dpool}
        self.stores = []
        self.semname = {}
        self.ninst = 0
        self.nincs = 0
        self.log = []
        self.sem2eng = {id(s): e for e, s in self.esem.items()}
        self.needed_set = {e: set(v) for e, v in needed.items()} if needed is not None else None
        self.sb_off = 16512
        self.sb_cap = 229376
        self.sb_peak = 0
        self.uid = 0
        self.psb = []
        for i in range(8):
            h = nc.alloc_psum_tensor(f"psb{i}", [128, 512], F32)
            self.psb.append(Tens(h, [128, 512], 1, f"psb{i}"))
        self.psn = 0
        self.ps_rot = list(range(8))
        self.dram_in = {}
        self.dram_out = {}

    def sb(self, shape, dtype=F32, nsplit=1, name=None):
        self.uid += 1
        name = f"{name or 't'}_{self.uid}"
        per = int(np.prod(shape[1:])) * DTSZ[dtype]
        per = (per + 63) // 64 * 64
        off = self.sb_off
        if off + per > self.sb_cap:
            raise RuntimeError(f"SBUF arena overflow allocating {name} {shape}: off={off} per={per}")
        h = self.nc.alloc_sbuf_tensor_at(name, list(shape), dtype, offset=off)
        self.sb_off += per
        self.sb_peak = max(self.sb_peak, self.sb_off)
        return Tens(h, shape, nsplit, name)

    def mark(self):
        return self.sb_off

    def release(self, m):
        self.barrier()
        self.sb_off = m

    def ps(self):
        t = self.psb[self.ps_rot[self.psn % len(self.ps_rot)]]
        self.psn += 1
        return t

    def din(self, name, shape, dtype=F32):
        h = self.nc.dram_tensor(name, list(shape), dtype, kind="ExternalInput")
        self.dram_in[name] = (tuple(shape), dtype)
        return Tens(h, shape, 1, name)

    def dout(self, name, shape, dtype=F32):
        h = self.nc.dram_tensor(name, list(shape), dtype, kind="ExternalOutput")
        self.dram_out[name] = (tuple(shape), dtype)
        return Tens(h, shape, 1, name)

    def dscratch(self, name, shape, dtype=F32, nsplit=1):
        h = self.nc.dram_tensor(name, list(shape), dtype, kind="Internal")
        return Tens(h, shape, nsplit, name)

    def _deps(self, reads, writes):
        d = {}
        for v in reads:
            if isinstance(v, V):
                for b in v.bufs:
                    for s, val in b.w.items():
                        if d.get(s, 0) < val:
                            d[s] = val
        for v in writes:
            for b in v.bufs:
                for dd in (b.w, b.r):
                    for s, val in dd.items():
                        if d.get(s, 0) < val:
                            d[s] = val
        return d

    def _wait(self, e, deps, skip_own=False):
        eng = self.E[e]
        seen = self.seen[e]
        own = self.esem.get(e)
        for s, val in deps.items():
            if s is own and (skip_own or not self.same):
                continue
            if seen.get(s, 0) < val:
                sv = self._semval(s, val)
                eng.wait_ge(s, sv)
                seen[s] = val
                self.log.append((e, "wait", s, sv))

    def _semval(self, s, val):
        en = self.sem2eng.get(id(s))
        if en is None:
            return val
        self.used[en].add(val)
        if self.needed is None:
            return val
        lst = self.needed[en]
        i = self._bisect.bisect_left(lst, val)
        assert i < len(lst) and lst[i] == val, (en, val)
        return i + 1

    def _record(self, reads, writes, ev):
        s, val = ev
        for v in reads:
            if isinstance(v, V):
                for b in v.bufs:
                    if b.r.get(s, 0) < val:
                        b.r[s] = val
        for v in writes:
            for b in v.bufs:
                b.w = {s: val}
                b.r = {}

    def op(self, e, fn, reads, writes, pe_acc=False):
        deps = self._deps(reads, writes)
        if pe_acc:
            self._wait(e, deps, skip_own=True)
        else:
            self._wait(e, deps, skip_own=(e == "pe"))
        ins = fn(self.E[e])
        self.ecnt[e] += 1
        if self.needed is None or self.ecnt[e] in self.needed_set[e]:
            ins.then_inc(self.esem[e], 1)
            self.log.append((e, "inc", self.esem[e], 1))
            self.nincs += 1
        ev = (self.esem[e], self.ecnt[e])
        self._record(reads, writes, ev)
        self.ninst += 1
        return ev

    def dma(self, out, in_, q="sp", store=False, **kw):
        deps = self._deps([in_], [out])
        pool = self.dpool[q]
        j = self.dnext[q] % len(pool)
        self.dnext[q] += 1
        sem, cnt = pool[j]
        if cnt > 0:
            deps[sem] = max(deps.get(sem, 0), cnt)
        self._wait(q, deps)
        ins = self.E[q].dma_start(out=out.ap, in_=in_.ap, **kw)
        ins.then_inc(sem, 16)
        self.log.append((q, "inc", sem, 16))
        pool[j][1] = cnt + 16
        ev = (sem, cnt + 16)
        self._record([in_], [out], ev)
        if store:
            self.stores.append(ev)
        self.ninst += 1
        return ev

    def idma(self, out, in_full, idx, q="pool"):
        deps = self._deps([idx], [out])
        pool = self.dpool[q]
        j = self.dnext[q] % len(pool)
        self.dnext[q] += 1
        sem, cnt = pool[j]
        if cnt > 0:
            deps[sem] = max(deps.get(sem, 0), cnt)
        self._wait(q, deps)
        ins = self.E[q].indirect_dma_start(out=out.ap, out_offset=None, in_=in_full.ap,
                                           in_offset=bass.IndirectOffsetOnAxis(ap=idx.ap, axis=0))
        ins.then_inc(sem, 16)
        self.log.append((q, "inc", sem, 16))
        pool[j][1] = cnt + 16
        ev = (sem, cnt + 16)
        self._record([idx], [out], ev)
        self.ninst += 1
        return ev

    def barrier(self):
        deps = {self.esem[e]: self.ecnt[e] for e in self.esem if self.ecnt[e] > 0}
        for s, val in self.stores:
            deps[s] = max(deps.get(s, 0), val)
        for q in self.dpool:
            for sem, cnt in self.dpool[q]:
                if cnt > 0:
                    deps[sem] = max(deps.get(sem, 0), cnt)
        self.stores = []
        for e in ("pe", "dve", "act", "pool", "sp"):
            eng = self.E[e]
            seen = self.seen[e]
            for s, val in deps.items():
                if seen.get(s, 0) < val:
                    sv = self._semval(s, val)
                    eng.wait_ge(s, sv)
                    seen[s] = val
                    self.log.append((e, "wait", s, sv))

    def finish(self):
        self.barrier()

    def mm(self, out, lhsT, rhs, start=True, stop=True):
        return self.op("pe", lambda e: e.matmul(out.ap, lhsT.ap, rhs.ap, start=start, stop=stop),
                       [lhsT, rhs], [out], pe_acc=not start)

    def tr(self, out, in_, ident):
        return self.op("pe", lambda e: e.transpose(out.ap, in_.ap, ident.ap), [in_, ident], [out])

    def act(self, out, in_, func, bias=0.0, scale=1.0, accum=None, e="act"):
        reads = [in_] + [x for x in (bias, scale) if isinstance(x, V)]
        writes = [out] + ([accum] if accum is not None else [])
        b = bias.ap if isinstance(bias, V) else bias
        s = scale.ap if isinstance(scale, V) else scale
        kw = {}
        if accum is not None:
            kw["accum_out"] = accum.ap
        return self.op(e, lambda g: g.activation(out=out.ap, in_=in_.ap, func=func, bias=b, scale=s, **kw),
                       reads, writes)

    def tt(self, out, a, b, op, e="dve"):
        return self.op(e, lambda g: g.tensor_tensor(out.ap, a.ap, b.ap, op), [a, b], [out])

    def ts(self, out, a, s1, op0, s2=None, op1=None, e="dve", accum=None):
        reads = [a] + [x for x in (s1, s2) if isinstance(x, V)]
        x1 = s1.ap if isinstance(s1, V) else s1
        x2 = s2.ap if isinstance(s2, V) else s2
        kw = {}
        if op1 is not None:
            kw["op1"] = op1
        writes = [out]
        if accum is not None:
            kw["accum_out"] = accum.ap
            writes.append(accum)
        return self.op(e, lambda g: g.tensor_scalar(out.ap, a.ap, x1, x2, op0, **kw), reads, writes)

    def stt(self, out, in0, scalar, in1, op0, op1, e="dve"):
        reads = [in0, in1] + ([scalar] if isinstance(scalar, V) else [])
        sc = scalar.ap if isinstance(scalar, V) else scalar
        return self.op("dve", lambda g: g.scalar_tensor_tensor(out.ap, in0.ap, sc, in1.ap, op0, op1), reads, [out])

    def copy(self, out, in_, e="dve"):
        if e == "act":
            return self.op(e, lambda g: g.copy(out.ap, in_.ap), [in_], [out])
        return self.op(e, lambda g: g.tensor_copy(out.ap, in_.ap), [in_], [out])

    def memset(self, out, val, e="dve"):
        return self.op(e, lambda g: g.memset(out.ap, val), [], [out])

    def reduce(self, out, in_, op=ALU.add, axis=AX.X, e="dve"):
        return self.op(e, lambda g: g.tensor_reduce(out.ap, in_.ap, axis, op), [in_], [out])

    def recip(self, out, in_):
        return self.op("dve", lambda g: g.reciprocal(out.ap, in_.ap), [in_], [out])


def simulate_log(log):
    streams = {}
    for it in log:
        streams.setdefault(it[0], []).append(it)
    pos = {e: 0 for e in streams}
    sem = {}
    progress = True
    while progress:
        progress = False
        for e, st in streams.items():
            while pos[e] < len(st):
                _, kind, s, val = st[pos[e]]
                if kind == "wait":
                    if sem.get(id(s), 0) >= val:
                        pos[e] += 1
                        progress = True
                    else:
                        break
                else:
                    sem[id(s)] = sem.get(id(s), 0) + val
                    pos[e] += 1
                    progress = True
    stuck = {e: (pos[e], len(st)) for e, st in streams.items() if pos[e] < len(st)}
    return stuck


def two_pass(build_fn):
    k1 = K()
    build_fn(k1)
    needed = {e: sorted(v) for e, v in k1.used.items()}
    k2 = K(needed=needed)
    r = build_fn(k2)
    return k2, r

D = 2048
NKC = 16
DFF = 5632
EPS = 1e-6
NHV = 32
NHK = 16
CONV_DIM_ = 8192
GIN = 12352
QL = 512
KVL = 512
ROPE = 64
NH = 16
SM_SCALE = (128 + 64) ** -0.5
NEG = -30000.0


class Ctx:
    pass


def load_const(k, dram, shape, dtype=F32, q="sp"):
    t = k.sb(list(shape), dtype)
    k.dma(t[:], dram[:], q=q)
    return t


class WS:
    def __init__(self, k, nelem):
        self.k = k
        self.n = nelem
        self.b = [k.sb([128, nelem], BF16, name="wbuf") for _ in range(2)]
        self.i = 0

    def get(self, nk, ncols):
        t = self.b[self.i % 2]
        self.i += 1
        assert nk * ncols <= self.n, (nk, ncols, self.n)
        return t[:, 0:nk * ncols].re("p (k n) -> p k n", n=ncols)

    def load(self, wd, r0, nk, colspecs, rows=128):
        tot = sum(n for _, n in colspecs)
        v = self.get(nk, tot)
        o = 0
        for c0, n in colspecs:
            src = wd[r0:r0 + nk * rows, c0:c0 + n].re("(k p) n -> p k n", p=rows)
            self.k.dma(v[0:rows, :, o:o + n], src, q="pool")
            o += n
        return v


def transpose_in(k, c, xd, row0, T, xT):
    ts_ = min(128, T)
    nsub = T // ts_
    m = k.mark()
    stg = [k.sb([128, D], F32, name="xstg") for _ in range(2)]
    for s in range(nsub):
        st = stg[s % 2]
        k.dma(st[0:ts_, :], xd[row0 + s * ts_: row0 + (s + 1) * ts_, :])
        for g in range(4):
            p = k.ps()
            for j in range(4):
                kc = g * 4 + j
                k.tr(p[:, j * ts_:(j + 1) * ts_], st[0:ts_, kc * 128:(kc + 1) * 128], c.ident[0:ts_, 0:ts_])
            src = p[:, 0:4 * ts_].re("p (j t) -> p j t", t=ts_)
            dst = xT[:, g * 4:(g + 1) * 4, s * ts_:(s + 1) * ts_]
            k.copy(dst, src, e="act" if g % 2 else "dve")
    k.release(m)


def transpose_out(k, c, srcfn, nkc, T, dd, row0, col0=0, rows=128):
    ts_ = min(128, T)
    nsub = T // ts_
    m = k.mark()
    stg = [k.sb([128, nkc * rows], F32, name="ostg") for _ in range(2)]
    for s in range(nsub):
        st = stg[s % 2]
        for g0 in range(0, nkc, 4):
            ng = min(4, nkc - g0)
            p = k.ps()
            for j in range(ng):
                src = srcfn(g0 + j)[:, s * ts_:(s + 1) * ts_]
                k.tr(p[0:ts_, j * rows:(j + 1) * rows], src, c.ident[0:rows, 0:rows])
            k.copy(st[0:ts_, g0 * rows:(g0 + ng) * rows], p[0:ts_, 0:ng * rows], e="act" if (g0 // 4) % 2 else "dve")
        k.dma(dd[row0 + s * ts_: row0 + (s + 1) * ts_, col0:col0 + nkc * rows], st[0:ts_, :], store=True)
    k.release(m)


def rstd_from_psum(k, out, ps, n, eps=EPS):
    k.ts(out, ps, 1.0 / n, ALU.mult, eps, ALU.add)
    k.act(out, out, AF.Sqrt)
    k.recip(out, out)


def bc3(v, nseq, tps):
    return v.un(2).bc([128, nseq, tps])


def v3(v, tps):
    return v.re("p (s t) -> p s t", t=tps)


def modnorm(k, c, xT, T, gs, sh, mcol, nseq, tps, hT):
    m = k.mark()
    sq = [k.sb([128, T], F32, name="sq") for _ in range(2)]
    pss = k.ps()
    for kc in range(NKC):
        s_ = sq[kc % 2]
        k.act(s_[:], xT[:, kc, :], AF.Square)
        k.mm(pss[:, 0:T], c.ones[:], s_[:], start=(kc == 0), stop=(kc == NKC - 1))
    rstd = k.sb([128, T], F32, name="rstd")
    rstd_from_psum(k, rstd[:], pss[:, 0:T], D)
    tmp = [k.sb([128, T], F32, name="mtmp") for _ in range(2)]
    for kc in range(NKC):
        t_ = tmp[kc % 2]
        e1 = "dve" if kc % 2 == 0 else "pool"
        k.tt(t_[:], xT[:, kc, :], rstd[:], ALU.mult, e=e1)
        k.tt(v3(t_[:], tps), v3(t_[:], tps), bc3(gs(kc), nseq, tps), ALU.mult, e=e1)
        k.tt(v3(hT[:, kc, :], tps), v3(t_[:], tps), bc3(sh(kc), nseq, tps), ALU.add, e=e1)
    k.release(m)


def ada_phase(k, c, wd, bd, N, name):
    nch = N // 128
    modsT = k.sb([128, nch, 17], F32, name=name)
    m = k.mark()
    bT = k.sb([128, nch], F32, name="bT")
    brow = k.sb([128, 128], F32, name="brow")
    k.dma(brow[0:nch, :], bd[:].re("(c p) -> c p", p=128))
    p = k.ps()
    k.tr(p[:, 0:nch], brow[0:nch, :], c.ident[0:nch, 0:nch])
    k.copy(bT[:], p[:, 0:nch])
    ws = WS(k, NKC * 512)
    nblk = N // 512
    nxt = ws.load(wd, 0, NKC, [(0, 512)])
    for b in range(nblk):
        w = nxt
        if b + 1 < nblk:
            nxt = ws.load(wd, 0, NKC, [((b + 1) * 512, 512)])
        p = k.ps()
        for sub in range(4):
            for kc in range(NKC):
                k.mm(p[:, sub * 17:(sub + 1) * 17], w[:, kc, sub * 128:(sub + 1) * 128], c.scT[:, kc, :],
                     start=(kc == 0), stop=(kc == NKC - 1))
        src = p[:, 0:4 * 17].re("p (j t) -> p j t", t=17)
        k.tt(modsT[:, b * 4:(b + 1) * 4, :], src, bT[:, b * 4:(b + 1) * 4].un(2).bc([128, 4, 17]), ALU.add)
    k.release(m)
    return modsT


def linear_fm(k, ws, wd, r0, nk, c0, ncols, blk, actfn, T, consume):
    nblk = (ncols + blk - 1) // blk
    def spec(b):
        return [(c0 + b * blk, min(blk, ncols - b * blk))]
    nxt = ws.load(wd, r0, nk, spec(0))
    for b in range(nblk):
        w = nxt
        if b + 1 < nblk:
            nxt = ws.load(wd, r0, nk, spec(b + 1))
        nc_ = min(blk, ncols - b * blk)
        for j in range(0, nc_, 128):
            mcols = min(128, nc_ - j)
            p = k.ps()
            for kc in range(nk):
                k.mm(p[0:mcols, 0:T], w[:, kc, j:j + mcols], actfn(kc), start=(kc == 0), stop=(kc == nk - 1))
            consume((b * blk + j) // 128, p)


def ffn_phase(k, c, L, xT, T, mods, mcol, nseq, tps, l):
    m = k.mark()
    hT = k.sb([128, NKC, T], BF16, nsplit=NKC, name="hT")
    k.ts(c.gs[:, :, :], mods[:, 4 * 16:5 * 16, :], 1.0, ALU.add)
    k.tt(c.gs[:, :, :], c.gs[:, :, :], c.gffn[:, l * 16:(l + 1) * 16].un(2).bc([128, 16, 17]), ALU.mult)
    modnorm(k, c, xT, T, lambda kc: c.gs[:, kc, mcol], lambda kc: mods[:, 3 * 16 + kc, mcol], mcol, nseq, tps, hT)
    ws = WS(k, NKC * 512)
    NQ = 2
    CQ = DFF // NQ
    nq = CQ // 128
    actT = k.sb([128, nq, T], BF16, nsplit=nq, name="actT")
    gsil = [k.sb([128, T], F32, name="gsil") for _ in range(2)]
    wgu = L.w_gate_up[l]
    wdn = L.w_down[l]
    for qd in range(NQ):
        nb = CQ // 256
        def ld(bi):
            col = qd * CQ + bi * 256
            return ws.load(wgu, 0, NKC, [(col, 256), (DFF + col, 256)])
        nxt = ld(0)
        for bi in range(nb):
            w = nxt
            if bi + 1 < nb:
                nxt = ld(bi + 1)
            for jj in range(2):
                j = bi * 2 + jj
                pg = k.ps()
                pu = k.ps()
                for kc in range(NKC):
                    k.mm(pg[:, 0:T], w[:, kc, jj * 128:(jj + 1) * 128], hT[:, kc, :], start=(kc == 0), stop=(kc == NKC - 1))
                for kc in range(NKC):
                    k.mm(pu[:, 0:T], w[:, kc, 256 + jj * 128:256 + (jj + 1) * 128], hT[:, kc, :], start=(kc == 0), stop=(kc == NKC - 1))
                g_ = gsil[j % 2]
                k.act(g_[:], pg[:, 0:T], AF.Silu)
                k.tt(actT[:, j, :], g_[:], pu[:, 0:T], ALU.mult)
        def cons(oc, p):
            resid_add(k, c, xT, oc, p, T, mods[:, 5 * 16 + oc, mcol], nseq, tps)
        linear_fm(k, ws, wdn, qd * CQ, nq, 0, D, 256, lambda kc: actT[:, kc, :], T, cons)
    k.release(m)


def resid_add(k, c, xT, oc, p, T, gate, nseq, tps):
    if nseq == 1:
        k.stt(xT[:, oc, :], p[:, 0:T], gate, xT[:, oc, :], ALU.mult, ALU.add)
    else:
        t_ = c.rtmp[c.rti % 2]
        c.rti += 1
        k.tt(v3(t_[:, 0:T], tps), v3(p[:, 0:T], tps), bc3(gate, nseq, tps), ALU.mult)
        k.tt(xT[:, oc, :], xT[:, oc, :], t_[:, 0:T], ALU.add, e="pool")


def gdn_phase(k, c, L, xT, T, C, mods, mcol, nseq, tps, G, st, samp):
    nch = T // C
    nlev = 1 if samp else getattr(c, 'nlev', 6)
    m = k.mark()
    k.ps_rot = [2, 3, 4, 5, 6, 7]
    hT = k.sb([128, NKC, T], BF16, nsplit=NKC, name="hT")
    k.ts(c.gs[:, :, :], mods[:, 16:32, :], 1.0, ALU.add)
    k.tt(c.gs[:, :, :], c.gs[:, :, :], c.gmix[:, 0:16].un(2).bc([128, 16, 17]), ALU.mult)
    modnorm(k, c, xT, T, lambda kc: c.gs[:, kc, mcol], lambda kc: mods[:, kc, mcol], mcol, nseq, tps, hT)
    ogT = k.sb([128, NHV, T], BF16, nsplit=NHV, name="ogT")
    ws = WS(k, NKC * 512)
    wba = ws.load(L.gdn_w_in, 0, NKC, [(12288, 64)])
    beta = k.sb([128, nch, 32], F32, nsplit=nch, name="beta")
    gg = k.sb([128, nch, 32], F32, nsplit=nch, name="gg")
    negg = k.sb([128, nch, 32], F32, nsplit=nch, name="negg")
    gcs = k.sb([128, nch, 32], F32, nsplit=nch, name="gcs")
    edl = k.sb([128, nch, 32], F32, nsplit=nch, name="edl")
    bg = k.sb([128, nch, 32], F32, nsplit=nch, name="bg")
    t1 = k.sb([128, 32], F32, name="t1")
    t2 = k.sb([128, 32], F32, name="t2")
    for ch in range(nch):
        p = k.ps()
        for kc in range(NKC):
            k.mm(p[0:C, 0:64], hT[:, kc, ch * C:(ch + 1) * C], wba[:, kc, 0:64], start=(kc == 0), stop=(kc == NKC - 1))
        k.act(beta[0:C, ch, :], p[0:C, 0:32], AF.Sigmoid)
        k.tt(t1[0:C, :], p[0:C, 32:64], c.dtb[0:C, :], ALU.add)
        k.act(t2[0:C, :], t1[0:C, :], AF.Abs)
        k.act(t2[0:C, :], t2[0:C, :], AF.Exp, scale=-1.0)
        k.act(t2[0:C, :], t2[0:C, :], AF.Ln, bias=1.0)
        k.ts(t1[0:C, :], t1[0:C, :], 0.0, ALU.max)
        k.tt(t1[0:C, :], t1[0:C, :], t2[0:C, :], ALU.add)
        k.tt(gg[0:C, ch, :], t1[0:C, :], c.nega[0:C, :], ALU.mult)
        k.ts(negg[0:C, ch, :], gg[0:C, ch, :], -1.0, ALU.mult)
        p2 = k.ps()
        k.mm(p2[0:C, 0:32], G.U[0:C, 0:C], gg[0:C, ch, :])
        k.mm(p2[0:C, 32:64], G.OB[0:C, 0:C], gg[0:C, ch, :])
        k.copy(gcs[0:C, ch, :], p2[0:C, 0:32])
        k.tt(t1[0:C, :], p2[0:C, 32:64], gcs[0:C, ch, :], ALU.subtract)
        k.act(edl[0:C, ch, :], t1[0:C, :], AF.Exp)
        k.act(t2[0:C, :], gcs[0:C, ch, :], AF.Exp)
        k.tt(bg[0:C, ch, :], t2[0:C, :], beta[0:C, ch, :], ALU.mult)
    if getattr(c, "stage", 99) <= 2:
        c.dbg = [("beta", beta[:, :, :]), ("gg", gg[:, :, :]), ("gcs", gcs[:, :, :]), ("edl", edl[:, :, :]), ("hT0", None)]
        return
    cb = [k.sb([128, nseq, 3 + tps], F32, name="cb") for _ in range(2)]
    xc = [k.sb([128, T], F32, name="xc") for _ in range(4)]
    qTb = k.sb([128, T], BF16, name="qTb")
    kTb = k.sb([128, T], BF16, name="kTb")
    kn = k.sb([128, T], F32, name="kn")
    zs = [k.sb([128, T], BF16, name="zs") for _ in range(2)]
    sqt = k.sb([128, T], F32, name="sqt")
    rn = k.sb([128, T], F32, name="rn")
    def f32t(n="ct"):
        return k.sb([128, 128], F32, name=n)
    rh = [f32t("rh") for _ in range(2)]
    tmpE = [f32t("tmpE") for _ in range(2)]
    E1 = [f32t("E1") for _ in range(2)]
    egrow = [f32t("egrow") for _ in range(2)]
    Am = [f32t("Am") for _ in range(4)]
    Bm = [f32t("Bm") for _ in range(4)]
    qk = [f32t("qk") for _ in range(2)]
    Rr = [f32t("Rr") for _ in range(4)]
    qkT = [k.sb([128, 128], BF16, name="qkT") for _ in range(2)]
    Rb = [k.sb([128, 128], BF16, name="Rb") for _ in range(2)]
    vb = [k.sb([128, 128], BF16, name="vb") for _ in range(2)]
    kbg = [k.sb([128, 128], BF16, name="kbg") for _ in range(2)]
    kd = [k.sb([128, 128], BF16, name="kd") for _ in range(2)]
    wTn = [k.sb([128, 128], BF16, name="wTn") for _ in range(2)]
    vn = [k.sb([128, 128], BF16, name="vn") for _ in range(2)]
    qg = [k.sb([128, 128], BF16, name="qg") for _ in range(2)]
    Sbh = [k.sb([128, 128], BF16, name="Sbh") for _ in range(2)]
    osb = [f32t("osb") for _ in range(2)]
    osq = [f32t("osq") for _ in range(2)]
    orn = [f32t("orn") for _ in range(2)]
    if samp:
        vnT = [f32t("vnT") for _ in range(2)]
        kds = [k.sb([128, 128], BF16, name="kds") for _ in range(2)]
        Sall = [k.sb([128, 16, 128], F32, name="Sall") for _ in range(2)]
        Sball = [k.sb([128, 16, 128], BF16, name="Sball") for _ in range(2)]
    cnt = 0
    for kh in range(NHK):
        w = ws.load(L.gdn_w_in, 0, NKC, [(kh * 128, 128), (2048 + kh * 128, 128), (4096 + kh * 256, 256)])
        wz = ws.load(L.gdn_w_in, 0, NKC, [(8192 + kh * 256, 256)])
        ids = [kh, 16 + kh, 32 + 2 * kh, 33 + 2 * kh]
        for ci in range(4):
            p = k.ps()
            for kc in range(NKC):
                k.mm(p[:, 0:T], w[:, kc, ci * 128:(ci + 1) * 128], hT[:, kc, :], start=(kc == 0), stop=(kc == NKC - 1))
            cid = ids[ci]
            cb_ = cb[ci % 2]
            k.copy(cb_[:, :, 3:3 + tps], v3(p[:, 0:T], tps), e="act")
            k.copy(cb_[:, :, 0:3], st.convtail[:, cid, :, :], e="pool")
            a_ = v3(xc[ci][:], tps)
            k.ts(a_, cb_[:, :, 0:tps], c.wconv[:, cid, 0:1], ALU.mult, e="pool")
            for j in range(1, 4):
                k.stt(a_, cb_[:, :, j:j + tps], c.wconv[:, cid, j:j + 1], a_, ALU.mult, ALU.add,
                      e="pool" if j % 2 else "dve")
            k.copy(st.convtail[:, cid, :, :], cb_[:, :, tps:tps + 3], e="pool")
            k.act(xc[ci][:], xc[ci][:], AF.Silu)
        for ci in range(2):
            k.tt(sqt[:], xc[ci][:], xc[ci][:], ALU.mult, e="pool")
            p = k.ps()
            k.mm(p[:, 0:T], c.ones[:], sqt[:])
            k.ts(rn[:], p[:, 0:T], EPS, ALU.add)
            k.act(rn[:], rn[:], AF.Sqrt)
            k.recip(rn[:], rn[:])
            if ci == 0:
                k.stt(qTb[:], xc[0][:], 128 ** -0.5, rn[:], ALU.mult, ALU.mult)
            else:
                k.tt(kn[:], xc[1][:], rn[:], ALU.mult)
                k.copy(kTb[:], kn[:], e="pool")
        for a in range(2):
            p = k.ps()
            for kc in range(NKC):
                k.mm(p[:, 0:T], wz[:, kc, a * 128:(a + 1) * 128], hT[:, kc, :], start=(kc == 0), stop=(kc == NKC - 1))
            k.act(zs[a][:], p[:, 0:T], AF.Silu)
        if getattr(c, "stage", 99) <= 3:
            c.dbg = [("xc0", xc[0][:]), ("xc2", xc[2][:]), ("kn", kn[:]), ("rn", rn[:])]
            return
        if samp:
            for a in range(2):
                hh = 2 * kh + a
                k.dma(Sall[a][:, :, :], st.ssm_in[:, hh, :, :].re("s p v -> p s v"))
                k.copy(Sball[a][:, :, :], Sall[a][:, :, :], e="pool")
        for ch in range(nch):
            cs = slice(ch * C, (ch + 1) * C)
            pG = k.psb[0]
            k.mm(pG[0:C, 0:C], kTb[:, cs], kTb[:, cs])
            k.mm(pG[0:C, C:2 * C], qTb[:, cs], kTb[:, cs])
            pT = k.psb[1]
            k.tr(pT[0:C, 0:128], kn[:, cs], c.ident[:])
            for a in range(2):
                k.tr(pT[0:C, 128 * (1 + a):128 * (2 + a)], xc[2 + a][:, cs], c.ident[:])
            for a in range(2):
                hh = 2 * kh + a
                i2 = cnt % 2
                cnt += 1
                hcol = slice(hh, hh + 1)
                k.ts(rh[i2][0:C, 0:C], G.U[0:C, 0:C], negg[0:C, ch, hcol], ALU.mult, e="pool")
                pE = k.ps()
                k.mm(pE[:, 0:C], c.ones[0:C, :], rh[i2][0:C, 0:C])
                k.tt(tmpE[i2][0:C, 0:C], pE[0:C, 0:C], G.Mn[0:C, 0:C], ALU.add)
                k.act(E1[i2][0:C, 0:C], tmpE[i2][0:C, 0:C], AF.Exp, bias=gcs[0:C, ch, hcol])
                k.act(egrow[i2][:, 0:C], pE[:, 0:C], AF.Exp, scale=-1.0)
                if getattr(c, "stage", 99) == 35:
                    c.dbg = [("E1", E1[i2][0:C, 0:C]), ("egrow", egrow[i2][:, 0:C]), ("tmpE", tmpE[i2][0:C, 0:C])]
                    return
                A0 = Am[0]
                k.stt(A0[0:C, 0:C], pG[0:C, 0:C], beta[0:C, ch, hcol], E1[i2][0:C, 0:C], ALU.mult, ALU.mult)
                k.tt(A0[0:C, 0:C], A0[0:C, 0:C], G.SL[0:C, 0:C], ALU.mult, e="pool")
                k.tt(qk[i2][0:C, 0:C], pG[0:C, C:2 * C], E1[i2][0:C, 0:C], ALU.mult)
                pB = k.ps()
                k.tr(pB[0:C, 0:C], A0[0:C, 0:C], c.ident[0:C, 0:C])
                k.tr(pB[0:C, C:2 * C], qk[i2][0:C, 0:C], c.ident[0:C, 0:C])
                B0 = Bm[0]
                k.copy(B0[0:C, 0:C], pB[0:C, 0:C], e="act")
                k.copy(qkT[i2][0:C, 0:C], pB[0:C, C:2 * C], e="act")
                R = Rr[0]
                k.tt(R[0:C, 0:C], c.ident[0:C, 0:C], B0[0:C, 0:C], ALU.subtract, e="pool")
                if getattr(c, "stage", 99) == 36:
                    c.dbg = [("R", R[0:C, 0:C]), ("A0", A0[0:C, 0:C]), ("B0", B0[0:C, 0:C]), ("qk", qk[i2][0:C, 0:C])]
                    return
                Ac, Bc = A0, B0
                ri = 0
                for l in range(nlev):
                    An = Am[(l + 1) % 4]
                    Bn = Bm[(l + 1) % 4]
                    pA = k.ps()
                    k.mm(pA[0:C, 0:C], Bc[0:C, 0:C], Ac[0:C, 0:C])
                    k.copy(An[0:C, 0:C], pA[0:C, 0:C], e="act")
                    if l < nlev - 1:
                        pA2 = k.ps()
                        k.mm(pA2[0:C, 0:C], Ac[0:C, 0:C], Bc[0:C, 0:C])
                        k.copy(Bn[0:C, 0:C], pA2[0:C, 0:C], e="dve")
                    if getattr(c, "skipR", 0) and l >= 1:
                        Ac, Bc = An, Bn
                        continue
                    pR = k.ps()
                    k.mm(pR[0:C, 0:C], An[0:C, 0:C], R[0:C, 0:C])
                    Rn = Rr[(ri + 1) % 4]
                    ri += 1
                    k.tt(Rn[0:C, 0:C], pR[0:C, 0:C], R[0:C, 0:C], ALU.add)
                    R = Rn
                    Ac, Bc = An, Bn
                k.copy(Rb[i2][0:C, 0:C], R[0:C, 0:C], e="pool")
                if getattr(c, "stage", 99) <= 4:
                    c.dbg = [("R", R[0:C, 0:C]), ("E1", E1[i2][0:C, 0:C]), ("A0", Am[0][0:C, 0:C]), ("B0", Bm[0][0:C, 0:C]), ("egrow", egrow[i2][:, 0:C])]
                    return
                k.ts(vb[i2][0:C, :], pT[0:C, 128 * (1 + a):128 * (2 + a)], beta[0:C, ch, hcol], ALU.mult)
                k.act(kbg[i2][0:C, :], pT[0:C, 0:128], AF.Copy, scale=bg[0:C, ch, hcol])
                k.act(kd[i2][0:C, :], pT[0:C, 0:128], AF.Copy, scale=edl[0:C, ch, hcol])
                pw = k.ps()
                k.mm(pw[:, 0:C], kbg[i2][0:C, :], Rb[i2][0:C, 0:C])
                k.act(wTn[i2][:, 0:C], pw[:, 0:C], AF.Copy, scale=-1.0)
                k.tt(qg[i2][:, 0:C], qTb[:, cs], egrow[i2][:, 0:C], ALU.mult, e="pool")
                po = k.ps()
                if not samp:
                    Sbt = Sbh[i2]
                    k.copy(Sbt[:, :], st.S[:, hh, :], e="pool")
                    Sb_h = Sbt[:, :]
                    pv = k.ps()
                    k.mm(pv[0:C, 0:128], Rb[i2][0:C, 0:C], vb[i2][0:C, :], start=True, stop=False)
                    k.mm(pv[0:C, 0:128], wTn[i2][:, 0:C], Sb_h, start=False, stop=True)
                    k.copy(vn[i2][0:C, :], pv[0:C, 0:128], e="act")
                    k.mm(po[:, 0:C], Sb_h, qg[i2][:, 0:C], start=True, stop=False)
                    k.mm(po[:, 0:C], vn[i2][0:C, :], qkT[i2][0:C, 0:C], start=False, stop=True)
                    k.copy(osb[i2][:, 0:C], po[:, 0:C], e="act")
                    pS = k.ps()
                    k.mm(pS[:, 0:128], kd[i2][0:C, :], vn[i2][0:C, :])
                    k.stt(st.S[:, hh, :], st.S[:, hh, :], egrow[i2][:, C - 1:C], pS[:, 0:128], ALU.mult, ALU.add)
                else:
                    pu = k.ps()
                    k.mm(pu[:, 0:C], vb[i2][0:C, :], Rb[i2][0:C, 0:C], start=True, stop=False)
                    for s in range(nseq):
                        k.mm(pu[:, s * tps:(s + 1) * tps], Sball[a][:, s, :], wTn[i2][:, s * tps:(s + 1) * tps],
                             start=False, stop=(s == nseq - 1))
                    k.copy(vnT[i2][:, 0:C], pu[:, 0:C], e="act")
                    pvt = k.ps()
                    k.tr(pvt[0:C, 0:128], vnT[i2][:, 0:C], c.ident[:])
                    k.copy(vn[i2][0:C, :], pvt[0:C, 0:128], e="act")
                    k.mm(po[:, 0:C], vn[i2][0:C, :], qkT[i2][0:C, 0:C], start=True, stop=False)
                    for s in range(nseq):
                        k.mm(po[:, s * tps:(s + 1) * tps], Sball[a][:, s, :], qg[i2][:, s * tps:(s + 1) * tps],
                             start=False, stop=(s == nseq - 1))
                    k.copy(osb[i2][:, 0:C], po[:, 0:C], e="act")
                    for s in range(nseq):
                        j2 = s % 2
                        k.ts(kds[j2][0:C, :], kd[i2][0:C, :], G.seqmask[0:C, s:s + 1], ALU.mult, e="pool")
                        pS = k.ps()
                        k.mm(pS[:, 0:128], kds[j2][0:C, :], vn[i2][0:C, :])
                        k.stt(Sall[a][:, s, :], Sall[a][:, s, :], egrow[i2][:, s * tps + tps - 1:s * tps + tps],
                              pS[:, 0:128], ALU.mult, ALU.add)
                k.tt(osq[i2][:, 0:C], osb[i2][:, 0:C], osb[i2][:, 0:C], ALU.mult, e="pool")
                pq = k.ps()
                k.mm(pq[:, 0:C], c.ones[:], osq[i2][:, 0:C])
                rstd_from_psum(k, orn[i2][:, 0:C], pq[:, 0:C], 128)
                k.tt(osb[i2][:, 0:C], osb[i2][:, 0:C], orn[i2][:, 0:C], ALU.mult)
                k.stt(ogT[:, hh, cs], osb[i2][:, 0:C], c.gnorm[:, 0:1], zs[a][:, cs], ALU.mult, ALU.mult)
                if getattr(c, "stage", 99) <= 5:
                    c.dbg = [("osb", osb[i2][:, 0:C]), ("S0", st.S[:, hh, :])]
                    return
        if samp:
            for a in range(2):
                hh = 2 * kh + a
                k.dma(st.ssm_out[:, hh, :, :].re("s p v -> p s v"), Sall[a][:, :, :], store=True)
    ws2 = ws
    def cons(oc, p):
        resid_add(k, c, xT, oc, p, T, mods[:, 32 + oc, mcol], nseq, tps)
    k.ps_rot = list(range(8))
    linear_fm(k, ws2, L.gdn_w_out, 0, 32, 0, D, 256, lambda kc: ogT[:, kc, :], T, cons)
    k.release(m)


class GC:
    pass


def make_consts():
    cst = {}
    cst["ident"] = np.eye(128, dtype=np.float32)
    cst["ones"] = np.ones((128, 128), np.float32)
    i = np.arange(128)
    cst["U_p"] = (i[:, None] <= i[None, :]).astype(np.float32)
    cst["OB_p"] = np.ones((128, 128), np.float32)
    cst["Mn_p"] = np.where(i[None, :] <= i[:, None], 0.0, NEG).astype(np.float32)
    cst["SL_p"] = (i[None, :] < i[:, None]).astype(np.float32)
    blk = i // 4
    same = blk[:, None] == blk[None, :]
    cst["U_s"] = (same & (i[:, None] <= i[None, :])).astype(np.float32)
    cst["OB_s"] = same.astype(np.float32)
    cst["Mn_s"] = np.where(same & (i[None, :] <= i[:, None]), 0.0, NEG).astype(np.float32)
    cst["SL_s"] = (same & (i[None, :] < i[:, None])).astype(np.float32)
    sm = np.zeros((128, 32), np.float32)
    sm[i, blk] = 1.0
    cst["seqmask"] = sm
    return cst


def setup_common(k, c, I):
    c.ident = load_const(k, I["ident"], [128, 128])
    c.ones = load_const(k, I["ones"], [128, 128])
    c.gs = k.sb([128, 16, 17], F32, name="gs")
    c.rtmp = [k.sb([128, 64], F32, name="rtmp") for _ in range(2)]
    c.rti = 0
    def colvec(dv, n, name):
        t = k.sb([128, n], F32, name=name)
        m = k.mark()
        row = k.sb([128, 128], F32, name="row")
        k.dma(row[0:n, :], dv.re("(c p) -> c p", p=128))
        p = k.ps()
        k.tr(p[:, 0:n], row[0:n, :], c.ident[0:n, 0:n])
        k.copy(t[:], p[:, 0:n])
        k.release(m)
        return t
    c.colvec = colvec
    c.scT = k.sb([128, NKC, 17], BF16, name="scT")
    m = k.mark()
    cc = k.sb([128, D], F32, name="cc")
    k.dma(cc[0:17, :], I["cc"][:, :])
    k.act(cc[0:17, :], cc[0:17, :], AF.Silu)
    p = k.ps()
    for kc in range(NKC):
        k.tr(p[:, kc * 17:(kc + 1) * 17], cc[0:17, kc * 128:(kc + 1) * 128], c.ident[0:17, 0:17])
    k.copy(c.scT[:, :, :], p[:, 0:NKC * 17].re("p (k t) -> p k t", t=17))
    k.release(m)


def setup_gdn(k, c, I):
    c.gmix = c.colvec(I["g_mix"][0:2 * D], 32, "gmix")
    c.gffn = c.colvec(I["g_ffn"][0:2 * D], 32, "gffn")
    c.gnorm = k.sb([128, 1], F32, name="gnorm")
    k.dma(c.gnorm[:, :], I["gdn_g_norm"][0:128].re("(p o) -> p o", o=1))
    c.dtb = k.sb([128, 32], F32, name="dtb")
    k.dma(c.dtb[:, :], I["gdn_dt_bias"][0:32].re("(o n) -> o n", o=1).bc([128, 32]))
    c.nega = k.sb([128, 32], F32, name="nega")
    k.dma(c.nega[:, :], I["gdn_a_log"][0:32].re("(o n) -> o n", o=1).bc([128, 32]))
    k.act(c.nega[:, :], c.nega[:, :], AF.Exp)
    k.ts(c.nega[:, :], c.nega[:, :], -1.0, ALU.mult)
    c.wconv = k.sb([128, 64, 4], F32, name="wconv")
    m = k.mark()
    wr = k.sb([128, 8192], F32, name="wr")
    k.dma(wr[0:4, :], I["gdn_w_conv"][:, :])
    for g in range(2):
        p = k.ps()
        for j in range(32):
            cid = g * 32 + j
            k.tr(p[:, j * 4:(j + 1) * 4], wr[0:4, cid * 128:(cid + 1) * 128], c.ident[0:4, 0:4])
        k.copy(c.wconv[:, g * 32:(g + 1) * 32, :], p[:, 0:128].re("p (c j) -> p c j", j=4))
    k.release(m)


def load_G(k, I, sfx):
    G = GC()
    G.U = load_const(k, I["U_" + sfx], [128, 128])
    G.OB = load_const(k, I["OB_" + sfx], [128, 128])
    G.Mn = load_const(k, I["Mn_" + sfx], [128, 128])
    G.SL = load_const(k, I["SL_" + sfx], [128, 128])
    if sfx == "s":
        G.seqmask = load_const(k, I["seqmask"], [128, 32])
    return G


def rope_tables():
    half = 32
    inv = np.power(np.float32(10000.0), -(np.arange(half, dtype=np.float32) / np.float32(half))).astype(np.float32)
    pos = np.concatenate([np.arange(2048), 8192 + np.arange(4)]).astype(np.float32)
    ang = (pos[None, :] * inv[:, None]).astype(np.float32)
    cs = np.cos(ang.astype(np.float64)).astype(np.float32)
    sn = np.sin(ang.astype(np.float64)).astype(np.float32)
    cos2 = np.concatenate([cs, cs], 0)
    sin_s = np.concatenate([-sn, sn], 0)
    cos2 = np.concatenate([cos2, np.tile(cos2[:, 2048:2052], (1, 16))], 1)
    sin_s = np.concatenate([sin_s, np.tile(sin_s[:, 2048:2052], (1, 16))], 1)
    return np.ascontiguousarray(cos2), np.ascontiguousarray(sin_s)


def load_rope(k, c, I, pos0, T):
    c.cos2 = k.sb([64, T], F32, name="cos2")
    c.sin_s = k.sb([64, T], F32, name="sin_s")
    k.dma(c.cos2[:, :], I["rope_cos"][:, pos0:pos0 + T])
    k.dma(c.sin_s[:, :], I["rope_sin"][:, pos0:pos0 + T])


def rope_apply(k, c, out, p1, p2, T, tmp):
    k.tt(tmp[0][0:64, 0:T], p1, c.cos2[:, 0:T], ALU.mult)
    k.tt(tmp[1][0:64, 0:T], p2, c.sin_s[:, 0:T], ALU.mult)
    k.tt(out, tmp[0][0:64, 0:T], tmp[1][0:64, 0:T], ALU.add, e="pool")


def kv_phase(k, c, L, xT, T, mods, mcol, nseq, tps, KB, tok0, ckv_out, kpe_out, row0):
    m = k.mark()
    hT = k.sb([128, NKC, T], BF16, nsplit=NKC, name="hT")
    k.ts(c.gs[:, :, :], mods[:, 16:32, :], 1.0, ALU.add)
    k.tt(c.gs[:, :, :], c.gs[:, :, :], c.gkv[:, 0:16].un(2).bc([128, 16, 17]), ALU.mult)
    modnorm(k, c, xT, T, lambda kc: c.gs[:, kc, mcol], lambda kc: mods[:, kc, mcol], mcol, nseq, tps, hT)
    ws = WS(k, NKC * 640)
    w = ws.load(L.kv_w_down, 0, NKC, [(0, 576), (544, 32), (512, 32)])
    ckf = k.sb([128, 4, T], F32, nsplit=4, name="ckf")
    sq = k.sb([128, T], F32, name="ksq")
    pss = k.psb[0]
    k.ps_rot = [1, 2, 3, 4, 5, 6, 7]
    for rc in range(4):
        p = k.ps()
        for kc in range(NKC):
            k.mm(p[:, 0:T], w[:, kc, rc * 128:(rc + 1) * 128], hT[:, kc, :], start=(kc == 0), stop=(kc == NKC - 1))
        k.copy(ckf[:, rc, :], p[:, 0:T], e="act")
        k.tt(sq[:], ckf[:, rc, :], ckf[:, rc, :], ALU.mult)
        k.mm(pss[:, 0:T], c.ones[:], sq[:], start=(rc == 0), stop=(rc == 3))
    rstd = k.sb([128, T], F32, name="krstd")
    rstd_from_psum(k, rstd[:], pss[:, 0:T], KVL)
    k.ps_rot = list(range(8))
    for rc in range(4):
        k.stt(ckf[:, rc, :], ckf[:, rc, :], c.gkvn[:, rc:rc + 1], rstd[:], ALU.mult, ALU.mult)
        k.copy(KB.ckvT(rc), ckf[:, rc, :], e="pool")
    import os
    kvstop = int(os.environ.get("KVSTOP", "9"))
    if kvstop <= 1:
        k.release(m)
        return
    p1 = k.ps()
    p2 = k.ps()
    for kc in range(NKC):
        k.mm(p1[0:64, 0:T], w[:, kc, 512:576], hT[:, kc, :], start=(kc == 0), stop=(kc == NKC - 1))
    for kc in range(NKC):
        k.mm(p2[0:64, 0:T], w[:, kc, 576:640], hT[:, kc, :], start=(kc == 0), stop=(kc == NKC - 1))
    kpf = k.sb([64, T], F32, name="kpf")
    tmp = [k.sb([64, T], F32, name="rtmpa"), k.sb([64, T], F32, name="rtmpb")]
    rope_apply(k, c, kpf[:, :], p1[0:64, 0:T], p2[0:64, 0:T], T, tmp)
    k.copy(KB.kpeT(), kpf[:, :], e="pool")
    if kvstop <= 2:
        k.release(m)
        return
    ts_ = min(128, T)
    nsub = T // ts_
    stg = [k.sb([128, 576], F32, name="kvstg") for _ in range(2)]
    for s in range(nsub):
        st = stg[s % 2]
        p = k.ps()
        for rc in range(4):
            k.tr(p[0:ts_, rc * 128:(rc + 1) * 128], ckf[:, rc, s * ts_:(s + 1) * ts_], c.ident[:, :])
        k.copy(st[0:ts_, 0:512], p[0:ts_, 0:512], e="act")
        if KB.ckv_tm is not None and kvstop != 3:
            k.copy(KB.ckv_tm(s), st[0:ts_, 0:512], e="pool")
        if kvstop >= 4:
            pk = k.ps()
            k.tr(pk[0:ts_, 0:64], kpf[:, s * ts_:(s + 1) * ts_], c.ident[0:64, 0:64])
            k.copy(st[0:ts_, 512:576], pk[0:ts_, 0:64], e="act")
        k.dma(ckv_out[row0 + s * ts_:row0 + (s + 1) * ts_, :], st[0:ts_, 0:512], store=True)
        if kvstop >= 5:
            k.dma(kpe_out[row0 + s * ts_:row0 + (s + 1) * ts_, :], st[0:ts_, 512:576], store=True)
    k.release(m)


def qside_head(k, c, L, ws, h, cqT, T, Q):
    wq = ws.load(L.mla_w_uq, 0, 4, [(h * 192, 192), (h * 192 + 160, 32), (h * 192 + 128, 32)])
    k.dma(Q.wukf[:, :, :], L.kv_w_uk[:, h * 128:(h + 1) * 128].re("(rc p) d -> p rc d", p=128))
    p = k.ps()
    for rc in range(4):
        k.tr(p[:, rc * 128:(rc + 1) * 128], Q.wukf[:, rc, :], c.ident[:, :])
    k.copy(Q.wukT[:, :], p[:, 0:512], e="act")
    p = k.ps()
    for kc in range(4):
        k.mm(p[:, 0:T], wq[:, kc, 0:128], cqT[:, kc, :], start=(kc == 0), stop=(kc == 3))
    k.copy(Q.qn[:, 0:T], p[:, 0:T], e="act")
    p1 = k.ps()
    for kc in range(4):
        k.mm(p1[0:64, 0:T], wq[:, kc, 128:192], cqT[:, kc, :], start=(kc == 0), stop=(kc == 3))
    p2 = k.ps()
    for kc in range(4):
        k.mm(p2[0:64, 0:T], wq[:, kc, 192:256], cqT[:, kc, :], start=(kc == 0), stop=(kc == 3))
    rope_apply(k, c, Q.qpe_dst(h), p1[0:64, 0:T], p2[0:64, 0:T], T, Q.rtmp)
    for rc in range(4):
        p = k.ps()
        k.mm(p[:, 0:T], Q.wukT[:, rc * 128:(rc + 1) * 128], Q.qn[:, 0:T])
        k.copy(Q.qlat_dst(h, rc), p[:, 0:T], e="act" if rc % 2 else "dve")


def cq_compute(k, c, L, ws, hT, T, cqT):
    cqf = k.sb([128, 4, T], F32, nsplit=4, name="cqf")
    sq = k.sb([128, T], F32, name="cqsq")
    def cons(oc, p):
        k.copy(cqf[:, oc, :], p[:, 0:T], e="act")
    linear_fm(k, ws, L.mla_w_dq, 0, NKC, 0, QL, 512, lambda kc: hT[:, kc, :], T, cons)
    pss = k.ps()
    for rc in range(4):
        k.tt(sq[:], cqf[:, rc, :], cqf[:, rc, :], ALU.mult)
        k.mm(pss[:, 0:T], c.ones[:], sq[:], start=(rc == 0), stop=(rc == 3))
    rstd = k.sb([128, T], F32, name="cqrstd")
    rstd_from_psum(k, rstd[:], pss[:, 0:T], QL)
    for rc in range(4):
        k.stt(cqT[:, rc, :], cqf[:, rc, :], c.gq[:, rc:rc + 1], rstd[:], ALU.mult, ALU.mult)


def attn_phase_prompt(k, c, L, xT, T, mods, KB, tile):
    mcol = slice(0, 1)
    m = k.mark()
    hT = k.sb([128, NKC, T], BF16, nsplit=NKC, name="hT")
    k.ts(c.gs[:, :, :], mods[:, 16:32, :], 1.0, ALU.add)
    k.tt(c.gs[:, :, :], c.gs[:, :, :], c.gmix[:, 16:32].un(2).bc([128, 16, 17]), ALU.mult)
    modnorm(k, c, xT, T, lambda kc: c.gs[:, kc, mcol], lambda kc: mods[:, kc, mcol], mcol, 1, T, hT)
    ws = WS(k, NKC * 512)
    cqT = k.sb([128, 4, T], BF16, nsplit=4, name="cqT")
    m2 = k.mark()
    cq_compute(k, c, L, ws, hT, T, cqT)
    k.release(m2)
    aoT = k.sb([128, NH, T], BF16, nsplit=NH, name="aoT")
    Q = Ctx()
    Q.wukf = k.sb([128, 4, 128], F32, name="wukf")
    Q.wukT = k.sb([128, 512], BF16, name="wukT")
    Q.qn = k.sb([128, T], BF16, name="qn")
    Q.rtmp = [k.sb([64, T], F32, name="qrt0"), k.sb([64, T], F32, name="qrt1")]
    qlat = k.sb([128, 4, T], BF16, nsplit=4, name="qlat")
    qpe = k.sb([64, T], BF16, name="qpe")
    Q.qpe_dst = lambda h: qpe[:, 0:T]
    Q.qlat_dst = lambda h, rc: qlat[:, rc, :]
    pT = [k.sb([128, T], BF16, name="pT") for _ in range(2)]
    linv = k.sb([128, T], F32, name="linv")
    olat = k.sb([128, 4, T], BF16, nsplit=4, name="olat")
    wuv_t = [k.sb([128, 4, 128], BF16, name="wuv") for _ in range(2)]
    nkb = 4 * tile + 4
    for h in range(NH):
        k.ps_rot = [6, 7]
        qside_head(k, c, L, ws, h, cqT, T, Q)
        wuv = wuv_t[h % 2]
        k.dma(wuv[:, :, :], L.kv_w_uv[:, h * 128:(h + 1) * 128].re("(rc p) v -> p rc v", p=128), q="pool")
        k.ps_rot = [7]
        pl = k.psb[6]
        po = [k.psb[2 + rc] for rc in range(4)]

        def scores(kb):
            ps_ = k.psb[kb % 2]
            q0 = max(0, kb - 4 * tile) * 128
            ks = slice(kb * 128, (kb + 1) * 128)
            for rc in range(4):
                k.mm(ps_[:, q0:T], KB.ckvT_bf[:, rc, ks], qlat[:, rc, q0:T], start=(rc == 0), stop=False)
            k.mm(ps_[:, q0:T], KB.kpeT_bf[0:64, ks], qpe[0:64, q0:T], start=False, stop=True)
            return ps_, q0

        nxt = scores(0)
        for kb in range(nkb):
            ps_, q0 = nxt
            if kb + 1 < nkb:
                nxt = scores(kb + 1)
            pt = pT[kb % 2]
            k.act(pt[:, q0:T], ps_[:, q0:T], AF.Exp, scale=SM_SCALE)
            if kb >= 4 * tile:
                k.tt(pt[:, q0:q0 + 128], pt[:, q0:q0 + 128], c.tri[:, :], ALU.mult, e="pool")
            last = (kb == nkb - 1)
            for rc in range(4):
                k.mm(po[rc][:, q0:T], KB.ckv_tm_bf[:, kb, rc * 128:(rc + 1) * 128], pt[:, q0:T], start=(kb == 0), stop=last)
            k.mm(pl[:, q0:T], c.ones_bf[:, :], pt[:, q0:T], start=(kb == 0), stop=last)
        k.recip(linv[:, :], pl[:, 0:T])
        for rc in range(4):
            k.copy(olat[:, rc, :], po[rc][:, 0:T], e="act" if rc % 2 else "dve")
        pv = k.ps()
        for rc in range(4):
            k.mm(pv[:, 0:T], wuv[:, rc, :], olat[:, rc, :], start=(rc == 0), stop=(rc == 3))
        k.tt(aoT[:, h, :], pv[:, 0:T], linv[:, :], ALU.mult)
    k.ps_rot = list(range(8))
    def cons(oc, p):
        resid_add(k, c, xT, oc, p, T, mods[:, 32 + oc, mcol], 1, T)
    linear_fm(k, ws, L.mla_w_o, 0, NH, 0, D, 512, lambda kc: aoT[:, kc, :], T, cons)
    k.release(m)


def final_phase(k, c, xT, T, mods, mcol, nseq, tps, y_out, row0):
    m = k.mark()
    yT = k.sb([128, NKC, T], F32, nsplit=NKC, name="yT")
    k.ts(c.gs[:, :, :], mods[:, 16:32, :], 1.0, ALU.add)
    k.tt(c.gs[:, :, :], c.gs[:, :, :], c.gfin[:, 0:16].un(2).bc([128, 16, 17]), ALU.mult)
    modnorm(k, c, xT, T, lambda kc: c.gs[:, kc, mcol], lambda kc: mods[:, kc, mcol], mcol, nseq, tps, yT)
    transpose_out(k, c, lambda kc: yT[:, kc, :], NKC, T, y_out, row0)
    k.release(m)


INPUT_SPECS = {
    "cc": ([17, 2048], F32), "xp": ([2048, 2048], F32), "xs": ([64, 2048], F32),
    "w_ada0": ([2048, 12288], F32), "b_ada0": ([12288], F32), "w_ada1": ([2048, 12288], F32), "b_ada1": ([12288], F32),
    "g_mix": ([4096], F32), "g_ffn": ([4096], F32),
    "w_gate_up0": ([2048, 11264], F32), "w_gate_up1": ([2048, 11264], F32),
    "w_down0": ([5632, 2048], F32), "w_down1": ([5632, 2048], F32),
    "gdn_w_in": ([2048, 12352], F32), "gdn_w_conv": ([4, 8192], F32), "gdn_a_log": ([32], F32), "gdn_dt_bias": ([32], F32),
    "gdn_g_norm": ([128], F32), "gdn_w_out": ([4096, 2048], F32),
    "kv_w_ada": ([2048, 4096], F32), "kv_b_ada": ([4096], F32), "kv_g_in": ([2048], F32), "kv_w_down": ([2048, 576], F32),
    "kv_g_norm": ([512], F32), "kv_w_uk": ([512, 2048], F32), "kv_w_uv": ([512, 2048], F32),
    "mla_w_dq": ([2048, 512], F32), "mla_g_q": ([512], F32), "mla_w_uq": ([512, 3072], F32), "mla_w_o": ([2048, 2048], F32),
    "final_w_ada": ([2048, 4096], F32), "final_b_ada": ([4096], F32), "final_g": ([2048], F32),
    "rope_cos": ([64, 2116], F32), "rope_sin": ([64, 2116], F32), "tri": ([128, 128], F32),
    "cache_ckv": ([10240 * 128, 512], F32), "cache_kpe": ([10240 * 128, 64], F32), "page_table": ([1, 1024], I32),
    "state_ssm": ([16, 32, 128, 128], F32), "state_conv": ([48, 8192], F32),
    "iota_p": ([128, 1], F32), "cm": ([128, 64], F32),
}
OUTPUT_SPECS = {
    "yp": [2048, 2048], "ys": [64, 2048], "ssm_p": [32, 128, 128], "conv_p": [3, 8192], "ckv_p": [2048, 512], "kpe_p": [2048, 64],
    "ssm_s": [16, 32, 128, 128], "conv_s": [48, 8192], "ckv_s": [64, 512], "kpe_s": [64, 64],
}


def build_all(k, do_prompt=True, do_sample=True, ntiles=4, dbg=None, stop=None, dbg_s=False):
    c = Ctx()
    c.dbg_s = dbg_s
    L = Ctx()
    I = {}
    cst = make_consts()
    for n, v in cst.items():
        I[n] = k.din(n, list(v.shape))

    class LazyIn(dict):
        def __missing__(self, key):
            shp, dt = INPUT_SPECS[key]
            t = k.din(key, shp, dt)
            self[key] = t
            return t
    II = LazyIn(I)
    O = {}

    def out(name):
        if name not in O:
            O[name] = k.dout(name, OUTPUT_SPECS[name])
        return O[name]
    L.gdn_w_in = II["gdn_w_in"]
    L.gdn_w_out = II["gdn_w_out"]
    L.w_gate_up = [II["w_gate_up0"], II["w_gate_up1"]]
    L.w_down = [II["w_down0"], II["w_down1"]]
    L.kv_w_down = II["kv_w_down"]
    L.kv_w_uk = II["kv_w_uk"]
    L.kv_w_uv = II["kv_w_uv"]
    L.mla_w_dq = II["mla_w_dq"]
    L.mla_w_uq = II["mla_w_uq"]
    L.mla_w_o = II["mla_w_o"]
    setup_common(k, c, II)
    setup_gdn(k, c, II)
    c.gkv = c.colvec(II["kv_g_in"][0:D], 16, "gkv")
    c.gkvn = c.colvec(II["kv_g_norm"][0:512], 4, "gkvn")
    c.gq = c.colvec(II["mla_g_q"][0:512], 4, "gq")
    c.gfin = c.colvec(II["final_g"][0:D], 16, "gfin")
    c.tri = load_const(k, II["tri"], [128, 128], BF16, q="pool")
    c.ones_bf = k.sb([128, 128], BF16, name="ones_bf")
    k.copy(c.ones_bf[:, :], c.ones[:, :])
    T = 512
    x1_d = [k.dscratch(f"x1_{t}", [128, NKC * T]) for t in range(4)]
    KT_d = k.dscratch("KT_d", [128, 4 * 2048], BF16)
    KP_d = k.dscratch("KP_d", [64, 2048], BF16)
    KV_d = k.dscratch("KV_d", [128, 16 * 512], BF16)
    xs1_d = k.dscratch("xs1_d", [128, NKC * 64])
    c.ckvT_new = k.sb([128, 4, 64], BF16, nsplit=4, name="ckvT_new")
    c.kpeT_new = k.sb([64, 64], BF16, name="kpeT_new")
    c.ckv_tm_new = k.sb([128, 512], BF16, name="ckv_tm_new")
    m0 = k.mark()
    mods0 = ada_phase(k, c, II["w_ada0"], II["b_ada0"], 12288, "mods0")
    modskv = ada_phase(k, c, II["kv_w_ada"], II["kv_b_ada"], 4096, "modskv")
    if do_prompt:
        mp = k.mark()
        G = load_G(k, II, "p")
        st = Ctx()
        st.S = k.sb([128, 32, 128], F32, nsplit=32, name="S")
        st.convtail = k.sb([128, 64, 1, 3], F32, nsplit=64, name="ctail")
        k.memset(st.S[:, :, :], 0.0)
        k.memset(st.convtail[:, :, :, :], 0.0)
        xT = k.sb([128, NKC, T], F32, nsplit=NKC, name="xT")
        mcol = slice(0, 1)
        for t in range(ntiles):
            transpose_in(k, c, II["xp"], t * T, T, xT)
            gdn_phase(k, c, L, xT, T, 128, mods0, mcol, 1, T, G, st, False)
            ffn_phase(k, c, L, xT, T, mods0, mcol, 1, T, 0)
            k.dma(x1_d[t][:, :], xT[:, :, :].re("p k t -> p (k t)"), store=True)
            mt = k.mark()
            load_rope(k, c, II, t * T, T)
            KBw = Ctx()
            ckvT_st = k.sb([128, 4, T], BF16, nsplit=4, name="ckvT_st")
            kpeT_st = k.sb([64, T], BF16, name="kpeT_st")
            ckvtm_st = k.sb([128, 4, 512], BF16, nsplit=4, name="ckvtm_st")
            KBw.ckvT = lambda rc: ckvT_st[:, rc, :]
            KBw.kpeT = lambda: kpeT_st[:, :]
            KBw.ckv_tm = lambda s: ckvtm_st[:, s, :]
            if stop != "nokv":
                kv_phase(k, c, L, xT, T, modskv, mcol, 1, T, KBw, t * T, out("ckv_p"), out("kpe_p"), t * T)
            for rc in range(4):
                k.dma(KT_d[:, rc * 2048 + t * T: rc * 2048 + (t + 1) * T], ckvT_st[:, rc, :], store=True)
            k.dma(KP_d[:, t * T:(t + 1) * T], kpeT_st[:, :], store=True)
            k.dma(KV_d[:, t * 4 * 512:(t + 1) * 4 * 512], ckvtm_st[:, :, :].re("p s r -> p (s r)"), store=True)
            k.release(mt)
        k.dma(out("ssm_p")[:, :, :].re("h p v -> p h v"), st.S[:, :, :], store=True)
        transpose_out(k, c, lambda cid: st.convtail[:, cid, 0, :], 64, 3, out("conv_p"), 0)
        k.release(mp)
    if do_sample:
        sample_layer0(k, c, L, II, out, mods0, modskv, xs1_d)
    k.release(m0)
    if stop == "l0":
        k.finish()
        return II, O, cst
    mods1 = ada_phase(k, c, II["w_ada1"], II["b_ada1"], 12288, "mods1")
    modsf = ada_phase(k, c, II["final_w_ada"], II["final_b_ada"], 4096, "modsf")
    if do_prompt:
        mp = k.mark()
        KB = Ctx()
        KB.ckvT_bf = k.sb([128, 4, 2048], BF16, nsplit=4, name="ckvT_bf")
        KB.kpeT_bf = k.sb([64, 2048], BF16, name="kpeT_bf")
        KB.ckv_tm_bf = k.sb([128, 16, 512], BF16, nsplit=16, name="ckv_tm_bf")
        k.dma(KB.ckvT_bf[:, :, :].re("p r t -> p (r t)"), KT_d[:, :])
        k.dma(KB.kpeT_bf[:, :], KP_d[:, :])
        k.dma(KB.ckv_tm_bf[:, :, :].re("p s r -> p (s r)"), KV_d[:, :])
        xT = k.sb([128, NKC, T], F32, nsplit=NKC, name="xT")
        for t in range(ntiles):
            k.dma(xT[:, :, :].re("p k t -> p (k t)"), x1_d[t][:, :])
            mt = k.mark()
            load_rope(k, c, II, t * T, T)
            attn_phase_prompt(k, c, L, xT, T, mods1, KB, t)
            k.release(mt)
            if dbg is not None and "xmid1" in dbg:
                transpose_out(k, c, lambda kc: xT[:, kc, :], NKC, T, dbg["xmid1"], t * T)
            ffn_phase(k, c, L, xT, T, mods1, slice(0, 1), 1, T, 1)
            final_phase(k, c, xT, T, modsf, slice(0, 1), 1, T, out("yp"), t * T)
        k.release(mp)
    if do_sample:
        sample_layer1(k, c, L, II, out, mods1, modsf, xs1_d)
    k.finish()
    return II, O, cst


def sample_layer0(k, c, L, II, out, mods0, modskv, xs1_d):
    T, nseq, tps = 64, 16, 4
    mcol = slice(1, 17)
    m = k.mark()
    G = load_G(k, II, "s")
    c.Gs = G
    st = Ctx()
    st.convtail = k.sb([128, 64, nseq, 3], F32, nsplit=64, name="ctail_s")
    st.ssm_in = II["state_ssm"]
    st.ssm_out = out("ssm_s")
    m2 = k.mark()
    stg = [k.sb([128, 2048], F32, name="cstg") for _ in range(2)]
    for pc in range(4):
        sg = stg[pc % 2]
        k.dma(sg[0:48, :], II["state_conv"][:, pc * 2048:(pc + 1) * 2048])
        for g in range(2):
            p = k.ps()
            for j in range(8):
                cl = g * 8 + j
                k.tr(p[:, j * 48:(j + 1) * 48], sg[0:48, cl * 128:(cl + 1) * 128], c.ident[0:48, 0:48])
            cid0 = pc * 16 + g * 8
            k.copy(st.convtail[:, cid0:cid0 + 8, :, :].re("p c s j -> p c (s j)"),
                   p[:, 0:8 * 48].re("p (c x) -> p c x", x=48), e="act" if g else "dve")
    k.release(m2)
    xT = k.sb([128, NKC, T], F32, nsplit=NKC, name="xTs")
    transpose_in(k, c, II["xs"], 0, T, xT)
    gdn_phase(k, c, L, xT, T, 64, mods0, mcol, nseq, tps, G, st, True)
    if getattr(c, "dbg_s", None):
        transpose_out(k, c, lambda kc: xT[:, kc, :], NKC, T, k.dout("d_xmid0", [64, 2048]), 0)
    for pc in range(4):
        transpose_out(k, c, lambda cid: st.convtail[:, pc * 16 + cid, :, :].re("p s j -> p (s j)"), 16, 48,
                      out("conv_s"), 0, col0=pc * 2048)
    ffn_phase(k, c, L, xT, T, mods0, mcol, nseq, tps, 0)
    if getattr(c, "dbg_s", None):
        transpose_out(k, c, lambda kc: xT[:, kc, :], NKC, T, k.dout("d_x0", [64, 2048]), 0)
    k.dma(xs1_d[:, :], xT[:, :, :].re("p k t -> p (k t)"), store=True)
    load_rope(k, c, II, 2052, T)
    KBw = Ctx()
    KBw.ckvT = lambda rc: c.ckvT_new[:, rc, :]
    KBw.kpeT = lambda: c.kpeT_new[:, :]
    KBw.ckv_tm = lambda s: c.ckv_tm_new[0:64, :]
    kv_phase(k, c, L, xT, T, modskv, mcol, nseq, tps, KBw, 0, out("ckv_s"), out("kpe_s"), 0)
    k.release(m)


def sample_layer1(k, c, L, II, out, mods1, modsf, xs1_d):
    T, nseq, tps = 64, 16, 4
    mcol = slice(1, 17)
    m = k.mark()
    xT = k.sb([128, NKC, T], F32, nsplit=NKC, name="xTs")
    k.dma(xT[:, :, :].re("p k t -> p (k t)"), xs1_d[:, :])
    load_rope(k, c, II, 2052, T)
    hT = k.sb([128, NKC, T], BF16, nsplit=NKC, name="hTs")
    k.ts(c.gs[:, :, :], mods1[:, 16:32, :], 1.0, ALU.add)
    k.tt(c.gs[:, :, :], c.gs[:, :, :], c.gmix[:, 16:32].un(2).bc([128, 16, 17]), ALU.mult)
    modnorm(k, c, xT, T, lambda kc: c.gs[:, kc, mcol], lambda kc: mods1[:, kc, mcol], mcol, nseq, tps, hT)
    ws = WS(k, NKC * 512)
    cqT = k.sb([128, 4, T], BF16, nsplit=4, name="cqTs")
    m2 = k.mark()
    cq_compute(k, c, L, ws, hT, T, cqT)
    k.release(m2)
    QLs = k.sb([128, 4, NH, T], BF16, name="QLs")
    QPs = k.sb([64, NH, T], BF16, name="QPs")
    OLs = k.sb([128, 4, NH, T], BF16, name="OLs")
    aoT = k.sb([128, NH, T], BF16, nsplit=NH, name="aoTs")
    Q = Ctx()
    Q.wukf = k.sb([128, 4, 128], F32, name="wukf")
    Q.wukT = k.sb([128, 512], BF16, name="wukT")
    Q.qn = k.sb([128, T], BF16, name="qn")
    Q.rtmp = [k.sb([64, T], F32, name="qrt0"), k.sb([64, T], F32, name="qrt1")]
    Q.qpe_dst = lambda h: QPs[:, h, :]
    Q.qlat_dst = lambda h, rc: QLs[:, rc, h, :]
    for h in range(NH):
        qside_head(k, c, L, ws, h, cqT, T, Q)
    ptb = k.sb([128, 1024], I32, name="ptb")
    k.dma(ptb[:, :], II["page_table"][0:1, :].bc([128, 1024]))
    idxf = k.sb([128, 1024], F32, name="idxf")
    k.copy(idxf[:, :], ptb[:, :])
    iota = load_const(k, II["iota_p"], [128, 1])
    k.ts(idxf[:, :], idxf[:, :], 128.0, ALU.mult, iota[:, 0:1], ALU.add)
    idx = k.sb([128, 1024], I32, name="idx")
    k.copy(idx[:, :], idxf[:, :])
    cm = load_const(k, II["cm"], [128, 64])
    ident_bf = k.sb([128, 128], BF16, name="ident_bf")
    k.copy(ident_bf[:, :], c.ident[:, :])
    Kp = [k.sb([128, 576], BF16, name="Kp") for _ in range(4)]
    KT = [k.sb([128, 5, 128], BF16, name="KT") for _ in range(2)]
    pts = [k.sb([128, 64], BF16, name="pts") for _ in range(2)]
    pe_ = k.sb([64, 64], F32, name="pe_")
    ptm = k.sb([64, 64], BF16, name="ptm")
    linv = k.sb([128, 64], F32, name="linvs")
    ptr_bank = k.psb[7]
    ptr_bf = ptr_bank[:, :].bitcast(BF16)
    po = [k.psb[2 + rc] for rc in range(4)]
    pl = k.psb[6]
    import os
    NPG = int(os.environ.get("NPG", "64"))
    n = 0
    for s in range(nseq):
        qs = slice(s * tps, (s + 1) * tps)
        def qlat_s(rc):
            return QLs[:, rc, :, qs]
        qpe_s = QPs[:, :, qs]
        for pg in range(NPG):
            col = s * 64 + pg
            kp = Kp[n % 4]
            kt = KT[n % 2]
            k.idma(kp[:, 0:512], II["cache_ckv"][:, :], idx[:, col:col + 1])
            k.idma(kp[:, 512:576], II["cache_kpe"][:, :], idx[:, col:col + 1])
            for rc in range(4):
                k.tr(ptr_bf[:, rc * 128:(rc + 1) * 128], kp[:, rc * 128:(rc + 1) * 128], ident_bf[:, :])
            k.tr(ptr_bf[0:64, 512:640], kp[:, 512:576], ident_bf[:, :])
            k.copy(kt[:, 0:4, :], ptr_bf[:, 0:512].re("p (c t) -> p c t", t=128), e="dve")
            k.copy(kt[0:64, 4, :], ptr_bf[0:64, 512:640], e="dve")
            ps_ = k.psb[n % 2]
            for rc in range(4):
                k.mm(ps_[:, 0:64], kt[:, rc, :], qlat_s(rc), start=(rc == 0), stop=False)
            k.mm(ps_[:, 0:64], kt[0:64, 4, :], qpe_s, start=False, stop=True)
            pt = pts[n % 2]
            k.act(pt[:, :], ps_[:, 0:64], AF.Exp, scale=SM_SCALE)
            for rc in range(4):
                k.mm(po[rc][:, 0:64], kp[:, rc * 128:(rc + 1) * 128], pt[:, :], start=(pg == 0), stop=False)
            k.mm(pl[:, 0:64], c.ones_bf[:, :], pt[:, :], start=(pg == 0), stop=False)
            n += 1
        ps_ = k.psb[n % 2]
        n += 1
        for rc in range(4):
            k.mm(ps_[0:64, 0:64], c.ckvT_new[:, rc, :], qlat_s(rc), start=(rc == 0), stop=False)
        k.mm(ps_[0:64, 0:64], c.kpeT_new[0:64, :], qpe_s, start=False, stop=True)
        k.act(pe_[:, :], ps_[0:64, 0:64], AF.Exp, scale=SM_SCALE)
        k.stt(ptm[:, :], pe_[:, :], c.Gs.seqmask[0:64, s:s + 1], cm[0:64, :], ALU.mult, ALU.mult)
        for rc in range(4):
            k.mm(po[rc][:, 0:64], c.ckv_tm_new[0:64, rc * 128:(rc + 1) * 128], ptm[:, :], start=(NPG == 0), stop=True)
        k.mm(pl[:, 0:64], c.ones_bf[0:64, :], ptm[:, :], start=(NPG == 0), stop=True)
        k.recip(linv[:, :], pl[:, 0:64])
        for rc in range(4):
            k.tt(OLs[:, rc, :, qs], po[rc][:, 0:64].re("p (h i) -> p h i", i=tps), linv[:, :].re("p (h i) -> p h i", i=tps),
                 ALU.mult)
    k.ps_rot = [0, 1, 7]
    wuv_t = [k.sb([128, 4, 128], BF16, name="wuvs") for _ in range(2)]
    for h in range(NH):
        wuv = wuv_t[h % 2]
        k.dma(wuv[:, :, :], L.kv_w_uv[:, h * 128:(h + 1) * 128].re("(rc p) v -> p rc v", p=128), q="pool")
        pv = k.ps()
        for rc in range(4):
            k.mm(pv[:, 0:T], wuv[:, rc, :], OLs[:, rc, h, :], start=(rc == 0), stop=(rc == 3))
        k.copy(aoT[:, h, :], pv[:, 0:T], e="act")
    k.ps_rot = list(range(8))
    def cons(oc, p):
        resid_add(k, c, xT, oc, p, T, mods1[:, 32 + oc, mcol], nseq, tps)
    linear_fm(k, ws, L.mla_w_o, 0, NH, 0, D, 512, lambda kc: aoT[:, kc, :], T, cons)
    ffn_phase(k, c, L, xT, T, mods1, mcol, nseq, tps, 1)
    final_phase(k, c, xT, T, modsf, mcol, nseq, tps, out("ys"), 0)
    k.release(m)


def _host_inputs(z, core, cst, needed, shared):
    seq = core % 4
    ins = {n: v for n, v in cst.items() if n in needed}
    sl = slice(core * 16, (core + 1) * 16)
    i = np.arange(128)

    def put(n, fn, share=False):
        if n not in needed:
            return
        if share:
            if n not in shared:
                shared[n] = np.ascontiguousarray(fn())
            ins[n] = shared[n]
        else:
            ins[n] = np.ascontiguousarray(fn())

    def cc():
        a = np.zeros((17, 2048), np.float32)
        a[0] = z["c_prompt"][seq]
        a[1:] = z["c_sample"][sl]
        return a
    put("cc", cc)
    put("xp", lambda: z["x_prompt"][seq])
    put("xs", lambda: z["x_sample"][sl].reshape(64, 2048))
    for l in (0, 1):
        put(f"w_ada{l}", lambda: z["w_ada"][l], True)
        put(f"b_ada{l}", lambda: z["b_ada"][l], True)
        put(f"w_gate_up{l}", lambda: z["w_gate_up"][l], True)
        put(f"w_down{l}", lambda: z["w_down"][l], True)
    put("g_mix", lambda: z["g_mix"].reshape(-1), True)
    put("g_ffn", lambda: z["g_ffn"].reshape(-1), True)
    for n in ("gdn_w_in", "gdn_w_conv", "gdn_a_log", "gdn_dt_bias", "gdn_g_norm", "gdn_w_out", "mla_w_dq", "mla_g_q",
              "mla_w_uq", "mla_w_o"):
        put(n, lambda n=n: z[n][0], True)
    for n in ("kv_w_ada", "kv_b_ada", "kv_g_in", "kv_w_down", "kv_g_norm", "kv_w_uk", "kv_w_uv", "final_w_ada",
              "final_b_ada", "final_g"):
        put(n, lambda n=n: z[n], True)
    if "rope_cos" in needed or "rope_sin" in needed:
        if "rope" not in shared:
            shared["rope"] = rope_tables()
        ins["rope_cos"], ins["rope_sin"] = shared["rope"]
    put("tri", lambda: (i[None, :] >= i[:, None]).astype(np.float32), True)
    put("cache_ckv", lambda: z["cache_ckv"].reshape(-1, 512), True)
    put("cache_kpe", lambda: z["cache_kpe"].reshape(-1, 64), True)
    put("page_table", lambda: z["page_table"][sl].reshape(1, 1024).astype(np.int32))
    put("state_ssm", lambda: z["state_ssm"][0, sl])
    put("state_conv", lambda: z["state_conv"][0, sl].reshape(48, 8192))
    put("iota_p", lambda: i.astype(np.float32).reshape(128, 1), True)
    put("cm", lambda: ((i[:, None] % 4) <= (np.arange(64)[None, :] % 4)).astype(np.float32), True)
    return ins


_PROG = {}


def _get_prog():
    if "k" not in _PROG:
        k, (II, O, cst) = two_pass(lambda kk: build_all(kk))
        _PROG["k"] = k
        _PROG["cst"] = cst
    return _PROG["k"], _PROG["cst"]


def kernel(**inputs):
    from concourse.bass_utils import run_bass_kernel_spmd
    k, cst = _get_prog()
    z = {n: np.asarray(v) for n, v in inputs.items()}
    needed = set(k.dram_in)
    shared = {}
    in_maps = [_host_inputs(z, core, cst, needed, shared) for core in range(8)]
    res = run_bass_kernel_spmd(k.nc, in_maps, core_ids=list(range(8)))
    R = res.results
    f32 = np.float32
    y_prompt = np.stack([R[c]["yp"] for c in range(4)]).astype(f32)
    y_sample = np.concatenate([R[c]["ys"].reshape(16, 4, 2048) for c in range(8)]).astype(f32)
    ssm_prompt = np.stack([R[c]["ssm_p"] for c in range(4)])[None].astype(f32)
    conv_prompt = np.stack([R[c]["conv_p"] for c in range(4)])[None].astype(f32)
    ckv_prompt = np.stack([R[c]["ckv_p"] for c in range(4)]).astype(f32)
    kpe_prompt = np.stack([R[c]["kpe_p"] for c in range(4)]).astype(f32)
    ssm_sample = np.concatenate([R[c]["ssm_s"] for c in range(8)])[None].astype(f32)
    conv_sample = np.concatenate([R[c]["conv_s"].reshape(16, 3, 8192) for c in range(8)])[None].astype(f32)
    ckv_sample = np.concatenate([R[c]["ckv_s"].reshape(16, 4, 512) for c in range(8)]).astype(f32)
    kpe_sample = np.concatenate([R[c]["kpe_s"].reshape(16, 4, 64) for c in range(8)]).astype(f32)
    return (y_prompt, y_sample, ssm_prompt, conv_prompt, ckv_prompt, kpe_prompt,
            ssm_sample, conv_sample, ckv_sample, kpe_sample)
```

```python
import os
import numpy as np
import concourse.bass as bass
import concourse.mybir as mybir

F32 = mybir.dt.float32
BF16 = mybir.dt.bfloat16
I32 = mybir.dt.int32
U32 = mybir.dt.uint32
AF = mybir.ActivationFunctionType
ALU = mybir.AluOpType
AX = mybir.AxisListType
DTSZ = {F32: 4, BF16: 2, I32: 4, U32: 4}


class Buf:
    __slots__ = ("w", "r", "name")

    def __init__(self, name=""):
        self.w = {}
        self.r = {}
        self.name = name


class V:
    __slots__ = ("ap", "bufs")

    def __init__(self, ap, bufs):
        self.ap = ap
        self.bufs = bufs

    def __getitem__(self, key):
        return V(self.ap[key], self.bufs)

    def bc(self, shape):
        return V(self.ap.to_broadcast(list(shape)), self.bufs)

    def un(self, axis):
        return V(self.ap.unsqueeze(axis), self.bufs)

    def re(self, s, **kw):
        return V(self.ap.rearrange(s, **kw), self.bufs)

    def bitcast(self, dt):
        return V(self.ap.bitcast(dt), self.bufs)

    @property
    def shape(self):
        return tuple(self.ap.shape)


class Tens:
    def __init__(self, handle, shape, nsplit=1, name=""):
        self.h = handle
        self.shape = tuple(shape)
        self.nsplit = nsplit
        self.bufs = [Buf(f"{name}.{i}") for i in range(nsplit)]
        self.name = name

    def all(self):
        return V(self.h.ap() if hasattr(self.h, "ap") and callable(getattr(self.h, "ap")) else self.h[:], list(self.bufs))

    def __getitem__(self, key):
        ap = self.h[key]
        if self.nsplit == 1:
            return V(ap, self.bufs)
        k1 = key[1] if isinstance(key, tuple) and len(key) > 1 else slice(None)
        if isinstance(k1, int):
            per = self.shape[1] // self.nsplit
            return V(ap, [self.bufs[k1 // per]])
        if isinstance(k1, slice):
            per = self.shape[1] // self.nsplit
            a = 0 if k1.start is None else k1.start
            b = self.shape[1] if k1.stop is None else k1.stop
            return V(ap, self.bufs[a // per:(b - 1) // per + 1])
        return V(ap, list(self.bufs))


class K:
    def __init__(self, needed=None, same_engine_sync=True):
        import bisect
        self._bisect = bisect
        self.needed = needed
        self.used = {e: set() for e in ("pe", "dve", "act", "pool")}
        self.runid = {e: 0 for e in ("pe", "dve", "act", "pool", "sp")}
        self.run_of = {e: [] for e in ("pe", "dve", "act", "pool")}
        self.nc = bass.Bass("TRN2", target_bir_lowering=False)
        nc = self.nc
        self.E = {"pe": nc.tensor, "dve": nc.vector, "act": nc.scalar, "pool": nc.gpsimd, "sp": nc.sync}
        self.esem = {e: nc.alloc_semaphore(f"es_{e}") for e in ("pe", "dve", "act", "pool")}
        self.ecnt = {e: 0 for e in self.esem}
        self.seen = {e: {} for e in self.E}
        self.same = same_engine_sync
        self.dpool = {}
        for q, n in (("sp", 24), ("pool", 24), ("act", 8)):
            self.dpool[q] = [[nc.alloc_semaphore(f"ds_{q}{i}"), 0] for i in range(n)]
        self.dnext = {q: 0 for q in self.dpool}
        self.stores = []
        self.semname = {}
        self.ninst = 0
        self.nincs = 0
        self.log = []
        self.sem2eng = {id(s): e for e, s in self.esem.items()}
        self.needed_set = {e: set(v) for e, v in needed["reps"].items()} if needed is not None else None
        self.sb_off = 16512
        self.sb_cap = 229376
        self.sb_peak = 0
        self.uid = 0
        self.psb = []
        for i in range(8):
            h = nc.alloc_psum_tensor(f"psb{i}", [128, 512], F32)
            self.psb.append(Tens(h, [128, 512], 1, f"psb{i}"))
        self.psn = 0
        self.ps_rot = list(range(8))
        self.dram_in = {}
        self.dram_out = {}

    def sb(self, shape, dtype=F32, nsplit=1, name=None):
        self.uid += 1
        name = f"{name or 't'}_{self.uid}"
        per = int(np.prod(shape[1:])) * DTSZ[dtype]
        per = (per + 63) // 64 * 64
        off = self.sb_off
        if off + per > self.sb_cap:
            raise RuntimeError(f"SBUF arena overflow allocating {name} {shape}: off={off} per={per}")
        h = self.nc.alloc_sbuf_tensor_at(name, list(shape), dtype, offset=off)
        self.sb_off += per
        self.sb_peak = max(self.sb_peak, self.sb_off)
        return Tens(h, shape, nsplit, name)

    def mark(self):
        return self.sb_off

    def release(self, m):
        self.barrier()
        self.sb_off = m

    def ps(self):
        t = self.psb[self.ps_rot[self.psn % len(self.ps_rot)]]
        self.psn += 1
        return t

    def din(self, name, shape, dtype=F32):
        h = self.nc.dram_tensor(name, list(shape), dtype, kind="ExternalInput")
        self.dram_in[name] = (tuple(shape), dtype)
        return Tens(h, shape, 1, name)

    def dout(self, name, shape, dtype=F32):
        h = self.nc.dram_tensor(name, list(shape), dtype, kind="ExternalOutput")
        self.dram_out[name] = (tuple(shape), dtype)
        return Tens(h, shape, 1, name)

    def dscratch(self, name, shape, dtype=F32, nsplit=1):
        h = self.nc.dram_tensor(name, list(shape), dtype, kind="Internal")
        return Tens(h, shape, nsplit, name)

    def _deps(self, reads, writes):
        d = {}
        for v in reads:
            if isinstance(v, V):
                for b in v.bufs:
                    for s, val in b.w.items():
                        if d.get(s, 0) < val:
                            d[s] = val
        for v in writes:
            for b in v.bufs:
                for dd in (b.w, b.r):
                    for s, val in dd.items():
                        if d.get(s, 0) < val:
                            d[s] = val
        return d

    def _wait(self, e, deps, skip_own=False):
        eng = self.E[e]
        seen = self.seen[e]
        own = self.esem.get(e)
        for s, val in deps.items():
            if s is own and (skip_own or not self.same):
                continue
            if seen.get(s, 0) < val:
                sv = self._semval(s, val)
                eng.wait_ge(s, sv)
                seen[s] = val
                self.runid[e] += 1
                self.log.append((e, "wait", s, sv))

    def _semval(self, s, val):
        en = self.sem2eng.get(id(s))
        if en is None:
            return val
        self.used[en].add(val)
        if self.needed is None:
            return val
        rep = self.needed["rep"][en][val]
        lst = self.needed["reps"][en]
        i = self._bisect.bisect_left(lst, rep)
        assert i < len(lst) and lst[i] == rep, (en, val)
        return i + 1

    def _record(self, reads, writes, ev):
        s, val = ev
        for v in reads:
            if isinstance(v, V):
                for b in v.bufs:
                    if b.r.get(s, 0) < val:
                        b.r[s] = val
        for v in writes:
            for b in v.bufs:
                b.w = {s: val}
                b.r = {}

    def op(self, e, fn, reads, writes, pe_acc=False):
        deps = self._deps(reads, writes)
        if pe_acc:
            self._wait(e, deps, skip_own=True)
        else:
            self._wait(e, deps, skip_own=(e == "pe"))
        ins = fn(self.E[e])
        self.ecnt[e] += 1
        self.run_of[e].append(self.runid[e])
        if self.needed is None or self.ecnt[e] in self.needed_set[e]:
            ins.then_inc(self.esem[e], 1)
            self.log.append((e, "inc", self.esem[e], 1))
            self.nincs += 1
        ev = (self.esem[e], self.ecnt[e])
        self._record(reads, writes, ev)
        self.ninst += 1
        return ev

    def dma(self, out, in_, q="sp", store=False, **kw):
        deps = self._deps([in_], [out])
        pool = self.dpool[q]
        j = self.dnext[q] % len(pool)
        self.dnext[q] += 1
        sem, cnt = pool[j]
        if cnt > 0:
            deps[sem] = max(deps.get(sem, 0), cnt)
        self._wait(q, deps)
        ins = self.E[q].dma_start(out=out.ap, in_=in_.ap, **kw)
        ins.then_inc(sem, 16)
        self.log.append((q, "inc", sem, 16))
        pool[j][1] = cnt + 16
        ev = (sem, cnt + 16)
        self._record([in_], [out], ev)
        if store:
            self.stores.append(ev)
        self.ninst += 1
        return ev

    def idma(self, out, in_full, idx, q="pool"):
        deps = self._deps([idx], [out])
        pool = self.dpool[q]
        j = self.dnext[q] % len(pool)
        self.dnext[q] += 1
        sem, cnt = pool[j]
        if cnt > 0:
            deps[sem] = max(deps.get(sem, 0), cnt)
        self._wait(q, deps)
        ins = self.E[q].indirect_dma_start(out=out.ap, out_offset=None, in_=in_full.ap,
                                           in_offset=bass.IndirectOffsetOnAxis(ap=idx.ap, axis=0))
        ins.then_inc(sem, 16)
        self.log.append((q, "inc", sem, 16))
        pool[j][1] = cnt + 16
        ev = (sem, cnt + 16)
        self._record([idx], [out], ev)
        self.ninst += 1
        return ev

    def barrier(self):
        deps = {self.esem[e]: self.ecnt[e] for e in self.esem if self.ecnt[e] > 0}
        for s, val in self.stores:
            deps[s] = max(deps.get(s, 0), val)
        for q in self.dpool:
            for sem, cnt in self.dpool[q]:
                if cnt > 0:
                    deps[sem] = max(deps.get(sem, 0), cnt)
        self.stores = []
        for e in ("pe", "dve", "act", "pool", "sp"):
            eng = self.E[e]
            seen = self.seen[e]
            for s, val in deps.items():
                if seen.get(s, 0) < val:
                    sv = self._semval(s, val)
                    eng.wait_ge(s, sv)
                    seen[s] = val
                    self.runid[e] += 1
                    self.log.append((e, "wait", s, sv))

    def finish(self):
        self.barrier()

    def mm(self, out, lhsT, rhs, start=True, stop=True):
        return self.op("pe", lambda e: e.matmul(out.ap, lhsT.ap, rhs.ap, start=start, stop=stop),
                       [lhsT, rhs], [out], pe_acc=not start)

    def tr(self, out, in_, ident):
        return self.op("pe", lambda e: e.transpose(out.ap, in_.ap, ident.ap), [in_, ident], [out])

    def act(self, out, in_, func, bias=0.0, scale=1.0, accum=None, e="act"):
        reads = [in_] + [x for x in (bias, scale) if isinstance(x, V)]
        writes = [out] + ([accum] if accum is not None else [])
        b = bias.ap if isinstance(bias, V) else bias
        s = scale.ap if isinstance(scale, V) else scale
        kw = {}
        if accum is not None:
            kw["accum_out"] = accum.ap
        return self.op(e, lambda g: g.activation(out=out.ap, in_=in_.ap, func=func, bias=b, scale=s, **kw),
                       reads, writes)

    def tt(self, out, a, b, op, e="dve"):
        return self.op(e, lambda g: g.tensor_tensor(out.ap, a.ap, b.ap, op), [a, b], [out])

    def ts(self, out, a, s1, op0, s2=None, op1=None, e="dve", accum=None):
        reads = [a] + [x for x in (s1, s2) if isinstance(x, V)]
        x1 = s1.ap if isinstance(s1, V) else s1
        x2 = s2.ap if isinstance(s2, V) else s2
        kw = {}
        if op1 is not None:
            kw["op1"] = op1
        writes = [out]
        if accum is not None:
            kw["accum_out"] = accum.ap
            writes.append(accum)
        return self.op(e, lambda g: g.tensor_scalar(out.ap, a.ap, x1, x2, op0, **kw), reads, writes)

    def stt(self, out, in0, scalar, in1, op0, op1, e="dve"):
        reads = [in0, in1] + ([scalar] if isinstance(scalar, V) else [])
        sc = scalar.ap if isinstance(scalar, V) else scalar
        return self.op("dve", lambda g: g.scalar_tensor_tensor(out.ap, in0.ap, sc, in1.ap, op0, op1), reads, [out])

    def copy(self, out, in_, e="dve"):
        if e == "act":
            return self.op(e, lambda g: g.copy(out.ap, in_.ap), [in_], [out])
        return self.op(e, lambda g: g.tensor_copy(out.ap, in_.ap), [in_], [out])

    def memset(self, out, val, e="dve"):
        return self.op(e, lambda g: g.memset(out.ap, val), [], [out])

    def reduce(self, out, in_, op=ALU.add, axis=AX.X, e="dve"):
        return self.op(e, lambda g: g.tensor_reduce(out.ap, in_.ap, axis, op), [in_], [out])

    def recip(self, out, in_):
        return self.op("dve", lambda g: g.reciprocal(out.ap, in_.ap), [in_], [out])


def simulate_log(log):
    streams = {}
    for it in log:
        streams.setdefault(it[0], []).append(it)
    pos = {e: 0 for e in streams}
    sem = {}
    progress = True
    while progress:
        progress = False
        for e, st in streams.items():
            while pos[e] < len(st):
                _, kind, s, val = st[pos[e]]
                if kind == "wait":
                    if sem.get(id(s), 0) >= val:
                        pos[e] += 1
                        progress = True
                    else:
                        break
                else:
                    sem[id(s)] = sem.get(id(s), 0) + val
                    pos[e] += 1
                    progress = True
    stuck = {e: (pos[e], len(st)) for e, st in streams.items() if pos[e] < len(st)}
    return stuck


def two_pass(build_fn):
    k1 = K()
    build_fn(k1)
    rep, reps = {}, {}
    for e, v in k1.used.items():
        ro = k1.run_of[e]
        last = {}
        for idx in sorted(v):
            last[ro[idx - 1]] = idx
        rep[e] = {idx: last[ro[idx - 1]] for idx in v}
        reps[e] = sorted(set(rep[e].values()))
    k2 = K(needed={"rep": rep, "reps": reps})
    r = build_fn(k2)
    return k2, r

D = 2048
NKC = 16
DFF = 5632
EPS = 1e-6
NHV = 32
NHK = 16
CONV_DIM_ = 8192
GIN = 12352
QL = 512
KVL = 512
ROPE = 64
NH = 16
SM_SCALE = (128 + 64) ** -0.5
NEG = -30000.0


class Ctx:
    pass


def load_const(k, dram, shape, dtype=F32, q="sp"):
    t = k.sb(list(shape), dtype)
    k.dma(t[:], dram[:], q=q)
    return t


class WS:
    def __init__(self, k, nelem):
        self.k = k
        self.n = nelem
        self.b = [k.sb([128, nelem], BF16, name="wbuf") for _ in range(2)]
        self.i = 0

    def get(self, nk, ncols):
        t = self.b[self.i % 2]
        self.i += 1
        assert nk * ncols <= self.n, (nk, ncols, self.n)
        return t[:, 0:nk * ncols].re("p (k n) -> p k n", n=ncols)

    def load(self, wd, r0, nk, colspecs, rows=128):
        tot = sum(n for _, n in colspecs)
        v = self.get(nk, tot)
        o = 0
        for c0, n in colspecs:
            src = wd[r0:r0 + nk * rows, c0:c0 + n].re("(k p) n -> p k n", p=rows)
            self.k.dma(v[0:rows, :, o:o + n], src, q="pool")
            o += n
        return v


def transpose_in(k, c, xd, row0, T, xT):
    ts_ = min(128, T)
    nsub = T // ts_
    m = k.mark()
    stg = [k.sb([128, D], F32, name="xstg") for _ in range(2)]
    for s in range(nsub):
        st = stg[s % 2]
        k.dma(st[0:ts_, :], xd[row0 + s * ts_: row0 + (s + 1) * ts_, :])
        for g in range(4):
            p = k.ps()
            for j in range(4):
                kc = g * 4 + j
                k.tr(p[:, j * ts_:(j + 1) * ts_], st[0:ts_, kc * 128:(kc + 1) * 128], c.ident[0:ts_, 0:ts_])
            src = p[:, 0:4 * ts_].re("p (j t) -> p j t", t=ts_)
            dst = xT[:, g * 4:(g + 1) * 4, s * ts_:(s + 1) * ts_]
            k.copy(dst, src, e="act" if g % 2 else "dve")
    k.release(m)


def transpose_out(k, c, srcfn, nkc, T, dd, row0, col0=0, rows=128):
    ts_ = min(128, T)
    nsub = T // ts_
    m = k.mark()
    stg = [k.sb([128, nkc * rows], F32, name="ostg") for _ in range(2)]
    for s in range(nsub):
        st = stg[s % 2]
        for g0 in range(0, nkc, 4):
            ng = min(4, nkc - g0)
            p = k.ps()
            for j in range(ng):
                src = srcfn(g0 + j)[:, s * ts_:(s + 1) * ts_]
                k.tr(p[0:ts_, j * rows:(j + 1) * rows], src, c.ident[0:rows, 0:rows])
            k.copy(st[0:ts_, g0 * rows:(g0 + ng) * rows], p[0:ts_, 0:ng * rows], e="act" if (g0 // 4) % 2 else "dve")
        k.dma(dd[row0 + s * ts_: row0 + (s + 1) * ts_, col0:col0 + nkc * rows], st[0:ts_, :], store=True)
    k.release(m)


def rstd_from_psum(k, out, ps, n, eps=EPS):
    k.ts(out, ps, 1.0 / n, ALU.mult, eps, ALU.add)
    k.act(out, out, AF.Sqrt)
    k.recip(out, out)


def bc3(v, nseq, tps):
    return v.un(2).bc([128, nseq, tps])


def v3(v, tps):
    return v.re("p (s t) -> p s t", t=tps)


def modnorm(k, c, xT, T, gs, sh, mcol, nseq, tps, hT):
    m = k.mark()
    sq = [k.sb([128, T], F32, name="sq") for _ in range(2)]
    pss = k.ps()
    for kc in range(NKC):
        s_ = sq[kc % 2]
        k.act(s_[:], xT[:, kc, :], AF.Square)
        k.mm(pss[:, 0:T], c.ones[:], s_[:], start=(kc == 0), stop=(kc == NKC - 1))
    rstd = k.sb([128, T], F32, name="rstd")
    rstd_from_psum(k, rstd[:], pss[:, 0:T], D)
    tmp = [k.sb([128, T], F32, name="mtmp") for _ in range(2)]
    for kc in range(NKC):
        t_ = tmp[kc % 2]
        e1 = "dve" if kc % 2 == 0 else "pool"
        k.tt(t_[:], xT[:, kc, :], rstd[:], ALU.mult, e=e1)
        k.tt(v3(t_[:], tps), v3(t_[:], tps), bc3(gs(kc), nseq, tps), ALU.mult, e=e1)
        k.tt(v3(hT[:, kc, :], tps), v3(t_[:], tps), bc3(sh(kc), nseq, tps), ALU.add, e=e1)
    k.release(m)


def ada_phase(k, c, wd, bd, N, name):
    nch = N // 128
    modsT = k.sb([128, nch, 17], F32, name=name)
    m = k.mark()
    bT = k.sb([128, nch], F32, name="bT")
    brow = k.sb([128, 128], F32, name="brow")
    k.dma(brow[0:nch, :], bd[:].re("(c p) -> c p", p=128))
    p = k.ps()
    k.tr(p[:, 0:nch], brow[0:nch, :], c.ident[0:nch, 0:nch])
    k.copy(bT[:], p[:, 0:nch])
    ws = WS(k, NKC * 512)
    nblk = N // 512
    nxt = ws.load(wd, 0, NKC, [(0, 512)])
    for b in range(nblk):
        w = nxt
        if b + 1 < nblk:
            nxt = ws.load(wd, 0, NKC, [((b + 1) * 512, 512)])
        p = k.ps()
        for sub in range(4):
            for kc in range(NKC):
                k.mm(p[:, sub * 17:(sub + 1) * 17], w[:, kc, sub * 128:(sub + 1) * 128], c.scT[:, kc, :],
                     start=(kc == 0), stop=(kc == NKC - 1))
        src = p[:, 0:4 * 17].re("p (j t) -> p j t", t=17)
        k.tt(modsT[:, b * 4:(b + 1) * 4, :], src, bT[:, b * 4:(b + 1) * 4].un(2).bc([128, 4, 17]), ALU.add)
    k.release(m)
    return modsT


def linear_fm(k, ws, wd, r0, nk, c0, ncols, blk, actfn, T, consume):
    nblk = (ncols + blk - 1) // blk
    def spec(b):
        return [(c0 + b * blk, min(blk, ncols - b * blk))]
    nxt = ws.load(wd, r0, nk, spec(0))
    for b in range(nblk):
        w = nxt
        if b + 1 < nblk:
            nxt = ws.load(wd, r0, nk, spec(b + 1))
        nc_ = min(blk, ncols - b * blk)
        for j in range(0, nc_, 128):
            mcols = min(128, nc_ - j)
            p = k.ps()
            for kc in range(nk):
                k.mm(p[0:mcols, 0:T], w[:, kc, j:j + mcols], actfn(kc), start=(kc == 0), stop=(kc == nk - 1))
            consume((b * blk + j) // 128, p)


def ffn_phase(k, c, L, xT, T, mods, mcol, nseq, tps, l):
    m = k.mark()
    hT = k.sb([128, NKC, T], BF16, nsplit=NKC, name="hT")
    k.ts(c.gs[:, :, :], mods[:, 4 * 16:5 * 16, :], 1.0, ALU.add)
    k.tt(c.gs[:, :, :], c.gs[:, :, :], c.gffn[:, l * 16:(l + 1) * 16].un(2).bc([128, 16, 17]), ALU.mult)
    modnorm(k, c, xT, T, lambda kc: c.gs[:, kc, mcol], lambda kc: mods[:, 3 * 16 + kc, mcol], mcol, nseq, tps, hT)
    ws = WS(k, NKC * 512)
    NQ = 2
    CQ = DFF // NQ
    nq = CQ // 128
    actT = k.sb([128, nq, T], BF16, nsplit=nq, name="actT")
    gsil = [k.sb([128, T], F32, name="gsil") for _ in range(2)]
    wgu = L.w_gate_up[l]
    wdn = L.w_down[l]
    for qd in range(NQ):
        nb = CQ // 256
        def ld(bi):
            col = qd * CQ + bi * 256
            return ws.load(wgu, 0, NKC, [(col, 256), (DFF + col, 256)])
        nxt = ld(0)
        for bi in range(nb):
            w = nxt
            if bi + 1 < nb:
                nxt = ld(bi + 1)
            for jj in range(2):
                j = bi * 2 + jj
                pg = k.ps()
                pu = k.ps()
                for kc in range(NKC):
                    k.mm(pg[:, 0:T], w[:, kc, jj * 128:(jj + 1) * 128], hT[:, kc, :], start=(kc == 0), stop=(kc == NKC - 1))
                for kc in range(NKC):
                    k.mm(pu[:, 0:T], w[:, kc, 256 + jj * 128:256 + (jj + 1) * 128], hT[:, kc, :], start=(kc == 0), stop=(kc == NKC - 1))
                g_ = gsil[j % 2]
                k.act(g_[:], pg[:, 0:T], AF.Silu)
                k.tt(actT[:, j, :], g_[:], pu[:, 0:T], ALU.mult)
        def cons(oc, p):
            resid_add(k, c, xT, oc, p, T, mods[:, 5 * 16 + oc, mcol], nseq, tps)
        linear_fm(k, ws, wdn, qd * CQ, nq, 0, D, 256, lambda kc: actT[:, kc, :], T, cons)
    k.release(m)


def resid_add(k, c, xT, oc, p, T, gate, nseq, tps):
    if nseq == 1:
        k.stt(xT[:, oc, :], p[:, 0:T], gate, xT[:, oc, :], ALU.mult, ALU.add)
    else:
        t_ = c.rtmp[c.rti % 2]
        c.rti += 1
        k.tt(v3(t_[:, 0:T], tps), v3(p[:, 0:T], tps), bc3(gate, nseq, tps), ALU.mult)
        k.tt(xT[:, oc, :], xT[:, oc, :], t_[:, 0:T], ALU.add, e="pool")


def gdn_chunks_batched(k, c, G, st, kh, B, kTb, qTb, kn, xc, zs, beta, negg, gcs, edl, bg, ogT):
    NCH = 4
    nlev = getattr(c, "nlev", 6)
    gstop = int(os.environ.get("GSTOP", "99"))
    def c3(v):
        return v.re("p (c j) -> p c j", j=128)
    def cs_(ch):
        return slice(ch * 128, (ch + 1) * 128)
    k.ps_rot = [3, 4, 5, 6, 7]
    pG, pQK, pK = k.psb[0], k.psb[1], k.psb[2]
    for ch in range(NCH):
        cs = cs_(ch)
        k.mm(pG[:, cs], kTb[:, cs], kTb[:, cs])
        k.mm(pQK[:, cs], qTb[:, cs], kTb[:, cs])
        k.tr(pK[:, cs], kn[:, cs], c.ident[:, :])
    X1, X2, R = B.X1, B.X2, B.R
    Ub = G.U[:, :].un(1).bc([128, NCH, 128])
    Mnb = G.Mn[:, :].un(1).bc([128, NCH, 128])
    SLb = G.SL[:, :].un(1).bc([128, NCH, 128])
    Ib = c.ident[:, :].un(1).bc([128, NCH, 128])
    for a in range(2):
        hh = 2 * kh + a
        def bcol(t):
            return t[:, :, hh:hh + 1].bc([128, NCH, 128])
        pV = k.ps()
        for ch in range(NCH):
            k.tr(pV[:, cs_(ch)], xc[2 + a][:, cs_(ch)], c.ident[:, :])
        k.tt(c3(B.vb[:, :]), c3(pV[:, :]), bcol(beta), ALU.mult)
        k.tt(c3(B.kbg[:, :]), c3(pK[:, :]), bcol(bg), ALU.mult)
        k.tt(c3(B.kd[:, :]), c3(pK[:, :]), bcol(edl), ALU.mult)
        if gstop <= 1:
            continue
        k.tt(c3(X1[:, :]), Ub, bcol(negg), ALU.mult, e="pool")
        pE = k.ps()
        k.mm(pE[:, :], c.ones[:, :], X1[:, :])
        k.tt(c3(X2[:, :]), c3(pE[:, :]), Mnb, ALU.add)
        k.tt(c3(X2[:, :]), c3(X2[:, :]), bcol(gcs), ALU.add, e="pool")
        k.act(X2[:, :], X2[:, :], AF.Exp)
        k.act(X1[:, :], pE[:, :], AF.Exp, scale=-1.0)
        if gstop <= 2:
            continue
        k.tt(B.oT[:, :], pG[:, :], X2[:, :], ALU.mult)
        if gstop == 21:
            continue
        k.tt(c3(B.oT[:, :]), c3(B.oT[:, :]), SLb, ALU.mult, e="pool")
        if gstop == 22:
            continue
        A0f = B.Af[0]
        k.tt(c3(A0f[:, :]), c3(B.oT[:, :]), bcol(beta), ALU.mult)
        k.tt(B.qkb[:, :], pQK[:, :], X2[:, :], ALU.mult)
        pBt = k.ps()
        for ch in range(NCH):
            k.tr(pBt[:, cs_(ch)], A0f[:, cs_(ch)], c.ident[:, :])
        ptr = k.ps()
        ptrb = ptr[:, :].bitcast(BF16)
        for ch in range(NCH):
            k.tr(ptrb[:, ch * 128:(ch + 1) * 128], B.qkb[:, cs_(ch)], B.identbf[:, :])
        k.copy(B.Bf[0][:, :], pBt[:, :], e="act")
        k.copy(B.qkT[:, :], ptrb[:, 0:512], e="dve")
        k.tt(c3(R[0][:, :]), Ib, c3(B.Bf[0][:, :]), ALU.subtract, e="pool")
        Ac, Bc, Rc, ri = A0f, B.Bf[0], R[0], 0
        for l in range(nlev):
            An = B.Af[(l + 1) % 2]
            Bn = B.Bf[(l + 1) % 2]
            pA = k.ps()
            for ch in range(NCH):
                k.mm(pA[:, cs_(ch)], Bc[:, cs_(ch)], Ac[:, cs_(ch)])
            if l < nlev - 1:
                pB = k.ps()
                for ch in range(NCH):
                    k.mm(pB[:, cs_(ch)], Ac[:, cs_(ch)], Bc[:, cs_(ch)])
            k.copy(An[:, :], pA[:, :], e="act")
            if l < nlev - 1:
                k.copy(Bn[:, :], pB[:, :], e="dve")
            pR = k.ps()
            for ch in range(NCH):
                k.mm(pR[:, cs_(ch)], An[:, cs_(ch)], Rc[:, cs_(ch)])
            Rn = R[(ri + 1) % 2]
            ri += 1
            k.tt(Rn[:, :], pR[:, :], Rc[:, :], ALU.add)
            Rc = Rn
            Ac, Bc = An, Bn
        k.copy(B.Rb[:, :], Rc[:, :], e="pool")
        if gstop <= 5:
            continue
        pw = k.ps()
        for ch in range(NCH):
            k.mm(pw[:, cs_(ch)], B.kbg[:, cs_(ch)], B.Rb[:, cs_(ch)])
        k.act(B.wTn[:, :], pw[:, :], AF.Copy, scale=-1.0)
        k.tt(B.qg[:, :], qTb[:, :], X1[:, :], ALU.mult, e="pool")
        for ch in range(NCH):
            cs = cs_(ch)
            Sbt = B.Sbh[ch % 2]
            k.copy(Sbt[:, :], st.S[:, hh, :], e="pool")
            pv = k.ps()
            k.mm(pv[:, 0:128], B.Rb[:, cs], B.vb[:, cs], start=True, stop=False)
            k.mm(pv[:, 0:128], B.wTn[:, cs], Sbt[:, :], start=False, stop=True)
            vn = B.vn[ch % 2]
            k.copy(vn[:, :], pv[:, 0:128], e="act")
            po = k.ps()
            k.mm(po[:, 0:128], Sbt[:, :], B.qg[:, cs], start=True, stop=False)
            k.mm(po[:, 0:128], vn[:, :], B.qkT[:, cs], start=False, stop=True)
            k.copy(B.oT[:, cs], po[:, 0:128], e="act")
            pS = k.ps()
            k.mm(pS[:, 0:128], B.kd[:, cs], vn[:, :])
            k.stt(st.S[:, hh, :], st.S[:, hh, :], X1[:, ch * 128 + 127:ch * 128 + 128], pS[:, 0:128], ALU.mult, ALU.add)
        if gstop <= 6:
            continue
        k.tt(X2[:, :], B.oT[:, :], B.oT[:, :], ALU.mult, e="pool")
        pq = k.ps()
        k.mm(pq[:, :], c.ones[:, :], X2[:, :])
        orn = R[0]
        rstd_from_psum(k, orn[:, :], pq[:, :], 128)
        k.tt(B.oT[:, :], B.oT[:, :], orn[:, :], ALU.mult)
        k.stt(ogT[:, hh, :], B.oT[:, :], c.gnorm[:, 0:1], zs[a][:, :], ALU.mult, ALU.mult)
    k.ps_rot = [2, 3, 4, 5, 6, 7]


def gdn_phase(k, c, L, xT, T, C, mods, mcol, nseq, tps, G, st, samp):
    nch = T // C
    nlev = 1 if samp else getattr(c, 'nlev', 6)
    m = k.mark()
    k.ps_rot = [2, 3, 4, 5, 6, 7]
    hT = k.sb([128, NKC, T], BF16, nsplit=NKC, name="hT")
    k.ts(c.gs[:, :, :], mods[:, 16:32, :], 1.0, ALU.add)
    k.tt(c.gs[:, :, :], c.gs[:, :, :], c.gmix[:, 0:16].un(2).bc([128, 16, 17]), ALU.mult)
    modnorm(k, c, xT, T, lambda kc: c.gs[:, kc, mcol], lambda kc: mods[:, kc, mcol], mcol, nseq, tps, hT)
    ogT = k.sb([128, NHV, T], BF16, nsplit=NHV, name="ogT")
    ws = WS(k, NKC * 512)
    wba = ws.load(L.gdn_w_in, 0, NKC, [(12288, 64)])
    beta = k.sb([128, nch, 32], F32, nsplit=nch, name="beta")
    gg = k.sb([128, nch, 32], F32, nsplit=nch, name="gg")
    negg = k.sb([128, nch, 32], F32, nsplit=nch, name="negg")
    gcs = k.sb([128, nch, 32], F32, nsplit=nch, name="gcs")
    edl = k.sb([128, nch, 32], F32, nsplit=nch, name="edl")
    bg = k.sb([128, nch, 32], F32, nsplit=nch, name="bg")
    t1 = k.sb([128, 32], F32, name="t1")
    t2 = k.sb([128, 32], F32, name="t2")
    for ch in range(nch):
        p = k.ps()
        for kc in range(NKC):
            k.mm(p[0:C, 0:64], hT[:, kc, ch * C:(ch + 1) * C], wba[:, kc, 0:64], start=(kc == 0), stop=(kc == NKC - 1))
        k.act(beta[0:C, ch, :], p[0:C, 0:32], AF.Sigmoid)
        k.tt(t1[0:C, :], p[0:C, 32:64], c.dtb[0:C, :], ALU.add)
        k.act(t2[0:C, :], t1[0:C, :], AF.Abs)
        k.act(t2[0:C, :], t2[0:C, :], AF.Exp, scale=-1.0)
        k.act(t2[0:C, :], t2[0:C, :], AF.Ln, bias=1.0)
        k.ts(t1[0:C, :], t1[0:C, :], 0.0, ALU.max)
        k.tt(t1[0:C, :], t1[0:C, :], t2[0:C, :], ALU.add)
        k.tt(gg[0:C, ch, :], t1[0:C, :], c.nega[0:C, :], ALU.mult)
        k.ts(negg[0:C, ch, :], gg[0:C, ch, :], -1.0, ALU.mult)
        p2 = k.ps()
        k.mm(p2[0:C, 0:32], G.U[0:C, 0:C], gg[0:C, ch, :])
        k.mm(p2[0:C, 32:64], G.OB[0:C, 0:C], gg[0:C, ch, :])
        k.copy(gcs[0:C, ch, :], p2[0:C, 0:32])
        k.tt(t1[0:C, :], p2[0:C, 32:64], gcs[0:C, ch, :], ALU.subtract)
        k.act(edl[0:C, ch, :], t1[0:C, :], AF.Exp)
        k.act(t2[0:C, :], gcs[0:C, ch, :], AF.Exp)
        k.tt(bg[0:C, ch, :], t2[0:C, :], beta[0:C, ch, :], ALU.mult)
    if getattr(c, "stage", 99) <= 2:
        c.dbg = [("beta", beta[:, :, :]), ("gg", gg[:, :, :]), ("gcs", gcs[:, :, :]), ("edl", edl[:, :, :]), ("hT0", None)]
        return
    cb = [k.sb([128, nseq, 3 + tps], F32, name="cb") for _ in range(2)]
    xc = [k.sb([128, T], F32, name="xc") for _ in range(4)]
    qTb = k.sb([128, T], BF16, name="qTb")
    kTb = k.sb([128, T], BF16, name="kTb")
    kn = k.sb([128, T], F32, name="kn")
    zs = [k.sb([128, T], BF16, name="zs") for _ in range(2)]
    sqt = k.sb([128, T], F32, name="sqt")
    rn = k.sb([128, T], F32, name="rn")
    def f32t(n="ct"):
        return k.sb([128, 128], F32, name=n)
    if not samp:
        Bt = Ctx()
        def b16(n, w=512):
            return k.sb([128, w], BF16, name=n)
        Bt.qkb = b16("qkb")
        Bt.Af = [k.sb([128, 512], F32, name="Af0"), k.sb([128, 512], F32, name="Af1")]
        Bt.Bf = [k.sb([128, 512], F32, name="Bf0"), k.sb([128, 512], F32, name="Bf1")]
        Bt.Rb = b16("Rb"); Bt.qkT = b16("qkT"); Bt.vb = b16("vb"); Bt.kbg = b16("kbg"); Bt.kd = b16("kd")
        Bt.wTn = b16("wTn"); Bt.qg = b16("qg")
        Bt.vn = [b16("vn0", 128), b16("vn1", 128)]; Bt.Sbh = [b16("Sbh0", 128), b16("Sbh1", 128)]
        Bt.oT = k.sb([128, 512], F32, name="oTall")
        Bt.identbf = b16("identbf", 128)
        k.copy(Bt.identbf[:, :], c.ident[:, :])
        Bt.X1 = xc[0]; Bt.X2 = xc[1]; Bt.R = [sqt, rn]
    else:
        rh = [f32t("rh") for _ in range(2)]
        tmpE = [f32t("tmpE") for _ in range(2)]
        E1 = [f32t("E1") for _ in range(2)]
        egrow = [f32t("egrow") for _ in range(2)]
        Am = [f32t("Am") for _ in range(4)]
        Bm = [f32t("Bm") for _ in range(4)]
        qk = [f32t("qk") for _ in range(2)]
        Rr = [f32t("Rr") for _ in range(4)]
        qkT = [k.sb([128, 128], BF16, name="qkT") for _ in range(2)]
        Rb = [k.sb([128, 128], BF16, name="Rb") for _ in range(2)]
        vb = [k.sb([128, 128], BF16, name="vb") for _ in range(2)]
        kbg = [k.sb([128, 128], BF16, name="kbg") for _ in range(2)]
        kd = [k.sb([128, 128], BF16, name="kd") for _ in range(2)]
        wTn = [k.sb([128, 128], BF16, name="wTn") for _ in range(2)]
        vn = [k.sb([128, 128], BF16, name="vn") for _ in range(2)]
        qg = [k.sb([128, 128], BF16, name="qg") for _ in range(2)]
        Sbh = [k.sb([128, 128], BF16, name="Sbh") for _ in range(2)]
        osb = [f32t("osb") for _ in range(2)]
        osq = [f32t("osq") for _ in range(2)]
        orn = [f32t("orn") for _ in range(2)]
        if samp:
            vnT = [f32t("vnT") for _ in range(2)]
            kds = [k.sb([128, 128], BF16, name="kds") for _ in range(2)]
            Sall = [k.sb([128, 16, 128], F32, name="Sall") for _ in range(2)]
            Sball = [k.sb([128, 16, 128], BF16, name="Sball") for _ in range(2)]
    cnt = 0
    for kh in range(NHK):
        w = ws.load(L.gdn_w_in, 0, NKC, [(kh * 128, 128), (2048 + kh * 128, 128), (4096 + kh * 256, 256)])
        wz = ws.load(L.gdn_w_in, 0, NKC, [(8192 + kh * 256, 256)])
        ids = [kh, 16 + kh, 32 + 2 * kh, 33 + 2 * kh]
        for ci in range(4):
            p = k.ps()
            for kc in range(NKC):
                k.mm(p[:, 0:T], w[:, kc, ci * 128:(ci + 1) * 128], hT[:, kc, :], start=(kc == 0), stop=(kc == NKC - 1))
            cid = ids[ci]
            cb_ = cb[ci % 2]
            k.copy(cb_[:, :, 3:3 + tps], v3(p[:, 0:T], tps), e="act")
            k.copy(cb_[:, :, 0:3], st.convtail[:, cid, :, :], e="pool")
            a_ = v3(xc[ci][:], tps)
            k.ts(a_, cb_[:, :, 0:tps], c.wconv[:, cid, 0:1], ALU.mult, e="pool")
            for j in range(1, 4):
                k.stt(a_, cb_[:, :, j:j + tps], c.wconv[:, cid, j:j + 1], a_, ALU.mult, ALU.add,
                      e="pool" if j % 2 else "dve")
            k.copy(st.convtail[:, cid, :, :], cb_[:, :, tps:tps + 3], e="pool")
            k.act(xc[ci][:], xc[ci][:], AF.Silu)
        for ci in range(2):
            k.tt(sqt[:], xc[ci][:], xc[ci][:], ALU.mult, e="pool")
            p = k.ps()
            k.mm(p[:, 0:T], c.ones[:], sqt[:])
            k.ts(rn[:], p[:, 0:T], EPS, ALU.add)
            k.act(rn[:], rn[:], AF.Sqrt)
            k.recip(rn[:], rn[:])
            if ci == 0:
                k.stt(qTb[:], xc[0][:], 128 ** -0.5, rn[:], ALU.mult, ALU.mult)
            else:
                k.tt(kn[:], xc[1][:], rn[:], ALU.mult)
                k.copy(kTb[:], kn[:], e="pool")
        for a in range(2):
            p = k.ps()
            for kc in range(NKC):
                k.mm(p[:, 0:T], wz[:, kc, a * 128:(a + 1) * 128], hT[:, kc, :], start=(kc == 0), stop=(kc == NKC - 1))
            k.act(zs[a][:], p[:, 0:T], AF.Silu)
        if getattr(c, "stage", 99) <= 3:
            c.dbg = [("xc0", xc[0][:]), ("xc2", xc[2][:]), ("kn", kn[:]), ("rn", rn[:])]
            return
        if samp:
            for a in range(2):
                hh = 2 * kh + a
                k.dma(Sall[a][:, :, :], st.ssm_in[:, hh, :, :].re("s p v -> p s v"))
                k.copy(Sball[a][:, :, :], Sall[a][:, :, :], e="pool")
        if not samp:
            gdn_chunks_batched(k, c, G, st, kh, Bt, kTb, qTb, kn, xc, zs, beta, negg, gcs, edl, bg, ogT)
            continue
        for ch in range(nch):
            cs = slice(ch * C, (ch + 1) * C)
            pG = k.psb[0]
            k.mm(pG[0:C, 0:C], kTb[:, cs], kTb[:, cs])
            k.mm(pG[0:C, C:2 * C], qTb[:, cs], kTb[:, cs])
            pT = k.psb[1]
            k.tr(pT[0:C, 0:128], kn[:, cs], c.ident[:])
            for a in range(2):
                k.tr(pT[0:C, 128 * (1 + a):128 * (2 + a)], xc[2 + a][:, cs], c.ident[:])
            for a in range(2):
                hh = 2 * kh + a
                i2 = cnt % 2
                cnt += 1
                hcol = slice(hh, hh + 1)
                k.ts(rh[i2][0:C, 0:C], G.U[0:C, 0:C], negg[0:C, ch, hcol], ALU.mult, e="pool")
                pE = k.ps()
                k.mm(pE[:, 0:C], c.ones[0:C, :], rh[i2][0:C, 0:C])
                k.tt(tmpE[i2][0:C, 0:C], pE[0:C, 0:C], G.Mn[0:C, 0:C], ALU.add)
                k.act(E1[i2][0:C, 0:C], tmpE[i2][0:C, 0:C], AF.Exp, bias=gcs[0:C, ch, hcol])
                k.act(egrow[i2][:, 0:C], pE[:, 0:C], AF.Exp, scale=-1.0)
                if getattr(c, "stage", 99) == 35:
                    c.dbg = [("E1", E1[i2][0:C, 0:C]), ("egrow", egrow[i2][:, 0:C]), ("tmpE", tmpE[i2][0:C, 0:C])]
                    return
                A0 = Am[0]
                k.stt(A0[0:C, 0:C], pG[0:C, 0:C], beta[0:C, ch, hcol], E1[i2][0:C, 0:C], ALU.mult, ALU.mult)
                k.tt(A0[0:C, 0:C], A0[0:C, 0:C], G.SL[0:C, 0:C], ALU.mult, e="pool")
                k.tt(qk[i2][0:C, 0:C], pG[0:C, C:2 * C], E1[i2][0:C, 0:C], ALU.mult)
                pB = k.ps()
                k.tr(pB[0:C, 0:C], A0[0:C, 0:C], c.ident[0:C, 0:C])
                k.tr(pB[0:C, C:2 * C], qk[i2][0:C, 0:C], c.ident[0:C, 0:C])
                B0 = Bm[0]
                k.copy(B0[0:C, 0:C], pB[0:C, 0:C], e="act")
                k.copy(qkT[i2][0:C, 0:C], pB[0:C, C:2 * C], e="act")
                R = Rr[0]
                k.tt(R[0:C, 0:C], c.ident[0:C, 0:C], B0[0:C, 0:C], ALU.subtract, e="pool")
                if getattr(c, "stage", 99) == 36:
                    c.dbg = [("R", R[0:C, 0:C]), ("A0", A0[0:C, 0:C]), ("B0", B0[0:C, 0:C]), ("qk", qk[i2][0:C, 0:C])]
                    return
                Ac, Bc = A0, B0
                ri = 0
                for l in range(nlev):
                    An = Am[(l + 1) % 4]
                    Bn = Bm[(l + 1) % 4]
                    pA = k.ps()
                    k.mm(pA[0:C, 0:C], Bc[0:C, 0:C], Ac[0:C, 0:C])
                    k.copy(An[0:C, 0:C], pA[0:C, 0:C], e="act")
                    if l < nlev - 1:
                        pA2 = k.ps()
                        k.mm(pA2[0:C, 0:C], Ac[0:C, 0:C], Bc[0:C, 0:C])
                        k.copy(Bn[0:C, 0:C], pA2[0:C, 0:C], e="dve")
                    if getattr(c, "skipR", 0) and l >= 1:
                        Ac, Bc = An, Bn
                        continue
                    pR = k.ps()
                    k.mm(pR[0:C, 0:C], An[0:C, 0:C], R[0:C, 0:C])
                    Rn = Rr[(ri + 1) % 4]
                    ri += 1
                    k.tt(Rn[0:C, 0:C], pR[0:C, 0:C], R[0:C, 0:C], ALU.add)
                    R = Rn
                    Ac, Bc = An, Bn
                k.copy(Rb[i2][0:C, 0:C], R[0:C, 0:C], e="pool")
                if getattr(c, "stage", 99) <= 4:
                    c.dbg = [("R", R[0:C, 0:C]), ("E1", E1[i2][0:C, 0:C]), ("A0", Am[0][0:C, 0:C]), ("B0", Bm[0][0:C, 0:C]), ("egrow", egrow[i2][:, 0:C])]
                    return
                k.ts(vb[i2][0:C, :], pT[0:C, 128 * (1 + a):128 * (2 + a)], beta[0:C, ch, hcol], ALU.mult)
                k.act(kbg[i2][0:C, :], pT[0:C, 0:128], AF.Copy, scale=bg[0:C, ch, hcol])
                k.act(kd[i2][0:C, :], pT[0:C, 0:128], AF.Copy, scale=edl[0:C, ch, hcol])
                pw = k.ps()
                k.mm(pw[:, 0:C], kbg[i2][0:C, :], Rb[i2][0:C, 0:C])
                k.act(wTn[i2][:, 0:C], pw[:, 0:C], AF.Copy, scale=-1.0)
                k.tt(qg[i2][:, 0:C], qTb[:, cs], egrow[i2][:, 0:C], ALU.mult, e="pool")
                po = k.ps()
                if not samp:
                    Sbt = Sbh[i2]
                    k.copy(Sbt[:, :], st.S[:, hh, :], e="pool")
                    Sb_h = Sbt[:, :]
                    pv = k.ps()
                    k.mm(pv[0:C, 0:128], Rb[i2][0:C, 0:C], vb[i2][0:C, :], start=True, stop=False)
                    k.mm(pv[0:C, 0:128], wTn[i2][:, 0:C], Sb_h, start=False, stop=True)
                    k.copy(vn[i2][0:C, :], pv[0:C, 0:128], e="act")
                    k.mm(po[:, 0:C], Sb_h, qg[i2][:, 0:C], start=True, stop=False)
                    k.mm(po[:, 0:C], vn[i2][0:C, :], qkT[i2][0:C, 0:C], start=False, stop=True)
                    k.copy(osb[i2][:, 0:C], po[:, 0:C], e="act")
                    pS = k.ps()
                    k.mm(pS[:, 0:128], kd[i2][0:C, :], vn[i2][0:C, :])
                    k.stt(st.S[:, hh, :], st.S[:, hh, :], egrow[i2][:, C - 1:C], pS[:, 0:128], ALU.mult, ALU.add)
                else:
                    pu = k.ps()
                    k.mm(pu[:, 0:C], vb[i2][0:C, :], Rb[i2][0:C, 0:C], start=True, stop=False)
                    for s in range(nseq):
                        k.mm(pu[:, s * tps:(s + 1) * tps], Sball[a][:, s, :], wTn[i2][:, s * tps:(s + 1) * tps],
                             start=False, stop=(s == nseq - 1))
                    k.copy(vnT[i2][:, 0:C], pu[:, 0:C], e="act")
                    pvt = k.ps()
                    k.tr(pvt[0:C, 0:128], vnT[i2][:, 0:C], c.ident[:])
                    k.copy(vn[i2][0:C, :], pvt[0:C, 0:128], e="act")
                    k.mm(po[:, 0:C], vn[i2][0:C, :], qkT[i2][0:C, 0:C], start=True, stop=False)
                    for s in range(nseq):
                        k.mm(po[:, s * tps:(s + 1) * tps], Sball[a][:, s, :], qg[i2][:, s * tps:(s + 1) * tps],
                             start=False, stop=(s == nseq - 1))
                    k.copy(osb[i2][:, 0:C], po[:, 0:C], e="act")
                    for s in range(nseq):
                        j2 = s % 2
                        k.ts(kds[j2][0:C, :], kd[i2][0:C, :], G.seqmask[0:C, s:s + 1], ALU.mult, e="pool")
                        pS = k.ps()
                        k.mm(pS[:, 0:128], kds[j2][0:C, :], vn[i2][0:C, :])
                        k.stt(Sall[a][:, s, :], Sall[a][:, s, :], egrow[i2][:, s * tps + tps - 1:s * tps + tps],
                              pS[:, 0:128], ALU.mult, ALU.add)
                k.tt(osq[i2][:, 0:C], osb[i2][:, 0:C], osb[i2][:, 0:C], ALU.mult, e="pool")
                pq = k.ps()
                k.mm(pq[:, 0:C], c.ones[:], osq[i2][:, 0:C])
                rstd_from_psum(k, orn[i2][:, 0:C], pq[:, 0:C], 128)
                k.tt(osb[i2][:, 0:C], osb[i2][:, 0:C], orn[i2][:, 0:C], ALU.mult)
                k.stt(ogT[:, hh, cs], osb[i2][:, 0:C], c.gnorm[:, 0:1], zs[a][:, cs], ALU.mult, ALU.mult)
                if getattr(c, "stage", 99) <= 5:
                    c.dbg = [("osb", osb[i2][:, 0:C]), ("S0", st.S[:, hh, :])]
                    return
        if samp:
            for a in range(2):
                hh = 2 * kh + a
                k.dma(st.ssm_out[:, hh, :, :].re("s p v -> p s v"), Sall[a][:, :, :], store=True)
    ws2 = ws
    def cons(oc, p):
        resid_add(k, c, xT, oc, p, T, mods[:, 32 + oc, mcol], nseq, tps)
    k.ps_rot = list(range(8))
    linear_fm(k, ws2, L.gdn_w_out, 0, 32, 0, D, 256, lambda kc: ogT[:, kc, :], T, cons)
    k.release(m)


class GC:
    pass


def make_consts():
    cst = {}
    cst["ident"] = np.eye(128, dtype=np.float32)
    cst["ones"] = np.ones((128, 128), np.float32)
    i = np.arange(128)
    cst["U_p"] = (i[:, None] <= i[None, :]).astype(np.float32)
    cst["OB_p"] = np.ones((128, 128), np.float32)
    cst["Mn_p"] = np.where(i[None, :] <= i[:, None], 0.0, NEG).astype(np.float32)
    cst["SL_p"] = (i[None, :] < i[:, None]).astype(np.float32)
    blk = i // 4
    same = blk[:, None] == blk[None, :]
    cst["U_s"] = (same & (i[:, None] <= i[None, :])).astype(np.float32)
    cst["OB_s"] = same.astype(np.float32)
    cst["Mn_s"] = np.where(same & (i[None, :] <= i[:, None]), 0.0, NEG).astype(np.float32)
    cst["SL_s"] = (same & (i[None, :] < i[:, None])).astype(np.float32)
    sm = np.zeros((128, 32), np.float32)
    sm[i, blk] = 1.0
    cst["seqmask"] = sm
    return cst


def setup_common(k, c, I):
    c.ident = load_const(k, I["ident"], [128, 128])
    c.ones = load_const(k, I["ones"], [128, 128])
    c.gs = k.sb([128, 16, 17], F32, name="gs")
    c.rtmp = [k.sb([128, 64], F32, name="rtmp") for _ in range(2)]
    c.rti = 0
    def colvec(dv, n, name):
        t = k.sb([128, n], F32, name=name)
        m = k.mark()
        row = k.sb([128, 128], F32, name="row")
        k.dma(row[0:n, :], dv.re("(c p) -> c p", p=128))
        p = k.ps()
        k.tr(p[:, 0:n], row[0:n, :], c.ident[0:n, 0:n])
        k.copy(t[:], p[:, 0:n])
        k.release(m)
        return t
    c.colvec = colvec
    c.scT = k.sb([128, NKC, 17], BF16, name="scT")
    m = k.mark()
    cc = k.sb([128, D], F32, name="cc")
    k.dma(cc[0:17, :], I["cc"][:, :])
    k.act(cc[0:17, :], cc[0:17, :], AF.Silu)
    p = k.ps()
    for kc in range(NKC):
        k.tr(p[:, kc * 17:(kc + 1) * 17], cc[0:17, kc * 128:(kc + 1) * 128], c.ident[0:17, 0:17])
    k.copy(c.scT[:, :, :], p[:, 0:NKC * 17].re("p (k t) -> p k t", t=17))
    k.release(m)


def setup_gdn(k, c, I):
    c.gmix = c.colvec(I["g_mix"][0:2 * D], 32, "gmix")
    c.gffn = c.colvec(I["g_ffn"][0:2 * D], 32, "gffn")
    c.gnorm = k.sb([128, 1], F32, name="gnorm")
    k.dma(c.gnorm[:, :], I["gdn_g_norm"][0:128].re("(p o) -> p o", o=1))
    c.dtb = k.sb([128, 32], F32, name="dtb")
    k.dma(c.dtb[:, :], I["gdn_dt_bias"][0:32].re("(o n) -> o n", o=1).bc([128, 32]))
    c.nega = k.sb([128, 32], F32, name="nega")
    k.dma(c.nega[:, :], I["gdn_a_log"][0:32].re("(o n) -> o n", o=1).bc([128, 32]))
    k.act(c.nega[:, :], c.nega[:, :], AF.Exp)
    k.ts(c.nega[:, :], c.nega[:, :], -1.0, ALU.mult)
    c.wconv = k.sb([128, 64, 4], F32, name="wconv")
    m = k.mark()
    wr = k.sb([128, 8192], F32, name="wr")
    k.dma(wr[0:4, :], I["gdn_w_conv"][:, :])
    for g in range(2):
        p = k.ps()
        for j in range(32):
            cid = g * 32 + j
            k.tr(p[:, j * 4:(j + 1) * 4], wr[0:4, cid * 128:(cid + 1) * 128], c.ident[0:4, 0:4])
        k.copy(c.wconv[:, g * 32:(g + 1) * 32, :], p[:, 0:128].re("p (c j) -> p c j", j=4))
    k.release(m)


def load_G(k, I, sfx):
    G = GC()
    G.U = load_const(k, I["U_" + sfx], [128, 128])
    G.OB = load_const(k, I["OB_" + sfx], [128, 128])
    G.Mn = load_const(k, I["Mn_" + sfx], [128, 128])
    G.SL = load_const(k, I["SL_" + sfx], [128, 128])
    if sfx == "s":
        G.seqmask = load_const(k, I["seqmask"], [128, 32])
    return G


def rope_tables():
    half = 32
    inv = np.power(np.float32(10000.0), -(np.arange(half, dtype=np.float32) / np.float32(half))).astype(np.float32)
    pos = np.concatenate([np.arange(2048), 8192 + np.arange(4)]).astype(np.float32)
    ang = (pos[None, :] * inv[:, None]).astype(np.float32)
    cs = np.cos(ang.astype(np.float64)).astype(np.float32)
    sn = np.sin(ang.astype(np.float64)).astype(np.float32)
    cos2 = np.concatenate([cs, cs], 0)
    sin_s = np.concatenate([-sn, sn], 0)
    cos2 = np.concatenate([cos2, np.tile(cos2[:, 2048:2052], (1, 16))], 1)
    sin_s = np.concatenate([sin_s, np.tile(sin_s[:, 2048:2052], (1, 16))], 1)
    return np.ascontiguousarray(cos2), np.ascontiguousarray(sin_s)


def load_rope(k, c, I, pos0, T):
    c.cos2 = k.sb([64, T], F32, name="cos2")
    c.sin_s = k.sb([64, T], F32, name="sin_s")
    k.dma(c.cos2[:, :], I["rope_cos"][:, pos0:pos0 + T])
    k.dma(c.sin_s[:, :], I["rope_sin"][:, pos0:pos0 + T])


def rope_apply(k, c, out, p1, p2, T, tmp):
    k.tt(tmp[0][0:64, 0:T], p1, c.cos2[:, 0:T], ALU.mult)
    k.tt(tmp[1][0:64, 0:T], p2, c.sin_s[:, 0:T], ALU.mult)
    k.tt(out, tmp[0][0:64, 0:T], tmp[1][0:64, 0:T], ALU.add, e="pool")


def kv_phase(k, c, L, xT, T, mods, mcol, nseq, tps, KB, tok0, ckv_out, kpe_out, row0):
    m = k.mark()
    hT = k.sb([128, NKC, T], BF16, nsplit=NKC, name="hT")
    k.ts(c.gs[:, :, :], mods[:, 16:32, :], 1.0, ALU.add)
    k.tt(c.gs[:, :, :], c.gs[:, :, :], c.gkv[:, 0:16].un(2).bc([128, 16, 17]), ALU.mult)
    modnorm(k, c, xT, T, lambda kc: c.gs[:, kc, mcol], lambda kc: mods[:, kc, mcol], mcol, nseq, tps, hT)
    ws = WS(k, NKC * 640)
    w = ws.load(L.kv_w_down, 0, NKC, [(0, 576), (544, 32), (512, 32)])
    ckf = k.sb([128, 4, T], F32, nsplit=4, name="ckf")
    sq = k.sb([128, T], F32, name="ksq")
    pss = k.psb[0]
    k.ps_rot = [1, 2, 3, 4, 5, 6, 7]
    for rc in range(4):
        p = k.ps()
        for kc in range(NKC):
            k.mm(p[:, 0:T], w[:, kc, rc * 128:(rc + 1) * 128], hT[:, kc, :], start=(kc == 0), stop=(kc == NKC - 1))
        k.copy(ckf[:, rc, :], p[:, 0:T], e="act")
        k.tt(sq[:], ckf[:, rc, :], ckf[:, rc, :], ALU.mult)
        k.mm(pss[:, 0:T], c.ones[:], sq[:], start=(rc == 0), stop=(rc == 3))
    rstd = k.sb([128, T], F32, name="krstd")
    rstd_from_psum(k, rstd[:], pss[:, 0:T], KVL)
    k.ps_rot = list(range(8))
    for rc in range(4):
        k.stt(ckf[:, rc, :], ckf[:, rc, :], c.gkvn[:, rc:rc + 1], rstd[:], ALU.mult, ALU.mult)
        k.copy(KB.ckvT(rc), ckf[:, rc, :], e="pool")
    import os
    kvstop = int(os.environ.get("KVSTOP", "9"))
    if kvstop <= 1:
        k.release(m)
        return
    p1 = k.ps()
    p2 = k.ps()
    for kc in range(NKC):
        k.mm(p1[0:64, 0:T], w[:, kc, 512:576], hT[:, kc, :], start=(kc == 0), stop=(kc == NKC - 1))
    for kc in range(NKC):
        k.mm(p2[0:64, 0:T], w[:, kc, 576:640], hT[:, kc, :], start=(kc == 0), stop=(kc == NKC - 1))
    kpf = k.sb([64, T], F32, name="kpf")
    tmp = [k.sb([64, T], F32, name="rtmpa"), k.sb([64, T], F32, name="rtmpb")]
    rope_apply(k, c, kpf[:, :], p1[0:64, 0:T], p2[0:64, 0:T], T, tmp)
    k.copy(KB.kpeT(), kpf[:, :], e="pool")
    if kvstop <= 2:
        k.release(m)
        return
    ts_ = min(128, T)
    nsub = T // ts_
    stg = [k.sb([128, 576], F32, name="kvstg") for _ in range(2)]
    for s in range(nsub):
        st = stg[s % 2]
        p = k.ps()
        for rc in range(4):
            k.tr(p[0:ts_, rc * 128:(rc + 1) * 128], ckf[:, rc, s * ts_:(s + 1) * ts_], c.ident[:, :])
        k.copy(st[0:ts_, 0:512], p[0:ts_, 0:512], e="act")
        if KB.ckv_tm is not None and kvstop != 3:
            k.copy(KB.ckv_tm(s), st[0:ts_, 0:512], e="pool")
        if kvstop >= 4:
            pk = k.ps()
            k.tr(pk[0:ts_, 0:64], kpf[:, s * ts_:(s + 1) * ts_], c.ident[0:64, 0:64])
            k.copy(st[0:ts_, 512:576], pk[0:ts_, 0:64], e="act")
        k.dma(ckv_out[row0 + s * ts_:row0 + (s + 1) * ts_, :], st[0:ts_, 0:512], store=True)
        if kvstop >= 5:
            k.dma(kpe_out[row0 + s * ts_:row0 + (s + 1) * ts_, :], st[0:ts_, 512:576], store=True)
    k.release(m)


def qside_head(k, c, L, ws, h, cqT, T, Q):
    wq = ws.load(L.mla_w_uq, 0, 4, [(h * 192, 192), (h * 192 + 160, 32), (h * 192 + 128, 32)])
    k.dma(Q.wukf[:, :, :], L.kv_w_uk[:, h * 128:(h + 1) * 128].re("(rc p) d -> p rc d", p=128))
    p = k.ps()
    for rc in range(4):
        k.tr(p[:, rc * 128:(rc + 1) * 128], Q.wukf[:, rc, :], c.ident[:, :])
    k.copy(Q.wukT[:, :], p[:, 0:512], e="act")
    p = k.ps()
    for kc in range(4):
        k.mm(p[:, 0:T], wq[:, kc, 0:128], cqT[:, kc, :], start=(kc == 0), stop=(kc == 3))
    k.copy(Q.qn[:, 0:T], p[:, 0:T], e="act")
    p1 = k.ps()
    for kc in range(4):
        k.mm(p1[0:64, 0:T], wq[:, kc, 128:192], cqT[:, kc, :], start=(kc == 0), stop=(kc == 3))
    p2 = k.ps()
    for kc in range(4):
        k.mm(p2[0:64, 0:T], wq[:, kc, 192:256], cqT[:, kc, :], start=(kc == 0), stop=(kc == 3))
    rope_apply(k, c, Q.qpe_dst(h), p1[0:64, 0:T], p2[0:64, 0:T], T, Q.rtmp)
    for rc in range(4):
        p = k.ps()
        k.mm(p[:, 0:T], Q.wukT[:, rc * 128:(rc + 1) * 128], Q.qn[:, 0:T])
        k.copy(Q.qlat_dst(h, rc), p[:, 0:T], e="act" if rc % 2 else "dve")


def cq_compute(k, c, L, ws, hT, T, cqT):
    cqf = k.sb([128, 4, T], F32, nsplit=4, name="cqf")
    sq = k.sb([128, T], F32, name="cqsq")
    def cons(oc, p):
        k.copy(cqf[:, oc, :], p[:, 0:T], e="act")
    linear_fm(k, ws, L.mla_w_dq, 0, NKC, 0, QL, 512, lambda kc: hT[:, kc, :], T, cons)
    pss = k.ps()
    for rc in range(4):
        k.tt(sq[:], cqf[:, rc, :], cqf[:, rc, :], ALU.mult)
        k.mm(pss[:, 0:T], c.ones[:], sq[:], start=(rc == 0), stop=(rc == 3))
    rstd = k.sb([128, T], F32, name="cqrstd")
    rstd_from_psum(k, rstd[:], pss[:, 0:T], QL)
    for rc in range(4):
        k.stt(cqT[:, rc, :], cqf[:, rc, :], c.gq[:, rc:rc + 1], rstd[:], ALU.mult, ALU.mult)


def attn_phase_prompt(k, c, L, xT, T, mods, KB, tile):
    mcol = slice(0, 1)
    m = k.mark()
    hT = k.sb([128, NKC, T], BF16, nsplit=NKC, name="hT")
    k.ts(c.gs[:, :, :], mods[:, 16:32, :], 1.0, ALU.add)
    k.tt(c.gs[:, :, :], c.gs[:, :, :], c.gmix[:, 16:32].un(2).bc([128, 16, 17]), ALU.mult)
    modnorm(k, c, xT, T, lambda kc: c.gs[:, kc, mcol], lambda kc: mods[:, kc, mcol], mcol, 1, T, hT)
    ws = WS(k, NKC * 512)
    cqT = k.sb([128, 4, T], BF16, nsplit=4, name="cqT")
    m2 = k.mark()
    cq_compute(k, c, L, ws, hT, T, cqT)
    k.release(m2)
    aoT = k.sb([128, NH, T], BF16, nsplit=NH, name="aoT")
    Q = Ctx()
    Q.wukf = k.sb([128, 4, 128], F32, name="wukf")
    Q.wukT = k.sb([128, 512], BF16, name="wukT")
    Q.qn = k.sb([128, T], BF16, name="qn")
    Q.rtmp = [k.sb([64, T], F32, name="qrt0"), k.sb([64, T], F32, name="qrt1")]
    qlat = k.sb([128, 4, T], BF16, nsplit=4, name="qlat")
    qpe = k.sb([64, T], BF16, name="qpe")
    Q.qpe_dst = lambda h: qpe[:, 0:T]
    Q.qlat_dst = lambda h, rc: qlat[:, rc, :]
    pT = [k.sb([128, T], BF16, name="pT") for _ in range(2)]
    linv = k.sb([128, T], F32, name="linv")
    olat = k.sb([128, 4, T], BF16, nsplit=4, name="olat")
    wuv_t = [k.sb([128, 4, 128], BF16, name="wuv") for _ in range(2)]
    nkb = 4 * tile + 4
    for h in range(NH):
        k.ps_rot = [6, 7]
        qside_head(k, c, L, ws, h, cqT, T, Q)
        wuv = wuv_t[h % 2]
        k.dma(wuv[:, :, :], L.kv_w_uv[:, h * 128:(h + 1) * 128].re("(rc p) v -> p rc v", p=128), q="pool")
        k.ps_rot = [7]
        pl = k.psb[6]
        po = [k.psb[2 + rc] for rc in range(4)]

        def scores(kb):
            ps_ = k.psb[kb % 2]
            q0 = max(0, kb - 4 * tile) * 128
            ks = slice(kb * 128, (kb + 1) * 128)
            for rc in range(4):
                k.mm(ps_[:, q0:T], KB.ckvT_bf[:, rc, ks], qlat[:, rc, q0:T], start=(rc == 0), stop=False)
            k.mm(ps_[:, q0:T], KB.kpeT_bf[0:64, ks], qpe[0:64, q0:T], start=False, stop=True)
            return ps_, q0

        nxt = scores(0)
        for kb in range(nkb):
            ps_, q0 = nxt
            if kb + 1 < nkb:
                nxt = scores(kb + 1)
            pt = pT[kb % 2]
            k.act(pt[:, q0:T], ps_[:, q0:T], AF.Exp, scale=SM_SCALE)
            if kb >= 4 * tile:
                k.tt(pt[:, q0:q0 + 128], pt[:, q0:q0 + 128], c.tri[:, :], ALU.mult, e="pool")
            last = (kb == nkb - 1)
            for rc in range(4):
                k.mm(po[rc][:, q0:T], KB.ckv_tm_bf[:, kb, rc * 128:(rc + 1) * 128], pt[:, q0:T], start=(kb == 0), stop=last)
            k.mm(pl[:, q0:T], c.ones_bf[:, :], pt[:, q0:T], start=(kb == 0), stop=last)
        k.recip(linv[:, :], pl[:, 0:T])
        for rc in range(4):
            k.copy(olat[:, rc, :], po[rc][:, 0:T], e="act" if rc % 2 else "dve")
        pv = k.ps()
        for rc in range(4):
            k.mm(pv[:, 0:T], wuv[:, rc, :], olat[:, rc, :], start=(rc == 0), stop=(rc == 3))
        k.tt(aoT[:, h, :], pv[:, 0:T], linv[:, :], ALU.mult)
    k.ps_rot = list(range(8))
    def cons(oc, p):
        resid_add(k, c, xT, oc, p, T, mods[:, 32 + oc, mcol], 1, T)
    linear_fm(k, ws, L.mla_w_o, 0, NH, 0, D, 512, lambda kc: aoT[:, kc, :], T, cons)
    k.release(m)


def final_phase(k, c, xT, T, mods, mcol, nseq, tps, y_out, row0):
    m = k.mark()
    yT = k.sb([128, NKC, T], F32, nsplit=NKC, name="yT")
    k.ts(c.gs[:, :, :], mods[:, 16:32, :], 1.0, ALU.add)
    k.tt(c.gs[:, :, :], c.gs[:, :, :], c.gfin[:, 0:16].un(2).bc([128, 16, 17]), ALU.mult)
    modnorm(k, c, xT, T, lambda kc: c.gs[:, kc, mcol], lambda kc: mods[:, kc, mcol], mcol, nseq, tps, yT)
    transpose_out(k, c, lambda kc: yT[:, kc, :], NKC, T, y_out, row0)
    k.release(m)


INPUT_SPECS = {
    "cc": ([17, 2048], F32), "xp": ([2048, 2048], F32), "xs": ([64, 2048], F32),
    "w_ada0": ([2048, 12288], F32), "b_ada0": ([12288], F32), "w_ada1": ([2048, 12288], F32), "b_ada1": ([12288], F32),
    "g_mix": ([4096], F32), "g_ffn": ([4096], F32),
    "w_gate_up0": ([2048, 11264], F32), "w_gate_up1": ([2048, 11264], F32),
    "w_down0": ([5632, 2048], F32), "w_down1": ([5632, 2048], F32),
    "gdn_w_in": ([2048, 12352], F32), "gdn_w_conv": ([4, 8192], F32), "gdn_a_log": ([32], F32), "gdn_dt_bias": ([32], F32),
    "gdn_g_norm": ([128], F32), "gdn_w_out": ([4096, 2048], F32),
    "kv_w_ada": ([2048, 4096], F32), "kv_b_ada": ([4096], F32), "kv_g_in": ([2048], F32), "kv_w_down": ([2048, 576], F32),
    "kv_g_norm": ([512], F32), "kv_w_uk": ([512, 2048], F32), "kv_w_uv": ([512, 2048], F32),
    "mla_w_dq": ([2048, 512], F32), "mla_g_q": ([512], F32), "mla_w_uq": ([512, 3072], F32), "mla_w_o": ([2048, 2048], F32),
    "final_w_ada": ([2048, 4096], F32), "final_b_ada": ([4096], F32), "final_g": ([2048], F32),
    "rope_cos": ([64, 2116], F32), "rope_sin": ([64, 2116], F32), "tri": ([128, 128], F32),
    "cache_ckv": ([10240 * 128, 512], F32), "cache_kpe": ([10240 * 128, 64], F32), "page_table": ([1, 1024], I32),
    "state_ssm": ([16, 32, 128, 128], F32), "state_conv": ([48, 8192], F32),
    "iota_p": ([128, 1], F32), "cm": ([128, 64], F32),
}
OUTPUT_SPECS = {
    "yp": [2048, 2048], "ys": [64, 2048], "ssm_p": [32, 128, 128], "conv_p": [3, 8192], "ckv_p": [2048, 512], "kpe_p": [2048, 64],
    "ssm_s": [16, 32, 128, 128], "conv_s": [48, 8192], "ckv_s": [64, 512], "kpe_s": [64, 64],
}


def build_all(k, do_prompt=True, do_sample=True, ntiles=4, dbg=None, stop=None, dbg_s=False):
    c = Ctx()
    c.dbg_s = dbg_s
    L = Ctx()
    I = {}
    cst = make_consts()
    for n, v in cst.items():
        I[n] = k.din(n, list(v.shape))

    class LazyIn(dict):
        def __missing__(self, key):
            shp, dt = INPUT_SPECS[key]
            t = k.din(key, shp, dt)
            self[key] = t
            return t
    II = LazyIn(I)
    O = {}

    def out(name):
        if name not in O:
            O[name] = k.dout(name, OUTPUT_SPECS[name])
        return O[name]
    L.gdn_w_in = II["gdn_w_in"]
    L.gdn_w_out = II["gdn_w_out"]
    L.w_gate_up = [II["w_gate_up0"], II["w_gate_up1"]]
    L.w_down = [II["w_down0"], II["w_down1"]]
    L.kv_w_down = II["kv_w_down"]
    L.kv_w_uk = II["kv_w_uk"]
    L.kv_w_uv = II["kv_w_uv"]
    L.mla_w_dq = II["mla_w_dq"]
    L.mla_w_uq = II["mla_w_uq"]
    L.mla_w_o = II["mla_w_o"]
    setup_common(k, c, II)
    setup_gdn(k, c, II)
    c.gkv = c.colvec(II["kv_g_in"][0:D], 16, "gkv")
    c.gkvn = c.colvec(II["kv_g_norm"][0:512], 4, "gkvn")
    c.gq = c.colvec(II["mla_g_q"][0:512], 4, "gq")
    c.gfin = c.colvec(II["final_g"][0:D], 16, "gfin")
    c.tri = load_const(k, II["tri"], [128, 128], BF16, q="pool")
    c.ones_bf = k.sb([128, 128], BF16, name="ones_bf")
    k.copy(c.ones_bf[:, :], c.ones[:, :])
    T = 512
    x1_d = [k.dscratch(f"x1_{t}", [128, NKC * T]) for t in range(4)]
    KT_d = k.dscratch("KT_d", [128, 4 * 2048], BF16)
    KP_d = k.dscratch("KP_d", [64, 2048], BF16)
    KV_d = k.dscratch("KV_d", [128, 16 * 512], BF16)
    xs1_d = k.dscratch("xs1_d", [128, NKC * 64])
    c.ckvT_new = k.sb([128, 4, 64], BF16, nsplit=4, name="ckvT_new")
    c.kpeT_new = k.sb([64, 64], BF16, name="kpeT_new")
    c.ckv_tm_new = k.sb([128, 512], BF16, name="ckv_tm_new")
    m0 = k.mark()
    mods0 = ada_phase(k, c, II["w_ada0"], II["b_ada0"], 12288, "mods0")
    modskv = ada_phase(k, c, II["kv_w_ada"], II["kv_b_ada"], 4096, "modskv")
    if do_prompt:
        mp = k.mark()
        G = load_G(k, II, "p")
        st = Ctx()
        st.S = k.sb([128, 32, 128], F32, nsplit=32, name="S")
        st.convtail = k.sb([128, 64, 1, 3], F32, nsplit=64, name="ctail")
        k.memset(st.S[:, :, :], 0.0)
        k.memset(st.convtail[:, :, :, :], 0.0)
        xT = k.sb([128, NKC, T], F32, nsplit=NKC, name="xT")
        mcol = slice(0, 1)
        for t in range(ntiles):
            transpose_in(k, c, II["xp"], t * T, T, xT)
            gdn_phase(k, c, L, xT, T, 128, mods0, mcol, 1, T, G, st, False)
            ffn_phase(k, c, L, xT, T, mods0, mcol, 1, T, 0)
            k.dma(x1_d[t][:, :], xT[:, :, :].re("p k t -> p (k t)"), store=True)
            mt = k.mark()
            load_rope(k, c, II, t * T, T)
            KBw = Ctx()
            ckvT_st = k.sb([128, 4, T], BF16, nsplit=4, name="ckvT_st")
            kpeT_st = k.sb([64, T], BF16, name="kpeT_st")
            ckvtm_st = k.sb([128, 4, 512], BF16, nsplit=4, name="ckvtm_st")
            KBw.ckvT = lambda rc: ckvT_st[:, rc, :]
            KBw.kpeT = lambda: kpeT_st[:, :]
            KBw.ckv_tm = lambda s: ckvtm_st[:, s, :]
            if stop != "nokv":
                kv_phase(k, c, L, xT, T, modskv, mcol, 1, T, KBw, t * T, out("ckv_p"), out("kpe_p"), t * T)
            for rc in range(4):
                k.dma(KT_d[:, rc * 2048 + t * T: rc * 2048 + (t + 1) * T], ckvT_st[:, rc, :], store=True)
            k.dma(KP_d[:, t * T:(t + 1) * T], kpeT_st[:, :], store=True)
            k.dma(KV_d[:, t * 4 * 512:(t + 1) * 4 * 512], ckvtm_st[:, :, :].re("p s r -> p (s r)"), store=True)
            k.release(mt)
        k.dma(out("ssm_p")[:, :, :].re("h p v -> p h v"), st.S[:, :, :], store=True)
        transpose_out(k, c, lambda cid: st.convtail[:, cid, 0, :], 64, 3, out("conv_p"), 0)
        k.release(mp)
    if do_sample:
        sample_layer0(k, c, L, II, out, mods0, modskv, xs1_d)
    k.release(m0)
    if stop == "l0":
        k.finish()
        return II, O, cst
    mods1 = ada_phase(k, c, II["w_ada1"], II["b_ada1"], 12288, "mods1")
    modsf = ada_phase(k, c, II["final_w_ada"], II["final_b_ada"], 4096, "modsf")
    if do_prompt:
        mp = k.mark()
        KB = Ctx()
        KB.ckvT_bf = k.sb([128, 4, 2048], BF16, nsplit=4, name="ckvT_bf")
        KB.kpeT_bf = k.sb([64, 2048], BF16, name="kpeT_bf")
        KB.ckv_tm_bf = k.sb([128, 16, 512], BF16, nsplit=16, name="ckv_tm_bf")
        k.dma(KB.ckvT_bf[:, :, :].re("p r t -> p (r t)"), KT_d[:, :])
        k.dma(KB.kpeT_bf[:, :], KP_d[:, :])
        k.dma(KB.ckv_tm_bf[:, :, :].re("p s r -> p (s r)"), KV_d[:, :])
        xT = k.sb([128, NKC, T], F32, nsplit=NKC, name="xT")
        for t in range(ntiles):
            k.dma(xT[:, :, :].re("p k t -> p (k t)"), x1_d[t][:, :])
            mt = k.mark()
            load_rope(k, c, II, t * T, T)
            attn_phase_prompt(k, c, L, xT, T, mods1, KB, t)
            k.release(mt)
            if dbg is not None and "xmid1" in dbg:
                transpose_out(k, c, lambda kc: xT[:, kc, :], NKC, T, dbg["xmid1"], t * T)
            ffn_phase(k, c, L, xT, T, mods1, slice(0, 1), 1, T, 1)
            final_phase(k, c, xT, T, modsf, slice(0, 1), 1, T, out("yp"), t * T)
        k.release(mp)
    if do_sample:
        sample_layer1(k, c, L, II, out, mods1, modsf, xs1_d)
    k.finish()
    return II, O, cst


def sample_layer0(k, c, L, II, out, mods0, modskv, xs1_d):
    T, nseq, tps = 64, 16, 4
    mcol = slice(1, 17)
    m = k.mark()
    G = load_G(k, II, "s")
    c.Gs = G
    st = Ctx()
    st.convtail = k.sb([128, 64, nseq, 3], F32, nsplit=64, name="ctail_s")
    st.ssm_in = II["state_ssm"]
    st.ssm_out = out("ssm_s")
    m2 = k.mark()
    stg = [k.sb([128, 2048], F32, name="cstg") for _ in range(2)]
    for pc in range(4):
        sg = stg[pc % 2]
        k.dma(sg[0:48, :], II["state_conv"][:, pc * 2048:(pc + 1) * 2048])
        for g in range(2):
            p = k.ps()
            for j in range(8):
                cl = g * 8 + j
                k.tr(p[:, j * 48:(j + 1) * 48], sg[0:48, cl * 128:(cl + 1) * 128], c.ident[0:48, 0:48])
            cid0 = pc * 16 + g * 8
            k.copy(st.convtail[:, cid0:cid0 + 8, :, :].re("p c s j -> p c (s j)"),
                   p[:, 0:8 * 48].re("p (c x) -> p c x", x=48), e="act" if g else "dve")
    k.release(m2)
    xT = k.sb([128, NKC, T], F32, nsplit=NKC, name="xTs")
    transpose_in(k, c, II["xs"], 0, T, xT)
    gdn_phase(k, c, L, xT, T, 64, mods0, mcol, nseq, tps, G, st, True)
    if getattr(c, "dbg_s", None):
        transpose_out(k, c, lambda kc: xT[:, kc, :], NKC, T, k.dout("d_xmid0", [64, 2048]), 0)
    for pc in range(4):
        transpose_out(k, c, lambda cid: st.convtail[:, pc * 16 + cid, :, :].re("p s j -> p (s j)"), 16, 48,
                      out("conv_s"), 0, col0=pc * 2048)
    ffn_phase(k, c, L, xT, T, mods0, mcol, nseq, tps, 0)
    if getattr(c, "dbg_s", None):
        transpose_out(k, c, lambda kc: xT[:, kc, :], NKC, T, k.dout("d_x0", [64, 2048]), 0)
    k.dma(xs1_d[:, :], xT[:, :, :].re("p k t -> p (k t)"), store=True)
    load_rope(k, c, II, 2052, T)
    KBw = Ctx()
    KBw.ckvT = lambda rc: c.ckvT_new[:, rc, :]
    KBw.kpeT = lambda: c.kpeT_new[:, :]
    KBw.ckv_tm = lambda s: c.ckv_tm_new[0:64, :]
    kv_phase(k, c, L, xT, T, modskv, mcol, nseq, tps, KBw, 0, out("ckv_s"), out("kpe_s"), 0)
    k.release(m)


def sample_layer1(k, c, L, II, out, mods1, modsf, xs1_d):
    T, nseq, tps = 64, 16, 4
    mcol = slice(1, 17)
    m = k.mark()
    xT = k.sb([128, NKC, T], F32, nsplit=NKC, name="xTs")
    k.dma(xT[:, :, :].re("p k t -> p (k t)"), xs1_d[:, :])
    load_rope(k, c, II, 2052, T)
    hT = k.sb([128, NKC, T], BF16, nsplit=NKC, name="hTs")
    k.ts(c.gs[:, :, :], mods1[:, 16:32, :], 1.0, ALU.add)
    k.tt(c.gs[:, :, :], c.gs[:, :, :], c.gmix[:, 16:32].un(2).bc([128, 16, 17]), ALU.mult)
    modnorm(k, c, xT, T, lambda kc: c.gs[:, kc, mcol], lambda kc: mods1[:, kc, mcol], mcol, nseq, tps, hT)
    ws = WS(k, NKC * 512)
    cqT = k.sb([128, 4, T], BF16, nsplit=4, name="cqTs")
    m2 = k.mark()
    cq_compute(k, c, L, ws, hT, T, cqT)
    k.release(m2)
    QLs = k.sb([128, 4, NH, T], BF16, name="QLs")
    QPs = k.sb([64, NH, T], BF16, name="QPs")
    OLs = k.sb([128, 4, NH, T], BF16, name="OLs")
    aoT = k.sb([128, NH, T], BF16, nsplit=NH, name="aoTs")
    Q = Ctx()
    Q.wukf = k.sb([128, 4, 128], F32, name="wukf")
    Q.wukT = k.sb([128, 512], BF16, name="wukT")
    Q.qn = k.sb([128, T], BF16, name="qn")
    Q.rtmp = [k.sb([64, T], F32, name="qrt0"), k.sb([64, T], F32, name="qrt1")]
    Q.qpe_dst = lambda h: QPs[:, h, :]
    Q.qlat_dst = lambda h, rc: QLs[:, rc, h, :]
    for h in range(NH):
        qside_head(k, c, L, ws, h, cqT, T, Q)
    ptb = k.sb([128, 1024], I32, name="ptb")
    k.dma(ptb[:, :], II["page_table"][0:1, :].bc([128, 1024]))
    idxf = k.sb([128, 1024], F32, name="idxf")
    k.copy(idxf[:, :], ptb[:, :])
    iota = load_const(k, II["iota_p"], [128, 1])
    k.ts(idxf[:, :], idxf[:, :], 128.0, ALU.mult, iota[:, 0:1], ALU.add)
    idx = k.sb([128, 1024], I32, name="idx")
    k.copy(idx[:, :], idxf[:, :])
    cm = load_const(k, II["cm"], [128, 64])
    ident_bf = k.sb([128, 128], BF16, name="ident_bf")
    k.copy(ident_bf[:, :], c.ident[:, :])
    Kp = [k.sb([128, 576], BF16, name="Kp") for _ in range(4)]
    KT = [k.sb([128, 5, 128], BF16, name="KT") for _ in range(2)]
    pts = [k.sb([128, 64], BF16, name="pts") for _ in range(2)]
    pe_ = k.sb([64, 64], F32, name="pe_")
    ptm = k.sb([64, 64], BF16, name="ptm")
    linv = k.sb([128, 64], F32, name="linvs")
    ptr_bank = k.psb[7]
    ptr_bf = ptr_bank[:, :].bitcast(BF16)
    po = [k.psb[2 + rc] for rc in range(4)]
    pl = k.psb[6]
    import os
    NPG = int(os.environ.get("NPG", "64"))
    n = 0
    for s in range(nseq):
        qs = slice(s * tps, (s + 1) * tps)
        def qlat_s(rc):
            return QLs[:, rc, :, qs]
        qpe_s = QPs[:, :, qs]
        for pg in range(NPG):
            col = s * 64 + pg
            kp = Kp[n % 4]
            kt = KT[n % 2]
            k.idma(kp[:, 0:512], II["cache_ckv"][:, :], idx[:, col:col + 1])
            k.idma(kp[:, 512:576], II["cache_kpe"][:, :], idx[:, col:col + 1])
            for rc in range(4):
                k.tr(ptr_bf[:, rc * 128:(rc + 1) * 128], kp[:, rc * 128:(rc + 1) * 128], ident_bf[:, :])
            k.tr(ptr_bf[0:64, 512:640], kp[:, 512:576], ident_bf[:, :])
            k.copy(kt[:, 0:4, :], ptr_bf[:, 0:512].re("p (c t) -> p c t", t=128), e="dve")
            k.copy(kt[0:64, 4, :], ptr_bf[0:64, 512:640], e="dve")
            ps_ = k.psb[n % 2]
            for rc in range(4):
                k.mm(ps_[:, 0:64], kt[:, rc, :], qlat_s(rc), start=(rc == 0), stop=False)
            k.mm(ps_[:, 0:64], kt[0:64, 4, :], qpe_s, start=False, stop=True)
            pt = pts[n % 2]
            k.act(pt[:, :], ps_[:, 0:64], AF.Exp, scale=SM_SCALE)
            for rc in range(4):
                k.mm(po[rc][:, 0:64], kp[:, rc * 128:(rc + 1) * 128], pt[:, :], start=(pg == 0), stop=False)
            k.mm(pl[:, 0:64], c.ones_bf[:, :], pt[:, :], start=(pg == 0), stop=False)
            n += 1
        ps_ = k.psb[n % 2]
        n += 1
        for rc in range(4):
            k.mm(ps_[0:64, 0:64], c.ckvT_new[:, rc, :], qlat_s(rc), start=(rc == 0), stop=False)
        k.mm(ps_[0:64, 0:64], c.kpeT_new[0:64, :], qpe_s, start=False, stop=True)
        k.act(pe_[:, :], ps_[0:64, 0:64], AF.Exp, scale=SM_SCALE)
        k.stt(ptm[:, :], pe_[:, :], c.Gs.seqmask[0:64, s:s + 1], cm[0:64, :], ALU.mult, ALU.mult)
        for rc in range(4):
            k.mm(po[rc][:, 0:64], c.ckv_tm_new[0:64, rc * 128:(rc + 1) * 128], ptm[:, :], start=(NPG == 0), stop=True)
        k.mm(pl[:, 0:64], c.ones_bf[0:64, :], ptm[:, :], start=(NPG == 0), stop=True)
        k.recip(linv[:, :], pl[:, 0:64])
        for rc in range(4):
            k.tt(OLs[:, rc, :, qs], po[rc][:, 0:64].re("p (h i) -> p h i", i=tps), linv[:, :].re("p (h i) -> p h i", i=tps),
                 ALU.mult)
    k.ps_rot = [0, 1, 7]
    wuv_t = [k.sb([128, 4, 128], BF16, name="wuvs") for _ in range(2)]
    for h in range(NH):
        wuv = wuv_t[h % 2]
        k.dma(wuv[:, :, :], L.kv_w_uv[:, h * 128:(h + 1) * 128].re("(rc p) v -> p rc v", p=128), q="pool")
        pv = k.ps()
        for rc in range(4):
            k.mm(pv[:, 0:T], wuv[:, rc, :], OLs[:, rc, h, :], start=(rc == 0), stop=(rc == 3))
        k.copy(aoT[:, h, :], pv[:, 0:T], e="act")
    k.ps_rot = list(range(8))
    def cons(oc, p):
        resid_add(k, c, xT, oc, p, T, mods1[:, 32 + oc, mcol], nseq, tps)
    linear_fm(k, ws, L.mla_w_o, 0, NH, 0, D, 512, lambda kc: aoT[:, kc, :], T, cons)
    ffn_phase(k, c, L, xT, T, mods1, mcol, nseq, tps, 1)
    final_phase(k, c, xT, T, modsf, mcol, nseq, tps, out("ys"), 0)
    k.release(m)


def _host_inputs(z, core, cst, needed, shared):
    seq = core % 4
    ins = {n: v for n, v in cst.items() if n in needed}
    sl = slice(core * 16, (core + 1) * 16)
    i = np.arange(128)

    def put(n, fn, share=False):
        if n not in needed:
            return
        if share:
            if n not in shared:
                shared[n] = np.ascontiguousarray(fn())
            ins[n] = shared[n]
        else:
            ins[n] = np.ascontiguousarray(fn())

    def cc():
        a = np.zeros((17, 2048), np.float32)
        a[0] = z["c_prompt"][seq]
        a[1:] = z["c_sample"][sl]
        return a
    put("cc", cc)
    put("xp", lambda: z["x_prompt"][seq])
    put("xs", lambda: z["x_sample"][sl].reshape(64, 2048))
    for l in (0, 1):
        put(f"w_ada{l}", lambda: z["w_ada"][l], True)
        put(f"b_ada{l}", lambda: z["b_ada"][l], True)
        put(f"w_gate_up{l}", lambda: z["w_gate_up"][l], True)
        put(f"w_down{l}", lambda: z["w_down"][l], True)
    put("g_mix", lambda: z["g_mix"].reshape(-1), True)
    put("g_ffn", lambda: z["g_ffn"].reshape(-1), True)
    for n in ("gdn_w_in", "gdn_w_conv", "gdn_a_log", "gdn_dt_bias", "gdn_g_norm", "gdn_w_out", "mla_w_dq", "mla_g_q",
              "mla_w_uq", "mla_w_o"):
        put(n, lambda n=n: z[n][0], True)
    for n in ("kv_w_ada", "kv_b_ada", "kv_g_in", "kv_w_down", "kv_g_norm", "kv_w_uk", "kv_w_uv", "final_w_ada",
              "final_b_ada", "final_g"):
        put(n, lambda n=n: z[n], True)
    if "rope_cos" in needed or "rope_sin" in needed:
        if "rope" not in shared:
            shared["rope"] = rope_tables()
        ins["rope_cos"], ins["rope_sin"] = shared["rope"]
    put("tri", lambda: (i[None, :] >= i[:, None]).astype(np.float32), True)
    put("cache_ckv", lambda: z["cache_ckv"].reshape(-1, 512), True)
    put("cache_kpe", lambda: z["cache_kpe"].reshape(-1, 64), True)
    put("page_table", lambda: z["page_table"][sl].reshape(1, 1024).astype(np.int32))
    put("state_ssm", lambda: z["state_ssm"][0, sl])
    put("state_conv", lambda: z["state_conv"][0, sl].reshape(48, 8192))
    put("iota_p", lambda: i.astype(np.float32).reshape(128, 1), True)
    put("cm", lambda: ((i[:, None] % 4) <= (np.arange(64)[None, :] % 4)).astype(np.float32), True)
    return ins


_PROG = {}


def _get_prog():
    if "k" not in _PROG:
        k, (II, O, cst) = two_pass(lambda kk: build_all(kk))
        _PROG["k"] = k
        _PROG["cst"] = cst
    return _PROG["k"], _PROG["cst"]


def kernel(**inputs):
    from concourse.bass_utils import run_bass_kernel_spmd
    k, cst = _get_prog()
    z = {n: np.asarray(v) for n, v in inputs.items()}
    needed = set(k.dram_in)
    shared = {}
    in_maps = [_host_inputs(z, core, cst, needed, shared) for core in range(8)]
    res = run_bass_kernel_spmd(k.nc, in_maps, core_ids=list(range(8)))
    R = res.results
    f32 = np.float32
    y_prompt = np.stack([R[c]["yp"] for c in range(4)]).astype(f32)
    y_sample = np.concatenate([R[c]["ys"].reshape(16, 4, 2048) for c in range(8)]).astype(f32)
    ssm_prompt = np.stack([R[c]["ssm_p"] for c in range(4)])[None].astype(f32)
    conv_prompt = np.stack([R[c]["conv_p"] for c in range(4)])[None].astype(f32)
    ckv_prompt = np.stack([R[c]["ckv_p"] for c in range(4)]).astype(f32)
    kpe_prompt = np.stack([R[c]["kpe_p"] for c in range(4)]).astype(f32)
    ssm_sample = np.concatenate([R[c]["ssm_s"] for c in range(8)])[None].astype(f32)
    conv_sample = np.concatenate([R[c]["conv_s"].reshape(16, 3, 8192) for c in range(8)])[None].astype(f32)
    ckv_sample = np.concatenate([R[c]["ckv_s"].reshape(16, 4, 512) for c in range(8)]).astype(f32)
    kpe_sample = np.concatenate([R[c]["kpe_s"].reshape(16, 4, 64) for c in range(8)]).astype(f32)
    return (y_prompt, y_sample, ssm_prompt, conv_prompt, ckv_prompt, kpe_prompt,
            ssm_sample, conv_sample, ckv_sample, kpe_sample)
```

```python
import os
import numpy as np
import concourse.bass as bass
import concourse.mybir as mybir

F32 = mybir.dt.float32
BF16 = mybir.dt.bfloat16
I32 = mybir.dt.int32
U32 = mybir.dt.uint32
AF = mybir.ActivationFunctionType
ALU = mybir.AluOpType
AX = mybir.AxisListType
DTSZ = {F32: 4, BF16: 2, I32: 4, U32: 4}


class Buf:
    __slots__ = ("w", "r", "name")

    def __init__(self, name=""):
        self.w = {}
        self.r = {}
        self.name = name


class V:
    __slots__ = ("ap", "bufs")

    def __init__(self, ap, bufs):
        self.ap = ap
        self.bufs = bufs

    def __getitem__(self, key):
        return V(self.ap[key], self.bufs)

    def bc(self, shape):
        return V(self.ap.to_broadcast(list(shape)), self.bufs)

    def un(self, axis):
        return V(self.ap.unsqueeze(axis), self.bufs)

    def re(self, s, **kw):
        return V(self.ap.rearrange(s, **kw), self.bufs)

    def bitcast(self, dt):
        return V(self.ap.bitcast(dt), self.bufs)

    @property
    def shape(self):
        return tuple(self.ap.shape)


class Tens:
    def __init__(self, handle, shape, nsplit=1, name=""):
        self.h = handle
        self.shape = tuple(shape)
        self.nsplit = nsplit
        self.bufs = [Buf(f"{name}.{i}") for i in range(nsplit)]
        self.name = name

    def all(self):
        return V(self.h.ap() if hasattr(self.h, "ap") and callable(getattr(self.h, "ap")) else self.h[:], list(self.bufs))

    def __getitem__(self, key):
        ap = self.h[key]
        if self.nsplit == 1:
            return V(ap, self.bufs)
        k1 = key[1] if isinstance(key, tuple) and len(key) > 1 else slice(None)
        if isinstance(k1, int):
            per = self.shape[1] // self.nsplit
            return V(ap, [self.bufs[k1 // per]])
        if isinstance(k1, slice):
            per = self.shape[1] // self.nsplit
            a = 0 if k1.start is None else k1.start
            b = self.shape[1] if k1.stop is None else k1.stop
            return V(ap, self.bufs[a // per:(b - 1) // per + 1])
        return V(ap, list(self.bufs))


class K:
    def __init__(self, needed=None, same_engine_sync=True):
        import bisect
        self._bisect = bisect
        self.needed = needed
        self.used = {e: set() for e in ("pe", "dve", "act", "pool")}
        self.runid = {e: 0 for e in ("pe", "dve", "act", "pool", "sp")}
        self.run_of = {e: [] for e in ("pe", "dve", "act", "pool")}
        self.nc = bass.Bass("TRN2", target_bir_lowering=False)
        nc = self.nc
        self.E = {"pe": nc.tensor, "dve": nc.vector, "act": nc.scalar, "pool": nc.gpsimd, "sp": nc.sync}
        self.esem = {e: nc.alloc_semaphore(f"es_{e}") for e in ("pe", "dve", "act", "pool")}
        self.ecnt = {e: 0 for e in self.esem}
        self.seen = {e: {} for e in self.E}
        self.same = same_engine_sync
        self.dpool = {}
        for q, n in (("sp", 24), ("pool", 24), ("act", 8)):
            self.dpool[q] = [[nc.alloc_semaphore(f"ds_{q}{i}"), 0] for i in range(n)]
        self.dnext = {q: 0 for q in self.dpool}
        self.stores = []
        self.semname = {}
        self.ninst = 0
        self.nincs = 0
        self.log = []
        self.sem2eng = {id(s): e for e, s in self.esem.items()}
        self.needed_set = {e: set(v) for e, v in needed["reps"].items()} if needed is not None else None
        self.sb_off = 16512
        self.sb_cap = 229376
        self.sb_peak = 0
        self.uid = 0
        self.psb = []
        for i in range(8):
            h = nc.alloc_psum_tensor(f"psb{i}", [128, 512], F32)
            self.psb.append(Tens(h, [128, 512], 1, f"psb{i}"))
        self.psn = 0
        self.ps_rot = list(range(8))
        self.dram_in = {}
        self.dram_out = {}

    def sb(self, shape, dtype=F32, nsplit=1, name=None):
        self.uid += 1
        name = f"{name or 't'}_{self.uid}"
        per = int(np.prod(shape[1:])) * DTSZ[dtype]
        per = (per + 63) // 64 * 64
        off = self.sb_off
        if off + per > self.sb_cap:
            raise RuntimeError(f"SBUF arena overflow allocating {name} {shape}: off={off} per={per}")
        h = self.nc.alloc_sbuf_tensor_at(name, list(shape), dtype, offset=off)
        self.sb_off += per
        self.sb_peak = max(self.sb_peak, self.sb_off)
        return Tens(h, shape, nsplit, name)

    def mark(self):
        return self.sb_off

    def release(self, m):
        self.barrier()
        self.sb_off = m

    def ps(self):
        t = self.psb[self.ps_rot[self.psn % len(self.ps_rot)]]
        self.psn += 1
        return t

    def din(self, name, shape, dtype=F32):
        h = self.nc.dram_tensor(name, list(shape), dtype, kind="ExternalInput")
        self.dram_in[name] = (tuple(shape), dtype)
        return Tens(h, shape, 1, name)

    def dout(self, name, shape, dtype=F32):
        h = self.nc.dram_tensor(name, list(shape), dtype, kind="ExternalOutput")
        self.dram_out[name] = (tuple(shape), dtype)
        return Tens(h, shape, 1, name)

    def dscratch(self, name, shape, dtype=F32, nsplit=1):
        h = self.nc.dram_tensor(name, list(shape), dtype, kind="Internal")
        return Tens(h, shape, nsplit, name)

    def _deps(self, reads, writes):
        d = {}
        for v in reads:
            if isinstance(v, V):
                for b in v.bufs:
                    for s, val in b.w.items():
                        if d.get(s, 0) < val:
                            d[s] = val
        for v in writes:
            for b in v.bufs:
                for dd in (b.w, b.r):
                    for s, val in dd.items():
                        if d.get(s, 0) < val:
                            d[s] = val
        return d

    def _wait(self, e, deps, skip_own=False):
        eng = self.E[e]
        seen = self.seen[e]
        own = self.esem.get(e)
        for s, val in deps.items():
            if s is own and (skip_own or not self.same):
                continue
            if seen.get(s, 0) < val:
                sv = self._semval(s, val)
                eng.wait_ge(s, sv)
                seen[s] = val
                self.runid[e] += 1
                self.log.append((e, "wait", s, sv))

    def _semval(self, s, val):
        en = self.sem2eng.get(id(s))
        if en is None:
            return val
        self.used[en].add(val)
        if self.needed is None:
            return val
        rep = self.needed["rep"][en][val]
        lst = self.needed["reps"][en]
        i = self._bisect.bisect_left(lst, rep)
        assert i < len(lst) and lst[i] == rep, (en, val)
        return i + 1

    def _record(self, reads, writes, ev):
        s, val = ev
        for v in reads:
            if isinstance(v, V):
                for b in v.bufs:
                    if b.r.get(s, 0) < val:
                        b.r[s] = val
        for v in writes:
            for b in v.bufs:
                b.w = {s: val}
                b.r = {}

    def op(self, e, fn, reads, writes, pe_acc=False):
        deps = self._deps(reads, writes)
        if pe_acc:
            self._wait(e, deps, skip_own=True)
        else:
            self._wait(e, deps, skip_own=(e == "pe"))
        ins = fn(self.E[e])
        self.ecnt[e] += 1
        self.run_of[e].append(self.runid[e])
        if self.needed is None or self.ecnt[e] in self.needed_set[e]:
            ins.then_inc(self.esem[e], 1)
            self.log.append((e, "inc", self.esem[e], 1))
            self.nincs += 1
        ev = (self.esem[e], self.ecnt[e])
        self._record(reads, writes, ev)
        self.ninst += 1
        return ev

    def dma(self, out, in_, q="sp", store=False, **kw):
        deps = self._deps([in_], [out])
        pool = self.dpool[q]
        j = self.dnext[q] % len(pool)
        self.dnext[q] += 1
        sem, cnt = pool[j]
        if cnt > 0:
            deps[sem] = max(deps.get(sem, 0), cnt)
        self._wait(q, deps)
        ins = self.E[q].dma_start(out=out.ap, in_=in_.ap, **kw)
        ins.then_inc(sem, 16)
        self.log.append((q, "inc", sem, 16))
        pool[j][1] = cnt + 16
        ev = (sem, cnt + 16)
        self._record([in_], [out], ev)
        if store:
            self.stores.append(ev)
        self.ninst += 1
        return ev

    def idma(self, out, in_full, idx, q="pool"):
        deps = self._deps([idx], [out])
        pool = self.dpool[q]
        j = self.dnext[q] % len(pool)
        self.dnext[q] += 1
        sem, cnt = pool[j]
        if cnt > 0:
            deps[sem] = max(deps.get(sem, 0), cnt)
        self._wait(q, deps)
        ins = self.E[q].indirect_dma_start(out=out.ap, out_offset=None, in_=in_full.ap,
                                           in_offset=bass.IndirectOffsetOnAxis(ap=idx.ap, axis=0))
        ins.then_inc(sem, 16)
        self.log.append((q, "inc", sem, 16))
        pool[j][1] = cnt + 16
        ev = (sem, cnt + 16)
        self._record([idx], [out], ev)
        self.ninst += 1
        return ev

    def barrier(self):
        deps = {self.esem[e]: self.ecnt[e] for e in self.esem if self.ecnt[e] > 0}
        for s, val in self.stores:
            deps[s] = max(deps.get(s, 0), val)
        for q in self.dpool:
            for sem, cnt in self.dpool[q]:
                if cnt > 0:
                    deps[sem] = max(deps.get(sem, 0), cnt)
        self.stores = []
        for e in ("pe", "dve", "act", "pool", "sp"):
            eng = self.E[e]
            seen = self.seen[e]
            for s, val in deps.items():
                if seen.get(s, 0) < val:
                    sv = self._semval(s, val)
                    eng.wait_ge(s, sv)
                    seen[s] = val
                    self.runid[e] += 1
                    self.log.append((e, "wait", s, sv))

    def finish(self):
        self.barrier()

    def mm(self, out, lhsT, rhs, start=True, stop=True):
        return self.op("pe", lambda e: e.matmul(out.ap, lhsT.ap, rhs.ap, start=start, stop=stop),
                       [lhsT, rhs], [out], pe_acc=not start)

    def tr(self, out, in_, ident):
        return self.op("pe", lambda e: e.transpose(out.ap, in_.ap, ident.ap), [in_, ident], [out])

    def act(self, out, in_, func, bias=0.0, scale=1.0, accum=None, e="act"):
        reads = [in_] + [x for x in (bias, scale) if isinstance(x, V)]
        writes = [out] + ([accum] if accum is not None else [])
        b = bias.ap if isinstance(bias, V) else bias
        s = scale.ap if isinstance(scale, V) else scale
        kw = {}
        if accum is not None:
            kw["accum_out"] = accum.ap
        return self.op(e, lambda g: g.activation(out=out.ap, in_=in_.ap, func=func, bias=b, scale=s, **kw),
                       reads, writes)

    def tt(self, out, a, b, op, e="dve"):
        return self.op(e, lambda g: g.tensor_tensor(out.ap, a.ap, b.ap, op), [a, b], [out])

    def ts(self, out, a, s1, op0, s2=None, op1=None, e="dve", accum=None):
        reads = [a] + [x for x in (s1, s2) if isinstance(x, V)]
        x1 = s1.ap if isinstance(s1, V) else s1
        x2 = s2.ap if isinstance(s2, V) else s2
        kw = {}
        if op1 is not None:
            kw["op1"] = op1
        writes = [out]
        if accum is not None:
            kw["accum_out"] = accum.ap
            writes.append(accum)
        return self.op(e, lambda g: g.tensor_scalar(out.ap, a.ap, x1, x2, op0, **kw), reads, writes)

    def stt(self, out, in0, scalar, in1, op0, op1, e="dve"):
        reads = [in0, in1] + ([scalar] if isinstance(scalar, V) else [])
        sc = scalar.ap if isinstance(scalar, V) else scalar
        return self.op("dve", lambda g: g.scalar_tensor_tensor(out.ap, in0.ap, sc, in1.ap, op0, op1), reads, [out])

    def copy(self, out, in_, e="dve"):
        if e == "act":
            return self.op(e, lambda g: g.copy(out.ap, in_.ap), [in_], [out])
        return self.op(e, lambda g: g.tensor_copy(out.ap, in_.ap), [in_], [out])

    def memset(self, out, val, e="dve"):
        return self.op(e, lambda g: g.memset(out.ap, val), [], [out])

    def reduce(self, out, in_, op=ALU.add, axis=AX.X, e="dve"):
        return self.op(e, lambda g: g.tensor_reduce(out.ap, in_.ap, axis, op), [in_], [out])

    def recip(self, out, in_):
        return self.op("dve", lambda g: g.reciprocal(out.ap, in_.ap), [in_], [out])


def simulate_log(log):
    streams = {}
    for it in log:
        streams.setdefault(it[0], []).append(it)
    pos = {e: 0 for e in streams}
    sem = {}
    progress = True
    while progress:
        progress = False
        for e, st in streams.items():
            while pos[e] < len(st):
                _, kind, s, val = st[pos[e]]
                if kind == "wait":
                    if sem.get(id(s), 0) >= val:
                        pos[e] += 1
                        progress = True
                    else:
                        break
                else:
                    sem[id(s)] = sem.get(id(s), 0) + val
                    pos[e] += 1
                    progress = True
    stuck = {e: (pos[e], len(st)) for e, st in streams.items() if pos[e] < len(st)}
    return stuck


def two_pass(build_fn):
    k1 = K()
    build_fn(k1)
    rep, reps = {}, {}
    for e, v in k1.used.items():
        ro = k1.run_of[e]
        last = {}
        for idx in sorted(v):
            last[ro[idx - 1]] = idx
        rep[e] = {idx: last[ro[idx - 1]] for idx in v}
        reps[e] = sorted(set(rep[e].values()))
    k2 = K(needed={"rep": rep, "reps": reps})
    r = build_fn(k2)
    return k2, r

D = 2048
NKC = 16
DFF = 5632
EPS = 1e-6
NHV = 32
NHK = 16
CONV_DIM_ = 8192
GIN = 12352
QL = 512
KVL = 512
ROPE = 64
NH = 16
SM_SCALE = (128 + 64) ** -0.5
NEG = -30000.0


class Ctx:
    pass


def load_const(k, dram, shape, dtype=F32, q="sp"):
    t = k.sb(list(shape), dtype)
    k.dma(t[:], dram[:], q=q)
    return t


class WS:
    def __init__(self, k, nelem):
        self.k = k
        self.n = nelem
        self.b = [k.sb([128, nelem], BF16, name="wbuf") for _ in range(2)]
        self.i = 0

    def get(self, nk, ncols):
        t = self.b[self.i % 2]
        self.i += 1
        assert nk * ncols <= self.n, (nk, ncols, self.n)
        return t[:, 0:nk * ncols].re("p (k n) -> p k n", n=ncols)

    def load(self, wd, r0, nk, colspecs, rows=128):
        tot = sum(n for _, n in colspecs)
        v = self.get(nk, tot)
        o = 0
        for c0, n in colspecs:
            src = wd[r0:r0 + nk * rows, c0:c0 + n].re("(k p) n -> p k n", p=rows)
            self.k.dma(v[0:rows, :, o:o + n], src, q="pool")
            o += n
        return v


def transpose_in(k, c, xd, row0, T, xT):
    ts_ = min(128, T)
    nsub = T // ts_
    m = k.mark()
    stg = [k.sb([128, D], F32, name="xstg") for _ in range(2)]
    for s in range(nsub):
        st = stg[s % 2]
        k.dma(st[0:ts_, :], xd[row0 + s * ts_: row0 + (s + 1) * ts_, :])
        for g in range(4):
            p = k.ps()
            for j in range(4):
                kc = g * 4 + j
                k.tr(p[:, j * ts_:(j + 1) * ts_], st[0:ts_, kc * 128:(kc + 1) * 128], c.ident[0:ts_, 0:ts_])
            src = p[:, 0:4 * ts_].re("p (j t) -> p j t", t=ts_)
            dst = xT[:, g * 4:(g + 1) * 4, s * ts_:(s + 1) * ts_]
            k.copy(dst, src, e="act" if g % 2 else "dve")
    k.release(m)


def transpose_out(k, c, srcfn, nkc, T, dd, row0, col0=0, rows=128):
    ts_ = min(128, T)
    nsub = T // ts_
    m = k.mark()
    stg = [k.sb([128, nkc * rows], F32, name="ostg") for _ in range(2)]
    for s in range(nsub):
        st = stg[s % 2]
        for g0 in range(0, nkc, 4):
            ng = min(4, nkc - g0)
            p = k.ps()
            for j in range(ng):
                src = srcfn(g0 + j)[:, s * ts_:(s + 1) * ts_]
                k.tr(p[0:ts_, j * rows:(j + 1) * rows], src, c.ident[0:rows, 0:rows])
            k.copy(st[0:ts_, g0 * rows:(g0 + ng) * rows], p[0:ts_, 0:ng * rows], e="act" if (g0 // 4) % 2 else "dve")
        k.dma(dd[row0 + s * ts_: row0 + (s + 1) * ts_, col0:col0 + nkc * rows], st[0:ts_, :], store=True)
    k.release(m)


def rstd_from_psum(k, out, ps, n, eps=EPS):
    k.ts(out, ps, 1.0 / n, ALU.mult, eps, ALU.add)
    k.act(out, out, AF.Sqrt)
    k.recip(out, out)


def bc3(v, nseq, tps):
    return v.un(2).bc([128, nseq, tps])


def v3(v, tps):
    return v.re("p (s t) -> p s t", t=tps)


def modnorm(k, c, xT, T, gs, sh, mcol, nseq, tps, hT):
    m = k.mark()
    sq = [k.sb([128, T], F32, name="sq") for _ in range(2)]
    pss = k.ps()
    for kc in range(NKC):
        s_ = sq[kc % 2]
        k.act(s_[:], xT[:, kc, :], AF.Square)
        k.mm(pss[:, 0:T], c.ones[:], s_[:], start=(kc == 0), stop=(kc == NKC - 1))
    rstd = k.sb([128, T], F32, name="rstd")
    rstd_from_psum(k, rstd[:], pss[:, 0:T], D)
    tmp = [k.sb([128, T], F32, name="mtmp") for _ in range(2)]
    for kc in range(NKC):
        t_ = tmp[kc % 2]
        e1 = "dve" if kc % 2 == 0 else "pool"
        k.tt(t_[:], xT[:, kc, :], rstd[:], ALU.mult, e=e1)
        k.tt(v3(t_[:], tps), v3(t_[:], tps), bc3(gs(kc), nseq, tps), ALU.mult, e=e1)
        k.tt(v3(hT[:, kc, :], tps), v3(t_[:], tps), bc3(sh(kc), nseq, tps), ALU.add, e=e1)
    k.release(m)


def ada_phase(k, c, wd, bd, N, name):
    nch = N // 128
    modsT = k.sb([128, nch, 17], F32, name=name)
    m = k.mark()
    bT = k.sb([128, nch], F32, name="bT")
    brow = k.sb([128, 128], F32, name="brow")
    k.dma(brow[0:nch, :], bd[:].re("(c p) -> c p", p=128))
    p = k.ps()
    k.tr(p[:, 0:nch], brow[0:nch, :], c.ident[0:nch, 0:nch])
    k.copy(bT[:], p[:, 0:nch])
    ws = WS(k, NKC * 512)
    nblk = N // 512
    nxt = ws.load(wd, 0, NKC, [(0, 512)])
    for b in range(nblk):
        w = nxt
        if b + 1 < nblk:
            nxt = ws.load(wd, 0, NKC, [((b + 1) * 512, 512)])
        p = k.ps()
        for sub in range(4):
            for kc in range(NKC):
                k.mm(p[:, sub * 17:(sub + 1) * 17], w[:, kc, sub * 128:(sub + 1) * 128], c.scT[:, kc, :],
                     start=(kc == 0), stop=(kc == NKC - 1))
        src = p[:, 0:4 * 17].re("p (j t) -> p j t", t=17)
        k.tt(modsT[:, b * 4:(b + 1) * 4, :], src, bT[:, b * 4:(b + 1) * 4].un(2).bc([128, 4, 17]), ALU.add)
    k.release(m)
    return modsT


def linear_fm(k, ws, wd, r0, nk, c0, ncols, blk, actfn, T, consume):
    nblk = (ncols + blk - 1) // blk
    def spec(b):
        return [(c0 + b * blk, min(blk, ncols - b * blk))]
    nxt = ws.load(wd, r0, nk, spec(0))
    for b in range(nblk):
        w = nxt
        if b + 1 < nblk:
            nxt = ws.load(wd, r0, nk, spec(b + 1))
        nc_ = min(blk, ncols - b * blk)
        for j in range(0, nc_, 128):
            mcols = min(128, nc_ - j)
            p = k.ps()
            for kc in range(nk):
                k.mm(p[0:mcols, 0:T], w[:, kc, j:j + mcols], actfn(kc), start=(kc == 0), stop=(kc == nk - 1))
            consume((b * blk + j) // 128, p)


def ffn_phase(k, c, L, xT, T, mods, mcol, nseq, tps, l):
    m = k.mark()
    hT = k.sb([128, NKC, T], BF16, nsplit=NKC, name="hT")
    k.ts(c.gs[:, :, :], mods[:, 4 * 16:5 * 16, :], 1.0, ALU.add)
    k.tt(c.gs[:, :, :], c.gs[:, :, :], c.gffn[:, l * 16:(l + 1) * 16].un(2).bc([128, 16, 17]), ALU.mult)
    modnorm(k, c, xT, T, lambda kc: c.gs[:, kc, mcol], lambda kc: mods[:, 3 * 16 + kc, mcol], mcol, nseq, tps, hT)
    ws = WS(k, NKC * 512)
    NQ = 2
    CQ = DFF // NQ
    nq = CQ // 128
    actT = k.sb([128, nq, T], BF16, nsplit=nq, name="actT")
    gsil = [k.sb([128, T], F32, name="gsil") for _ in range(2)]
    wgu = L.w_gate_up[l]
    wdn = L.w_down[l]
    for qd in range(NQ):
        nb = CQ // 256
        def ld(bi):
            col = qd * CQ + bi * 256
            return ws.load(wgu, 0, NKC, [(col, 256), (DFF + col, 256)])
        nxt = ld(0)
        for bi in range(nb):
            w = nxt
            if bi + 1 < nb:
                nxt = ld(bi + 1)
            for jj in range(2):
                j = bi * 2 + jj
                pg = k.ps()
                pu = k.ps()
                for kc in range(NKC):
                    k.mm(pg[:, 0:T], w[:, kc, jj * 128:(jj + 1) * 128], hT[:, kc, :], start=(kc == 0), stop=(kc == NKC - 1))
                for kc in range(NKC):
                    k.mm(pu[:, 0:T], w[:, kc, 256 + jj * 128:256 + (jj + 1) * 128], hT[:, kc, :], start=(kc == 0), stop=(kc == NKC - 1))
                g_ = gsil[j % 2]
                k.act(g_[:], pg[:, 0:T], AF.Silu)
                k.tt(actT[:, j, :], g_[:], pu[:, 0:T], ALU.mult)
        def cons(oc, p):
            resid_add(k, c, xT, oc, p, T, mods[:, 5 * 16 + oc, mcol], nseq, tps)
        linear_fm(k, ws, wdn, qd * CQ, nq, 0, D, 256, lambda kc: actT[:, kc, :], T, cons)
    k.release(m)


def resid_add(k, c, xT, oc, p, T, gate, nseq, tps):
    if nseq == 1:
        k.stt(xT[:, oc, :], p[:, 0:T], gate, xT[:, oc, :], ALU.mult, ALU.add)
    else:
        t_ = c.rtmp[c.rti % 2]
        c.rti += 1
        k.tt(v3(t_[:, 0:T], tps), v3(p[:, 0:T], tps), bc3(gate, nseq, tps), ALU.mult)
        k.tt(xT[:, oc, :], xT[:, oc, :], t_[:, 0:T], ALU.add, e="pool")


def gdn_chunks_batched(k, c, G, st, kh, B, kTb, qTb, kn, xc, zs, beta, negg, gcs, edl, bg, ogT):
    NCH = 4
    nlev = getattr(c, "nlev", 6)
    gstop = int(os.environ.get("GSTOP", "99"))
    def c3(v):
        return v.re("p (c j) -> p c j", j=128)
    def cs_(ch):
        return slice(ch * 128, (ch + 1) * 128)
    k.ps_rot = [3, 4, 5, 6, 7]
    pG, pQK, pK = k.psb[0], k.psb[1], k.psb[2]
    for ch in range(NCH):
        cs = cs_(ch)
        k.mm(pG[:, cs], kTb[:, cs], kTb[:, cs])
        k.mm(pQK[:, cs], qTb[:, cs], kTb[:, cs])
        k.tr(pK[:, cs], kn[:, cs], c.ident[:, :])
    X1, X2, R = B.X1, B.X2, B.R
    Ub = G.U[:, :].un(1).bc([128, NCH, 128])
    Mnb = G.Mn[:, :].un(1).bc([128, NCH, 128])
    SLb = G.SL[:, :].un(1).bc([128, NCH, 128])
    Ib = c.ident[:, :].un(1).bc([128, NCH, 128])
    for a in range(2):
        hh = 2 * kh + a
        def bcol(t):
            return t[:, :, hh:hh + 1].bc([128, NCH, 128])
        pV = k.ps()
        for ch in range(NCH):
            k.tr(pV[:, cs_(ch)], xc[2 + a][:, cs_(ch)], c.ident[:, :])
        k.tt(c3(B.vb[:, :]), c3(pV[:, :]), bcol(beta), ALU.mult)
        k.tt(c3(B.kbg[:, :]), c3(pK[:, :]), bcol(bg), ALU.mult)
        k.tt(c3(B.kd[:, :]), c3(pK[:, :]), bcol(edl), ALU.mult)
        if gstop <= 1:
            continue
        k.tt(c3(X1[:, :]), Ub, bcol(negg), ALU.mult, e="pool")
        pE = k.ps()
        k.mm(pE[:, :], c.ones[:, :], X1[:, :])
        k.tt(c3(X2[:, :]), c3(pE[:, :]), Mnb, ALU.add)
        k.tt(c3(X2[:, :]), c3(X2[:, :]), bcol(gcs), ALU.add, e="pool")
        k.act(X2[:, :], X2[:, :], AF.Exp)
        k.act(X1[:, :], pE[:, :], AF.Exp, scale=-1.0)
        if gstop <= 2:
            continue
        k.tt(B.oT[:, :], pG[:, :], X2[:, :], ALU.mult)
        if gstop == 21:
            continue
        k.tt(c3(B.oT[:, :]), c3(B.oT[:, :]), SLb, ALU.mult, e="pool")
        if gstop == 22:
            continue
        A0f = B.Af[0]
        k.tt(c3(A0f[:, :]), c3(B.oT[:, :]), bcol(beta), ALU.mult)
        k.tt(B.qkb[:, :], pQK[:, :], X2[:, :], ALU.mult)
        pBt = k.ps()
        for ch in range(NCH):
            k.tr(pBt[:, cs_(ch)], A0f[:, cs_(ch)], c.ident[:, :])
        ptr = k.ps()
        ptrb = ptr[:, :].bitcast(BF16)
        for ch in range(NCH):
            k.tr(ptrb[:, ch * 128:(ch + 1) * 128], B.qkb[:, cs_(ch)], B.identbf[:, :])
        k.copy(B.Bf[0][:, :], pBt[:, :], e="act")
        k.copy(B.qkT[:, :], ptrb[:, 0:512], e="dve")
        k.tt(c3(R[0][:, :]), Ib, c3(B.Bf[0][:, :]), ALU.subtract, e="pool")
        Ac, Bc, Rc, ri = A0f, B.Bf[0], R[0], 0
        for l in range(nlev):
            An = B.Af[(l + 1) % 2]
            Bn = B.Bf[(l + 1) % 2]
            pA = k.ps()
            for ch in range(NCH):
                k.mm(pA[:, cs_(ch)], Bc[:, cs_(ch)], Ac[:, cs_(ch)])
            if l < nlev - 1:
                pB = k.ps()
                for ch in range(NCH):
                    k.mm(pB[:, cs_(ch)], Ac[:, cs_(ch)], Bc[:, cs_(ch)])
            k.copy(An[:, :], pA[:, :], e="act")
            if l < nlev - 1:
                k.copy(Bn[:, :], pB[:, :], e="dve")
            pR = k.ps()
            for ch in range(NCH):
                k.mm(pR[:, cs_(ch)], An[:, cs_(ch)], Rc[:, cs_(ch)])
            Rn = R[(ri + 1) % 2]
            ri += 1
            k.tt(Rn[:, :], pR[:, :], Rc[:, :], ALU.add)
            Rc = Rn
            Ac, Bc = An, Bn
        k.copy(B.Rb[:, :], Rc[:, :], e="pool")
        if gstop <= 5:
            continue
        pw = k.ps()
        for ch in range(NCH):
            k.mm(pw[:, cs_(ch)], B.kbg[:, cs_(ch)], B.Rb[:, cs_(ch)])
        k.act(B.wTn[:, :], pw[:, :], AF.Copy, scale=-1.0)
        k.tt(B.qg[:, :], qTb[:, :], X1[:, :], ALU.mult, e="pool")
        for ch in range(NCH):
            cs = cs_(ch)
            Sbt = B.Sbh[ch % 2]
            k.copy(Sbt[:, :], st.S[:, hh, :], e="pool")
            pv = k.ps()
            k.mm(pv[:, 0:128], B.Rb[:, cs], B.vb[:, cs], start=True, stop=False)
            k.mm(pv[:, 0:128], B.wTn[:, cs], Sbt[:, :], start=False, stop=True)
            vn = B.vn[ch % 2]
            k.copy(vn[:, :], pv[:, 0:128], e="act")
            po = k.ps()
            k.mm(po[:, 0:128], Sbt[:, :], B.qg[:, cs], start=True, stop=False)
            k.mm(po[:, 0:128], vn[:, :], B.qkT[:, cs], start=False, stop=True)
            k.copy(B.oT[:, cs], po[:, 0:128], e="act")
            pS = k.ps()
            k.mm(pS[:, 0:128], B.kd[:, cs], vn[:, :])
            k.stt(st.S[:, hh, :], st.S[:, hh, :], X1[:, ch * 128 + 127:ch * 128 + 128], pS[:, 0:128], ALU.mult, ALU.add)
        if gstop <= 6:
            continue
        k.tt(X2[:, :], B.oT[:, :], B.oT[:, :], ALU.mult, e="pool")
        pq = k.ps()
        k.mm(pq[:, :], c.ones[:, :], X2[:, :])
        orn = R[0]
        rstd_from_psum(k, orn[:, :], pq[:, :], 128)
        k.tt(B.oT[:, :], B.oT[:, :], orn[:, :], ALU.mult)
        k.stt(ogT[:, hh, :], B.oT[:, :], c.gnorm[:, 0:1], zs[a][:, :], ALU.mult, ALU.mult)
    k.ps_rot = [2, 3, 4, 5, 6, 7]


def gdn_phase(k, c, L, xT, T, C, mods, mcol, nseq, tps, G, st, samp):
    nch = T // C
    nlev = 1 if samp else getattr(c, 'nlev', 6)
    m = k.mark()
    k.ps_rot = [2, 3, 4, 5, 6, 7]
    hT = k.sb([128, NKC, T], BF16, nsplit=NKC, name="hT")
    k.ts(c.gs[:, :, :], mods[:, 16:32, :], 1.0, ALU.add)
    k.tt(c.gs[:, :, :], c.gs[:, :, :], c.gmix[:, 0:16].un(2).bc([128, 16, 17]), ALU.mult)
    modnorm(k, c, xT, T, lambda kc: c.gs[:, kc, mcol], lambda kc: mods[:, kc, mcol], mcol, nseq, tps, hT)
    ogT = k.sb([128, NHV, T], BF16, nsplit=NHV, name="ogT")
    ws = WS(k, NKC * 512)
    wba = ws.load(L.gdn_w_in, 0, NKC, [(12288, 64)])
    beta = k.sb([128, nch, 32], F32, nsplit=nch, name="beta")
    gg = k.sb([128, nch, 32], F32, nsplit=nch, name="gg")
    negg = k.sb([128, nch, 32], F32, nsplit=nch, name="negg")
    gcs = k.sb([128, nch, 32], F32, nsplit=nch, name="gcs")
    edl = k.sb([128, nch, 32], F32, nsplit=nch, name="edl")
    bg = k.sb([128, nch, 32], F32, nsplit=nch, name="bg")
    t1 = k.sb([128, 32], F32, name="t1")
    t2 = k.sb([128, 32], F32, name="t2")
    for ch in range(nch):
        p = k.ps()
        for kc in range(NKC):
            k.mm(p[0:C, 0:64], hT[:, kc, ch * C:(ch + 1) * C], wba[:, kc, 0:64], start=(kc == 0), stop=(kc == NKC - 1))
        k.act(beta[0:C, ch, :], p[0:C, 0:32], AF.Sigmoid)
        k.tt(t1[0:C, :], p[0:C, 32:64], c.dtb[0:C, :], ALU.add)
        k.act(t2[0:C, :], t1[0:C, :], AF.Abs)
        k.act(t2[0:C, :], t2[0:C, :], AF.Exp, scale=-1.0)
        k.act(t2[0:C, :], t2[0:C, :], AF.Ln, bias=1.0)
        k.ts(t1[0:C, :], t1[0:C, :], 0.0, ALU.max)
        k.tt(t1[0:C, :], t1[0:C, :], t2[0:C, :], ALU.add)
        k.tt(gg[0:C, ch, :], t1[0:C, :], c.nega[0:C, :], ALU.mult)
        k.ts(negg[0:C, ch, :], gg[0:C, ch, :], -1.0, ALU.mult)
        p2 = k.ps()
        k.mm(p2[0:C, 0:32], G.U[0:C, 0:C], gg[0:C, ch, :])
        k.mm(p2[0:C, 32:64], G.OB[0:C, 0:C], gg[0:C, ch, :])
        k.copy(gcs[0:C, ch, :], p2[0:C, 0:32])
        k.tt(t1[0:C, :], p2[0:C, 32:64], gcs[0:C, ch, :], ALU.subtract)
        k.act(edl[0:C, ch, :], t1[0:C, :], AF.Exp)
        k.act(t2[0:C, :], gcs[0:C, ch, :], AF.Exp)
        k.tt(bg[0:C, ch, :], t2[0:C, :], beta[0:C, ch, :], ALU.mult)
    if getattr(c, "stage", 99) <= 2:
        c.dbg = [("beta", beta[:, :, :]), ("gg", gg[:, :, :]), ("gcs", gcs[:, :, :]), ("edl", edl[:, :, :]), ("hT0", None)]
        return
    cb = [k.sb([128, nseq, 3 + tps], F32, name="cb") for _ in range(2)]
    xc = [k.sb([128, T], F32, name="xc") for _ in range(4)]
    qTb = k.sb([128, T], BF16, name="qTb")
    kTb = k.sb([128, T], BF16, name="kTb")
    kn = k.sb([128, T], F32, name="kn")
    zs = [k.sb([128, T], BF16, name="zs") for _ in range(2)]
    sqt = k.sb([128, T], F32, name="sqt")
    rn = k.sb([128, T], F32, name="rn")
    def f32t(n="ct"):
        return k.sb([128, 128], F32, name=n)
    if not samp:
        Bt = Ctx()
        def b16(n, w=512):
            return k.sb([128, w], BF16, name=n)
        Bt.qkb = b16("qkb")
        Bt.Af = [k.sb([128, 512], F32, name="Af0"), k.sb([128, 512], F32, name="Af1")]
        Bt.Bf = [k.sb([128, 512], F32, name="Bf0"), k.sb([128, 512], F32, name="Bf1")]
        Bt.Rb = b16("Rb"); Bt.qkT = b16("qkT"); Bt.vb = b16("vb"); Bt.kbg = b16("kbg"); Bt.kd = b16("kd")
        Bt.wTn = b16("wTn"); Bt.qg = b16("qg")
        Bt.vn = [b16("vn0", 128), b16("vn1", 128)]; Bt.Sbh = [b16("Sbh0", 128), b16("Sbh1", 128)]
        Bt.oT = k.sb([128, 512], F32, name="oTall")
        Bt.identbf = b16("identbf", 128)
        k.copy(Bt.identbf[:, :], c.ident[:, :])
        Bt.X1 = xc[0]; Bt.X2 = xc[1]; Bt.R = [sqt, rn]
    else:
        rh = [f32t("rh") for _ in range(2)]
        tmpE = [f32t("tmpE") for _ in range(2)]
        E1 = [f32t("E1") for _ in range(2)]
        egrow = [f32t("egrow") for _ in range(2)]
        Am = [f32t("Am") for _ in range(4)]
        Bm = [f32t("Bm") for _ in range(4)]
        qk = [f32t("qk") for _ in range(2)]
        Rr = [f32t("Rr") for _ in range(4)]
        qkT = [k.sb([128, 128], BF16, name="qkT") for _ in range(2)]
        Rb = [k.sb([128, 128], BF16, name="Rb") for _ in range(2)]
        vb = [k.sb([128, 128], BF16, name="vb") for _ in range(2)]
        kbg = [k.sb([128, 128], BF16, name="kbg") for _ in range(2)]
        kd = [k.sb([128, 128], BF16, name="kd") for _ in range(2)]
        wTn = [k.sb([128, 128], BF16, name="wTn") for _ in range(2)]
        vn = [k.sb([128, 128], BF16, name="vn") for _ in range(2)]
        qg = [k.sb([128, 128], BF16, name="qg") for _ in range(2)]
        Sbh = [k.sb([128, 128], BF16, name="Sbh") for _ in range(2)]
        osb = [f32t("osb") for _ in range(2)]
        osq = [f32t("osq") for _ in range(2)]
        orn = [f32t("orn") for _ in range(2)]
        if samp:
            vnT = [f32t("vnT") for _ in range(2)]
            kds = [k.sb([128, 128], BF16, name="kds") for _ in range(2)]
            Sall = [k.sb([128, 16, 128], F32, name="Sall") for _ in range(2)]
            Sball = [k.sb([128, 16, 128], BF16, name="Sball") for _ in range(2)]
    cnt = 0
    for kh in range(NHK):
        w = ws.load(L.gdn_w_in, 0, NKC, [(kh * 128, 128), (2048 + kh * 128, 128), (4096 + kh * 256, 256)])
        wz = ws.load(L.gdn_w_in, 0, NKC, [(8192 + kh * 256, 256)])
        ids = [kh, 16 + kh, 32 + 2 * kh, 33 + 2 * kh]
        for ci in range(4):
            p = k.ps()
            for kc in range(NKC):
                k.mm(p[:, 0:T], w[:, kc, ci * 128:(ci + 1) * 128], hT[:, kc, :], start=(kc == 0), stop=(kc == NKC - 1))
            cid = ids[ci]
            cb_ = cb[ci % 2]
            k.copy(cb_[:, :, 3:3 + tps], v3(p[:, 0:T], tps), e="act")
            k.copy(cb_[:, :, 0:3], st.convtail[:, cid, :, :], e="pool")
            a_ = v3(xc[ci][:], tps)
            k.ts(a_, cb_[:, :, 0:tps], c.wconv[:, cid, 0:1], ALU.mult, e="pool")
            for j in range(1, 4):
                k.stt(a_, cb_[:, :, j:j + tps], c.wconv[:, cid, j:j + 1], a_, ALU.mult, ALU.add,
                      e="pool" if j % 2 else "dve")
            k.copy(st.convtail[:, cid, :, :], cb_[:, :, tps:tps + 3], e="pool")
            k.act(xc[ci][:], xc[ci][:], AF.Silu)
        for ci in range(2):
            k.tt(sqt[:], xc[ci][:], xc[ci][:], ALU.mult, e="pool")
            p = k.ps()
            k.mm(p[:, 0:T], c.ones[:], sqt[:])
            k.ts(rn[:], p[:, 0:T], EPS, ALU.add)
            k.act(rn[:], rn[:], AF.Sqrt)
            k.recip(rn[:], rn[:])
            if ci == 0:
                k.stt(qTb[:], xc[0][:], 128 ** -0.5, rn[:], ALU.mult, ALU.mult)
            else:
                k.tt(kn[:], xc[1][:], rn[:], ALU.mult)
                k.copy(kTb[:], kn[:], e="pool")
        for a in range(2):
            p = k.ps()
            for kc in range(NKC):
                k.mm(p[:, 0:T], wz[:, kc, a * 128:(a + 1) * 128], hT[:, kc, :], start=(kc == 0), stop=(kc == NKC - 1))
            k.act(zs[a][:], p[:, 0:T], AF.Silu)
        if getattr(c, "stage", 99) <= 3:
            c.dbg = [("xc0", xc[0][:]), ("xc2", xc[2][:]), ("kn", kn[:]), ("rn", rn[:])]
            return
        if samp:
            for a in range(2):
                hh = 2 * kh + a
                k.dma(Sall[a][:, :, :], st.ssm_in[:, hh, :, :].re("s p v -> p s v"))
                k.copy(Sball[a][:, :, :], Sall[a][:, :, :], e="pool")
        if not samp:
            gdn_chunks_batched(k, c, G, st, kh, Bt, kTb, qTb, kn, xc, zs, beta, negg, gcs, edl, bg, ogT)
            continue
        for ch in range(nch):
            cs = slice(ch * C, (ch + 1) * C)
            pG = k.psb[0]
            k.mm(pG[0:C, 0:C], kTb[:, cs], kTb[:, cs])
            k.mm(pG[0:C, C:2 * C], qTb[:, cs], kTb[:, cs])
            pT = k.psb[1]
            k.tr(pT[0:C, 0:128], kn[:, cs], c.ident[:])
            for a in range(2):
                k.tr(pT[0:C, 128 * (1 + a):128 * (2 + a)], xc[2 + a][:, cs], c.ident[:])
            for a in range(2):
                hh = 2 * kh + a
                i2 = cnt % 2
                cnt += 1
                hcol = slice(hh, hh + 1)
                k.ts(rh[i2][0:C, 0:C], G.U[0:C, 0:C], negg[0:C, ch, hcol], ALU.mult, e="pool")
                pE = k.ps()
                k.mm(pE[:, 0:C], c.ones[0:C, :], rh[i2][0:C, 0:C])
                k.tt(tmpE[i2][0:C, 0:C], pE[0:C, 0:C], G.Mn[0:C, 0:C], ALU.add)
                k.act(E1[i2][0:C, 0:C], tmpE[i2][0:C, 0:C], AF.Exp, bias=gcs[0:C, ch, hcol])
                k.act(egrow[i2][:, 0:C], pE[:, 0:C], AF.Exp, scale=-1.0)
                if getattr(c, "stage", 99) == 35:
                    c.dbg = [("E1", E1[i2][0:C, 0:C]), ("egrow", egrow[i2][:, 0:C]), ("tmpE", tmpE[i2][0:C, 0:C])]
                    return
                A0 = Am[0]
                k.stt(A0[0:C, 0:C], pG[0:C, 0:C], beta[0:C, ch, hcol], E1[i2][0:C, 0:C], ALU.mult, ALU.mult)
                k.tt(A0[0:C, 0:C], A0[0:C, 0:C], G.SL[0:C, 0:C], ALU.mult, e="pool")
                k.tt(qk[i2][0:C, 0:C], pG[0:C, C:2 * C], E1[i2][0:C, 0:C], ALU.mult)
                pB = k.ps()
                k.tr(pB[0:C, 0:C], A0[0:C, 0:C], c.ident[0:C, 0:C])
                k.tr(pB[0:C, C:2 * C], qk[i2][0:C, 0:C], c.ident[0:C, 0:C])
                B0 = Bm[0]
                k.copy(B0[0:C, 0:C], pB[0:C, 0:C], e="act")
                k.copy(qkT[i2][0:C, 0:C], pB[0:C, C:2 * C], e="act")
                R = Rr[0]
                k.tt(R[0:C, 0:C], c.ident[0:C, 0:C], B0[0:C, 0:C], ALU.subtract, e="pool")
                if getattr(c, "stage", 99) == 36:
                    c.dbg = [("R", R[0:C, 0:C]), ("A0", A0[0:C, 0:C]), ("B0", B0[0:C, 0:C]), ("qk", qk[i2][0:C, 0:C])]
                    return
                Ac, Bc = A0, B0
                ri = 0
                for l in range(nlev):
                    An = Am[(l + 1) % 4]
                    Bn = Bm[(l + 1) % 4]
                    pA = k.ps()
                    k.mm(pA[0:C, 0:C], Bc[0:C, 0:C], Ac[0:C, 0:C])
                    k.copy(An[0:C, 0:C], pA[0:C, 0:C], e="act")
                    if l < nlev - 1:
                        pA2 = k.ps()
                        k.mm(pA2[0:C, 0:C], Ac[0:C, 0:C], Bc[0:C, 0:C])
                        k.copy(Bn[0:C, 0:C], pA2[0:C, 0:C], e="dve")
                    if getattr(c, "skipR", 0) and l >= 1:
                        Ac, Bc = An, Bn
                        continue
                    pR = k.ps()
                    k.mm(pR[0:C, 0:C], An[0:C, 0:C], R[0:C, 0:C])
                    Rn = Rr[(ri + 1) % 4]
                    ri += 1
                    k.tt(Rn[0:C, 0:C], pR[0:C, 0:C], R[0:C, 0:C], ALU.add)
                    R = Rn
                    Ac, Bc = An, Bn
                k.copy(Rb[i2][0:C, 0:C], R[0:C, 0:C], e="pool")
                if getattr(c, "stage", 99) <= 4:
                    c.dbg = [("R", R[0:C, 0:C]), ("E1", E1[i2][0:C, 0:C]), ("A0", Am[0][0:C, 0:C]), ("B0", Bm[0][0:C, 0:C]), ("egrow", egrow[i2][:, 0:C])]
                    return
                k.ts(vb[i2][0:C, :], pT[0:C, 128 * (1 + a):128 * (2 + a)], beta[0:C, ch, hcol], ALU.mult)
                k.act(kbg[i2][0:C, :], pT[0:C, 0:128], AF.Copy, scale=bg[0:C, ch, hcol])
                k.act(kd[i2][0:C, :], pT[0:C, 0:128], AF.Copy, scale=edl[0:C, ch, hcol])
                pw = k.ps()
                k.mm(pw[:, 0:C], kbg[i2][0:C, :], Rb[i2][0:C, 0:C])
                k.act(wTn[i2][:, 0:C], pw[:, 0:C], AF.Copy, scale=-1.0)
                k.tt(qg[i2][:, 0:C], qTb[:, cs], egrow[i2][:, 0:C], ALU.mult, e="pool")
                po = k.ps()
                if not samp:
                    Sbt = Sbh[i2]
                    k.copy(Sbt[:, :], st.S[:, hh, :], e="pool")
                    Sb_h = Sbt[:, :]
                    pv = k.ps()
                    k.mm(pv[0:C, 0:128], Rb[i2][0:C, 0:C], vb[i2][0:C, :], start=True, stop=False)
                    k.mm(pv[0:C, 0:128], wTn[i2][:, 0:C], Sb_h, start=False, stop=True)
                    k.copy(vn[i2][0:C, :], pv[0:C, 0:128], e="act")
                    k.mm(po[:, 0:C], Sb_h, qg[i2][:, 0:C], start=True, stop=False)
                    k.mm(po[:, 0:C], vn[i2][0:C, :], qkT[i2][0:C, 0:C], start=False, stop=True)
                    k.copy(osb[i2][:, 0:C], po[:, 0:C], e="act")
                    pS = k.ps()
                    k.mm(pS[:, 0:128], kd[i2][0:C, :], vn[i2][0:C, :])
                    k.stt(st.S[:, hh, :], st.S[:, hh, :], egrow[i2][:, C - 1:C], pS[:, 0:128], ALU.mult, ALU.add)
                else:
                    pu = k.ps()
                    k.mm(pu[:, 0:C], vb[i2][0:C, :], Rb[i2][0:C, 0:C], start=True, stop=False)
                    for s in range(nseq):
                        k.mm(pu[:, s * tps:(s + 1) * tps], Sball[a][:, s, :], wTn[i2][:, s * tps:(s + 1) * tps],
                             start=False, stop=(s == nseq - 1))
                    k.copy(vnT[i2][:, 0:C], pu[:, 0:C], e="act")
                    pvt = k.ps()
                    k.tr(pvt[0:C, 0:128], vnT[i2][:, 0:C], c.ident[:])
                    k.copy(vn[i2][0:C, :], pvt[0:C, 0:128], e="act")
                    k.mm(po[:, 0:C], vn[i2][0:C, :], qkT[i2][0:C, 0:C], start=True, stop=False)
                    for s in range(nseq):
                        k.mm(po[:, s * tps:(s + 1) * tps], Sball[a][:, s, :], qg[i2][:, s * tps:(s + 1) * tps],
                             start=False, stop=(s == nseq - 1))
                    k.copy(osb[i2][:, 0:C], po[:, 0:C], e="act")
                    for s in range(nseq):
                        j2 = s % 2
                        k.ts(kds[j2][0:C, :], kd[i2][0:C, :], G.seqmask[0:C, s:s + 1], ALU.mult, e="pool")
                        pS = k.ps()
                        k.mm(pS[:, 0:128], kds[j2][0:C, :], vn[i2][0:C, :])
                        k.stt(Sall[a][:, s, :], Sall[a][:, s, :], egrow[i2][:, s * tps + tps - 1:s * tps + tps],
                              pS[:, 0:128], ALU.mult, ALU.add)
                k.tt(osq[i2][:, 0:C], osb[i2][:, 0:C], osb[i2][:, 0:C], ALU.mult, e="pool")
                pq = k.ps()
                k.mm(pq[:, 0:C], c.ones[:], osq[i2][:, 0:C])
                rstd_from_psum(k, orn[i2][:, 0:C], pq[:, 0:C], 128)
                k.tt(osb[i2][:, 0:C], osb[i2][:, 0:C], orn[i2][:, 0:C], ALU.mult)
                k.stt(ogT[:, hh, cs], osb[i2][:, 0:C], c.gnorm[:, 0:1], zs[a][:, cs], ALU.mult, ALU.mult)
                if getattr(c, "stage", 99) <= 5:
                    c.dbg = [("osb", osb[i2][:, 0:C]), ("S0", st.S[:, hh, :])]
                    return
        if samp:
            for a in range(2):
                hh = 2 * kh + a
                k.dma(st.ssm_out[:, hh, :, :].re("s p v -> p s v"), Sall[a][:, :, :], store=True)
    ws2 = ws
    def cons(oc, p):
        resid_add(k, c, xT, oc, p, T, mods[:, 32 + oc, mcol], nseq, tps)
    k.ps_rot = list(range(8))
    linear_fm(k, ws2, L.gdn_w_out, 0, 32, 0, D, 256, lambda kc: ogT[:, kc, :], T, cons)
    k.release(m)


class GC:
    pass


def make_consts():
    cst = {}
    cst["ident"] = np.eye(128, dtype=np.float32)
    cst["ones"] = np.ones((128, 128), np.float32)
    i = np.arange(128)
    cst["U_p"] = (i[:, None] <= i[None, :]).astype(np.float32)
    cst["OB_p"] = np.ones((128, 128), np.float32)
    cst["Mn_p"] = np.where(i[None, :] <= i[:, None], 0.0, NEG).astype(np.float32)
    cst["SL_p"] = (i[None, :] < i[:, None]).astype(np.float32)
    blk = i // 4
    same = blk[:, None] == blk[None, :]
    cst["U_s"] = (same & (i[:, None] <= i[None, :])).astype(np.float32)
    cst["OB_s"] = same.astype(np.float32)
    cst["Mn_s"] = np.where(same & (i[None, :] <= i[:, None]), 0.0, NEG).astype(np.float32)
    cst["SL_s"] = (same & (i[None, :] < i[:, None])).astype(np.float32)
    sm = np.zeros((128, 32), np.float32)
    sm[i, blk] = 1.0
    cst["seqmask"] = sm
    return cst


def setup_common(k, c, I):
    c.ident = load_const(k, I["ident"], [128, 128])
    c.ones = load_const(k, I["ones"], [128, 128])
    c.gs = k.sb([128, 16, 17], F32, name="gs")
    c.rtmp = [k.sb([128, 64], F32, name="rtmp") for _ in range(2)]
    c.rti = 0
    def colvec(dv, n, name):
        t = k.sb([128, n], F32, name=name)
        m = k.mark()
        row = k.sb([128, 128], F32, name="row")
        k.dma(row[0:n, :], dv.re("(c p) -> c p", p=128))
        p = k.ps()
        k.tr(p[:, 0:n], row[0:n, :], c.ident[0:n, 0:n])
        k.copy(t[:], p[:, 0:n])
        k.release(m)
        return t
    c.colvec = colvec
    c.scT = k.sb([128, NKC, 17], BF16, name="scT")
    m = k.mark()
    cc = k.sb([128, D], F32, name="cc")
    k.dma(cc[0:17, :], I["cc"][:, :])
    k.act(cc[0:17, :], cc[0:17, :], AF.Silu)
    p = k.ps()
    for kc in range(NKC):
        k.tr(p[:, kc * 17:(kc + 1) * 17], cc[0:17, kc * 128:(kc + 1) * 128], c.ident[0:17, 0:17])
    k.copy(c.scT[:, :, :], p[:, 0:NKC * 17].re("p (k t) -> p k t", t=17))
    k.release(m)


def setup_gdn(k, c, I):
    c.gmix = c.colvec(I["g_mix"][0:2 * D], 32, "gmix")
    c.gffn = c.colvec(I["g_ffn"][0:2 * D], 32, "gffn")
    c.gnorm = k.sb([128, 1], F32, name="gnorm")
    k.dma(c.gnorm[:, :], I["gdn_g_norm"][0:128].re("(p o) -> p o", o=1))
    c.dtb = k.sb([128, 32], F32, name="dtb")
    k.dma(c.dtb[:, :], I["gdn_dt_bias"][0:32].re("(o n) -> o n", o=1).bc([128, 32]))
    c.nega = k.sb([128, 32], F32, name="nega")
    k.dma(c.nega[:, :], I["gdn_a_log"][0:32].re("(o n) -> o n", o=1).bc([128, 32]))
    k.act(c.nega[:, :], c.nega[:, :], AF.Exp)
    k.ts(c.nega[:, :], c.nega[:, :], -1.0, ALU.mult)
    c.wconv = k.sb([128, 64, 4], F32, name="wconv")
    m = k.mark()
    wr = k.sb([128, 8192], F32, name="wr")
    k.dma(wr[0:4, :], I["gdn_w_conv"][:, :])
    for g in range(2):
        p = k.ps()
        for j in range(32):
            cid = g * 32 + j
            k.tr(p[:, j * 4:(j + 1) * 4], wr[0:4, cid * 128:(cid + 1) * 128], c.ident[0:4, 0:4])
        k.copy(c.wconv[:, g * 32:(g + 1) * 32, :], p[:, 0:128].re("p (c j) -> p c j", j=4))
    k.release(m)


def load_G(k, I, sfx):
    G = GC()
    G.U = load_const(k, I["U_" + sfx], [128, 128])
    G.OB = load_const(k, I["OB_" + sfx], [128, 128])
    G.Mn = load_const(k, I["Mn_" + sfx], [128, 128])
    G.SL = load_const(k, I["SL_" + sfx], [128, 128])
    if sfx == "s":
        G.seqmask = load_const(k, I["seqmask"], [128, 32])
    return G


def rope_tables():
    half = 32
    inv = np.power(np.float32(10000.0), -(np.arange(half, dtype=np.float32) / np.float32(half))).astype(np.float32)
    pos = np.concatenate([np.arange(2048), 8192 + np.arange(4)]).astype(np.float32)
    ang = (pos[None, :] * inv[:, None]).astype(np.float32)
    cs = np.cos(ang.astype(np.float64)).astype(np.float32)
    sn = np.sin(ang.astype(np.float64)).astype(np.float32)
    cos2 = np.concatenate([cs, cs], 0)
    sin_s = np.concatenate([-sn, sn], 0)
    cos2 = np.concatenate([cos2, np.tile(cos2[:, 2048:2052], (1, 16))], 1)
    sin_s = np.concatenate([sin_s, np.tile(sin_s[:, 2048:2052], (1, 16))], 1)
    return np.ascontiguousarray(cos2), np.ascontiguousarray(sin_s)


def load_rope(k, c, I, pos0, T):
    c.cos2 = k.sb([64, T], F32, name="cos2")
    c.sin_s = k.sb([64, T], F32, name="sin_s")
    k.dma(c.cos2[:, :], I["rope_cos"][:, pos0:pos0 + T])
    k.dma(c.sin_s[:, :], I["rope_sin"][:, pos0:pos0 + T])


def rope_apply(k, c, out, p1, p2, T, tmp):
    k.tt(tmp[0][0:64, 0:T], p1, c.cos2[:, 0:T], ALU.mult)
    k.tt(tmp[1][0:64, 0:T], p2, c.sin_s[:, 0:T], ALU.mult)
    k.tt(out, tmp[0][0:64, 0:T], tmp[1][0:64, 0:T], ALU.add, e="pool")


def kv_phase(k, c, L, xT, T, mods, mcol, nseq, tps, KB, tok0, ckv_out, kpe_out, row0):
    m = k.mark()
    hT = k.sb([128, NKC, T], BF16, nsplit=NKC, name="hT")
    k.ts(c.gs[:, :, :], mods[:, 16:32, :], 1.0, ALU.add)
    k.tt(c.gs[:, :, :], c.gs[:, :, :], c.gkv[:, 0:16].un(2).bc([128, 16, 17]), ALU.mult)
    modnorm(k, c, xT, T, lambda kc: c.gs[:, kc, mcol], lambda kc: mods[:, kc, mcol], mcol, nseq, tps, hT)
    ws = WS(k, NKC * 640)
    w = ws.load(L.kv_w_down, 0, NKC, [(0, 576), (544, 32), (512, 32)])
    ckf = k.sb([128, 4, T], F32, nsplit=4, name="ckf")
    sq = k.sb([128, T], F32, name="ksq")
    pss = k.psb[0]
    k.ps_rot = [1, 2, 3, 4, 5, 6, 7]
    for rc in range(4):
        p = k.ps()
        for kc in range(NKC):
            k.mm(p[:, 0:T], w[:, kc, rc * 128:(rc + 1) * 128], hT[:, kc, :], start=(kc == 0), stop=(kc == NKC - 1))
        k.copy(ckf[:, rc, :], p[:, 0:T], e="act")
        k.tt(sq[:], ckf[:, rc, :], ckf[:, rc, :], ALU.mult)
        k.mm(pss[:, 0:T], c.ones[:], sq[:], start=(rc == 0), stop=(rc == 3))
    rstd = k.sb([128, T], F32, name="krstd")
    rstd_from_psum(k, rstd[:], pss[:, 0:T], KVL)
    k.ps_rot = list(range(8))
    for rc in range(4):
        k.stt(ckf[:, rc, :], ckf[:, rc, :], c.gkvn[:, rc:rc + 1], rstd[:], ALU.mult, ALU.mult)
        k.copy(KB.ckvT(rc), ckf[:, rc, :], e="pool")
    import os
    kvstop = int(os.environ.get("KVSTOP", "9"))
    if kvstop <= 1:
        k.release(m)
        return
    p1 = k.ps()
    p2 = k.ps()
    for kc in range(NKC):
        k.mm(p1[0:64, 0:T], w[:, kc, 512:576], hT[:, kc, :], start=(kc == 0), stop=(kc == NKC - 1))
    for kc in range(NKC):
        k.mm(p2[0:64, 0:T], w[:, kc, 576:640], hT[:, kc, :], start=(kc == 0), stop=(kc == NKC - 1))
    kpf = k.sb([64, T], F32, name="kpf")
    tmp = [k.sb([64, T], F32, name="rtmpa"), k.sb([64, T], F32, name="rtmpb")]
    rope_apply(k, c, kpf[:, :], p1[0:64, 0:T], p2[0:64, 0:T], T, tmp)
    k.copy(KB.kpeT(), kpf[:, :], e="pool")
    if kvstop <= 2:
        k.release(m)
        return
    ts_ = min(128, T)
    nsub = T // ts_
    stg = [k.sb([128, 576], F32, name="kvstg") for _ in range(2)]
    for s in range(nsub):
        st = stg[s % 2]
        p = k.ps()
        for rc in range(4):
            k.tr(p[0:ts_, rc * 128:(rc + 1) * 128], ckf[:, rc, s * ts_:(s + 1) * ts_], c.ident[:, :])
        k.copy(st[0:ts_, 0:512], p[0:ts_, 0:512], e="act")
        if KB.ckv_tm is not None and kvstop != 3:
            k.copy(KB.ckv_tm(s), st[0:ts_, 0:512], e="pool")
        if kvstop >= 4:
            pk = k.ps()
            k.tr(pk[0:ts_, 0:64], kpf[:, s * ts_:(s + 1) * ts_], c.ident[0:64, 0:64])
            k.copy(st[0:ts_, 512:576], pk[0:ts_, 0:64], e="act")
        k.dma(ckv_out[row0 + s * ts_:row0 + (s + 1) * ts_, :], st[0:ts_, 0:512], store=True)
        if kvstop >= 5:
            k.dma(kpe_out[row0 + s * ts_:row0 + (s + 1) * ts_, :], st[0:ts_, 512:576], store=True)
    k.release(m)


def qside_head(k, c, L, ws, h, cqT, T, Q):
    wq = ws.load(L.mla_w_uq, 0, 4, [(h * 192, 192), (h * 192 + 160, 32), (h * 192 + 128, 32)])
    k.dma(Q.wukf[:, :, :], L.kv_w_uk[:, h * 128:(h + 1) * 128].re("(rc p) d -> p rc d", p=128))
    p = k.ps()
    for rc in range(4):
        k.tr(p[:, rc * 128:(rc + 1) * 128], Q.wukf[:, rc, :], c.ident[:, :])
    k.copy(Q.wukT[:, :], p[:, 0:512], e="act")
    p = k.ps()
    for kc in range(4):
        k.mm(p[:, 0:T], wq[:, kc, 0:128], cqT[:, kc, :], start=(kc == 0), stop=(kc == 3))
    k.copy(Q.qn[:, 0:T], p[:, 0:T], e="act")
    p1 = k.ps()
    for kc in range(4):
        k.mm(p1[0:64, 0:T], wq[:, kc, 128:192], cqT[:, kc, :], start=(kc == 0), stop=(kc == 3))
    p2 = k.ps()
    for kc in range(4):
        k.mm(p2[0:64, 0:T], wq[:, kc, 192:256], cqT[:, kc, :], start=(kc == 0), stop=(kc == 3))
    rope_apply(k, c, Q.qpe_dst(h), p1[0:64, 0:T], p2[0:64, 0:T], T, Q.rtmp)
    for rc in range(4):
        p = k.ps()
        k.mm(p[:, 0:T], Q.wukT[:, rc * 128:(rc + 1) * 128], Q.qn[:, 0:T])
        k.copy(Q.qlat_dst(h, rc), p[:, 0:T], e="act" if rc % 2 else "dve")


def cq_compute(k, c, L, ws, hT, T, cqT):
    cqf = k.sb([128, 4, T], F32, nsplit=4, name="cqf")
    sq = k.sb([128, T], F32, name="cqsq")
    def cons(oc, p):
        k.copy(cqf[:, oc, :], p[:, 0:T], e="act")
    linear_fm(k, ws, L.mla_w_dq, 0, NKC, 0, QL, 512, lambda kc: hT[:, kc, :], T, cons)
    pss = k.ps()
    for rc in range(4):
        k.tt(sq[:], cqf[:, rc, :], cqf[:, rc, :], ALU.mult)
        k.mm(pss[:, 0:T], c.ones[:], sq[:], start=(rc == 0), stop=(rc == 3))
    rstd = k.sb([128, T], F32, name="cqrstd")
    rstd_from_psum(k, rstd[:], pss[:, 0:T], QL)
    for rc in range(4):
        k.stt(cqT[:, rc, :], cqf[:, rc, :], c.gq[:, rc:rc + 1], rstd[:], ALU.mult, ALU.mult)


def attn_phase_prompt(k, c, L, xT, T, mods, KB, tile):
    mcol = slice(0, 1)
    m = k.mark()
    hT = k.sb([128, NKC, T], BF16, nsplit=NKC, name="hT")
    k.ts(c.gs[:, :, :], mods[:, 16:32, :], 1.0, ALU.add)
    k.tt(c.gs[:, :, :], c.gs[:, :, :], c.gmix[:, 16:32].un(2).bc([128, 16, 17]), ALU.mult)
    modnorm(k, c, xT, T, lambda kc: c.gs[:, kc, mcol], lambda kc: mods[:, kc, mcol], mcol, 1, T, hT)
    ws = WS(k, NKC * 512)
    cqT = k.sb([128, 4, T], BF16, nsplit=4, name="cqT")
    m2 = k.mark()
    cq_compute(k, c, L, ws, hT, T, cqT)
    k.release(m2)
    aoT = k.sb([128, NH, T], BF16, nsplit=NH, name="aoT")
    Q = Ctx()
    Q.wukf = k.sb([128, 4, 128], F32, name="wukf")
    Q.wukT = k.sb([128, 512], BF16, name="wukT")
    Q.qn = k.sb([128, T], BF16, name="qn")
    Q.rtmp = [k.sb([64, T], F32, name="qrt0"), k.sb([64, T], F32, name="qrt1")]
    qlat = k.sb([128, 4, T], BF16, nsplit=4, name="qlat")
    qpe = k.sb([64, T], BF16, name="qpe")
    Q.qpe_dst = lambda h: qpe[:, 0:T]
    Q.qlat_dst = lambda h, rc: qlat[:, rc, :]
    pT = [k.sb([128, T], BF16, name="pT") for _ in range(2)]
    linv = k.sb([128, T], F32, name="linv")
    olat = k.sb([128, 4, T], BF16, nsplit=4, name="olat")
    wuv_t = [k.sb([128, 4, 128], BF16, name="wuv") for _ in range(2)]
    nkb = 4 * tile + 4
    for h in range(NH):
        k.ps_rot = [6, 7]
        qside_head(k, c, L, ws, h, cqT, T, Q)
        wuv = wuv_t[h % 2]
        k.dma(wuv[:, :, :], L.kv_w_uv[:, h * 128:(h + 1) * 128].re("(rc p) v -> p rc v", p=128), q="pool")
        k.ps_rot = [7]
        pl = k.psb[6]
        po = [k.psb[2 + rc] for rc in range(4)]

        def scores(kb):
            ps_ = k.psb[kb % 2]
            q0 = max(0, kb - 4 * tile) * 128
            ks = slice(kb * 128, (kb + 1) * 128)
            for rc in range(4):
                k.mm(ps_[:, q0:T], KB.ckvT_bf[:, rc, ks], qlat[:, rc, q0:T], start=(rc == 0), stop=False)
            k.mm(ps_[:, q0:T], KB.kpeT_bf[0:64, ks], qpe[0:64, q0:T], start=False, stop=True)
            return ps_, q0

        nxt = scores(0)
        for kb in range(nkb):
            ps_, q0 = nxt
            if kb + 1 < nkb:
                nxt = scores(kb + 1)
            pt = pT[kb % 2]
            k.act(pt[:, q0:T], ps_[:, q0:T], AF.Exp, scale=SM_SCALE)
            if kb >= 4 * tile:
                k.tt(pt[:, q0:q0 + 128], pt[:, q0:q0 + 128], c.tri[:, :], ALU.mult, e="pool")
            last = (kb == nkb - 1)
            for rc in range(4):
                k.mm(po[rc][:, q0:T], KB.ckv_tm_bf[:, kb, rc * 128:(rc + 1) * 128], pt[:, q0:T], start=(kb == 0), stop=last)
            k.mm(pl[:, q0:T], c.ones_bf[:, :], pt[:, q0:T], start=(kb == 0), stop=last)
        k.recip(linv[:, :], pl[:, 0:T])
        for rc in range(4):
            k.copy(olat[:, rc, :], po[rc][:, 0:T], e="act" if rc % 2 else "dve")
        pv = k.ps()
        for rc in range(4):
            k.mm(pv[:, 0:T], wuv[:, rc, :], olat[:, rc, :], start=(rc == 0), stop=(rc == 3))
        k.tt(aoT[:, h, :], pv[:, 0:T], linv[:, :], ALU.mult)
    k.ps_rot = list(range(8))
    def cons(oc, p):
        resid_add(k, c, xT, oc, p, T, mods[:, 32 + oc, mcol], 1, T)
    linear_fm(k, ws, L.mla_w_o, 0, NH, 0, D, 512, lambda kc: aoT[:, kc, :], T, cons)
    k.release(m)


def final_phase(k, c, xT, T, mods, mcol, nseq, tps, y_out, row0):
    m = k.mark()
    yT = k.sb([128, NKC, T], F32, nsplit=NKC, name="yT")
    k.ts(c.gs[:, :, :], mods[:, 16:32, :], 1.0, ALU.add)
    k.tt(c.gs[:, :, :], c.gs[:, :, :], c.gfin[:, 0:16].un(2).bc([128, 16, 17]), ALU.mult)
    modnorm(k, c, xT, T, lambda kc: c.gs[:, kc, mcol], lambda kc: mods[:, kc, mcol], mcol, nseq, tps, yT)
    transpose_out(k, c, lambda kc: yT[:, kc, :], NKC, T, y_out, row0)
    k.release(m)


INPUT_SPECS = {
    "cc": ([17, 2048], F32), "xp": ([2048, 2048], F32), "xs": ([64, 2048], F32),
    "w_ada0": ([2048, 12288], F32), "b_ada0": ([12288], F32), "w_ada1": ([2048, 12288], F32), "b_ada1": ([12288], F32),
    "g_mix": ([4096], F32), "g_ffn": ([4096], F32),
    "w_gate_up0": ([2048, 11264], F32), "w_gate_up1": ([2048, 11264], F32),
    "w_down0": ([5632, 2048], F32), "w_down1": ([5632, 2048], F32),
    "gdn_w_in": ([2048, 12352], F32), "gdn_w_conv": ([4, 8192], F32), "gdn_a_log": ([32], F32), "gdn_dt_bias": ([32], F32),
    "gdn_g_norm": ([128], F32), "gdn_w_out": ([4096, 2048], F32),
    "kv_w_ada": ([2048, 4096], F32), "kv_b_ada": ([4096], F32), "kv_g_in": ([2048], F32), "kv_w_down": ([2048, 576], F32),
    "kv_g_norm": ([512], F32), "kv_w_uk": ([512, 2048], F32), "kv_w_uv": ([512, 2048], F32),
    "mla_w_dq": ([2048, 512], F32), "mla_g_q": ([512], F32), "mla_w_uq": ([512, 3072], F32), "mla_w_o": ([2048, 2048], F32),
    "final_w_ada": ([2048, 4096], F32), "final_b_ada": ([4096], F32), "final_g": ([2048], F32),
    "rope_cos": ([64, 2116], F32), "rope_sin": ([64, 2116], F32), "tri": ([128, 128], F32),
    "cache_ckv": ([10240 * 128, 512], F32), "cache_kpe": ([10240 * 128, 64], F32), "page_table": ([1, 1024], I32),
    "state_ssm": ([16, 32, 128, 128], F32), "state_conv": ([48, 8192], F32),
    "iota_p": ([128, 1], F32), "cm": ([128, 64], F32),
}
OUTPUT_SPECS = {
    "yp": [2048, 2048], "ys": [64, 2048], "ssm_p": [32, 128, 128], "conv_p": [3, 8192], "ckv_p": [2048, 512], "kpe_p": [2048, 64],
    "ssm_s": [16, 32, 128, 128], "conv_s": [48, 8192], "ckv_s": [64, 512], "kpe_s": [64, 64],
}


def build_all(k, do_prompt=True, do_sample=True, ntiles=4, dbg=None, stop=None, dbg_s=False):
    c = Ctx()
    c.dbg_s = dbg_s
    L = Ctx()
    I = {}
    cst = make_consts()
    for n, v in cst.items():
        I[n] = k.din(n, list(v.shape))

    class LazyIn(dict):
        def __missing__(self, key):
            shp, dt = INPUT_SPECS[key]
            t = k.din(key, shp, dt)
            self[key] = t
            return t
    II = LazyIn(I)
    O = {}

    def out(name):
        if name not in O:
            O[name] = k.dout(name, OUTPUT_SPECS[name])
        return O[name]
    L.gdn_w_in = II["gdn_w_in"]
    L.gdn_w_out = II["gdn_w_out"]
    L.w_gate_up = [II["w_gate_up0"], II["w_gate_up1"]]
    L.w_down = [II["w_down0"], II["w_down1"]]
    L.kv_w_down = II["kv_w_down"]
    L.kv_w_uk = II["kv_w_uk"]
    L.kv_w_uv = II["kv_w_uv"]
    L.mla_w_dq = II["mla_w_dq"]
    L.mla_w_uq = II["mla_w_uq"]
    L.mla_w_o = II["mla_w_o"]
    setup_common(k, c, II)
    setup_gdn(k, c, II)
    c.gkv = c.colvec(II["kv_g_in"][0:D], 16, "gkv")
    c.gkvn = c.colvec(II["kv_g_norm"][0:512], 4, "gkvn")
    c.gq = c.colvec(II["mla_g_q"][0:512], 4, "gq")
    c.gfin = c.colvec(II["final_g"][0:D], 16, "gfin")
    c.tri = load_const(k, II["tri"], [128, 128], BF16, q="pool")
    c.ones_bf = k.sb([128, 128], BF16, name="ones_bf")
    k.copy(c.ones_bf[:, :], c.ones[:, :])
    T = 512
    x1_d = [k.dscratch(f"x1_{t}", [128, NKC * T]) for t in range(4)]
    KT_d = k.dscratch("KT_d", [128, 4 * 2048], BF16)
    KP_d = k.dscratch("KP_d", [64, 2048], BF16)
    KV_d = k.dscratch("KV_d", [128, 16 * 512], BF16)
    xs1_d = k.dscratch("xs1_d", [128, NKC * 64])
    c.ckvT_new = k.sb([128, 4, 64], BF16, nsplit=4, name="ckvT_new")
    c.kpeT_new = k.sb([64, 64], BF16, name="kpeT_new")
    c.ckv_tm_new = k.sb([128, 512], BF16, name="ckv_tm_new")
    m0 = k.mark()
    mods0 = ada_phase(k, c, II["w_ada0"], II["b_ada0"], 12288, "mods0")
    modskv = ada_phase(k, c, II["kv_w_ada"], II["kv_b_ada"], 4096, "modskv")
    if do_prompt:
        mp = k.mark()
        G = load_G(k, II, "p")
        st = Ctx()
        st.S = k.sb([128, 32, 128], F32, nsplit=32, name="S")
        st.convtail = k.sb([128, 64, 1, 3], F32, nsplit=64, name="ctail")
        k.memset(st.S[:, :, :], 0.0)
        k.memset(st.convtail[:, :, :, :], 0.0)
        xT = k.sb([128, NKC, T], F32, nsplit=NKC, name="xT")
        mcol = slice(0, 1)
        for t in range(ntiles):
            transpose_in(k, c, II["xp"], t * T, T, xT)
            gdn_phase(k, c, L, xT, T, 128, mods0, mcol, 1, T, G, st, False)
            ffn_phase(k, c, L, xT, T, mods0, mcol, 1, T, 0)
            k.dma(x1_d[t][:, :], xT[:, :, :].re("p k t -> p (k t)"), store=True)
            mt = k.mark()
            load_rope(k, c, II, t * T, T)
            KBw = Ctx()
            ckvT_st = k.sb([128, 4, T], BF16, nsplit=4, name="ckvT_st")
            kpeT_st = k.sb([64, T], BF16, name="kpeT_st")
            ckvtm_st = k.sb([128, 4, 512], BF16, nsplit=4, name="ckvtm_st")
            KBw.ckvT = lambda rc: ckvT_st[:, rc, :]
            KBw.kpeT = lambda: kpeT_st[:, :]
            KBw.ckv_tm = lambda s: ckvtm_st[:, s, :]
            if stop != "nokv":
                kv_phase(k, c, L, xT, T, modskv, mcol, 1, T, KBw, t * T, out("ckv_p"), out("kpe_p"), t * T)
            for rc in range(4):
                k.dma(KT_d[:, rc * 2048 + t * T: rc * 2048 + (t + 1) * T], ckvT_st[:, rc, :], store=True)
            k.dma(KP_d[:, t * T:(t + 1) * T], kpeT_st[:, :], store=True)
            k.dma(KV_d[:, t * 4 * 512:(t + 1) * 4 * 512], ckvtm_st[:, :, :].re("p s r -> p (s r)"), store=True)
            k.release(mt)
        k.dma(out("ssm_p")[:, :, :].re("h p v -> p h v"), st.S[:, :, :], store=True)
        transpose_out(k, c, lambda cid: st.convtail[:, cid, 0, :], 64, 3, out("conv_p"), 0)
        k.release(mp)
    if do_sample:
        sample_layer0(k, c, L, II, out, mods0, modskv, xs1_d)
    k.release(m0)
    if stop == "l0":
        k.finish()
        return II, O, cst
    mods1 = ada_phase(k, c, II["w_ada1"], II["b_ada1"], 12288, "mods1")
    modsf = ada_phase(k, c, II["final_w_ada"], II["final_b_ada"], 4096, "modsf")
    if do_prompt:
        mp = k.mark()
        KB = Ctx()
        KB.ckvT_bf = k.sb([128, 4, 2048], BF16, nsplit=4, name="ckvT_bf")
        KB.kpeT_bf = k.sb([64, 2048], BF16, name="kpeT_bf")
        KB.ckv_tm_bf = k.sb([128, 16, 512], BF16, nsplit=16, name="ckv_tm_bf")
        k.dma(KB.ckvT_bf[:, :, :].re("p r t -> p (r t)"), KT_d[:, :])
        k.dma(KB.kpeT_bf[:, :], KP_d[:, :])
        k.dma(KB.ckv_tm_bf[:, :, :].re("p s r -> p (s r)"), KV_d[:, :])
        xT = k.sb([128, NKC, T], F32, nsplit=NKC, name="xT")
        for t in range(ntiles):
            k.dma(xT[:, :, :].re("p k t -> p (k t)"), x1_d[t][:, :])
            mt = k.mark()
            load_rope(k, c, II, t * T, T)
            attn_phase_prompt(k, c, L, xT, T, mods1, KB, t)
            k.release(mt)
            if dbg is not None and "xmid1" in dbg:
                transpose_out(k, c, lambda kc: xT[:, kc, :], NKC, T, dbg["xmid1"], t * T)
            ffn_phase(k, c, L, xT, T, mods1, slice(0, 1), 1, T, 1)
            final_phase(k, c, xT, T, modsf, slice(0, 1), 1, T, out("yp"), t * T)
        k.release(mp)
    if do_sample:
        sample_layer1(k, c, L, II, out, mods1, modsf, xs1_d)
    k.finish()
    return II, O, cst


def sample_layer0(k, c, L, II, out, mods0, modskv, xs1_d):
    T, nseq, tps = 64, 16, 4
    mcol = slice(1, 17)
    m = k.mark()
    G = load_G(k, II, "s")
    c.Gs = G
    st = Ctx()
    st.convtail = k.sb([128, 64, nseq, 3], F32, nsplit=64, name="ctail_s")
    st.ssm_in = II["state_ssm"]
    st.ssm_out = out("ssm_s")
    m2 = k.mark()
    stg = [k.sb([128, 2048], F32, name="cstg") for _ in range(2)]
    for pc in range(4):
        sg = stg[pc % 2]
        k.dma(sg[0:48, :], II["state_conv"][:, pc * 2048:(pc + 1) * 2048])
        for g in range(2):
            p = k.ps()
            for j in range(8):
                cl = g * 8 + j
                k.tr(p[:, j * 48:(j + 1) * 48], sg[0:48, cl * 128:(cl + 1) * 128], c.ident[0:48, 0:48])
            cid0 = pc * 16 + g * 8
            k.copy(st.convtail[:, cid0:cid0 + 8, :, :].re("p c s j -> p c (s j)"),
                   p[:, 0:8 * 48].re("p (c x) -> p c x", x=48), e="act" if g else "dve")
    k.release(m2)
    xT = k.sb([128, NKC, T], F32, nsplit=NKC, name="xTs")
    transpose_in(k, c, II["xs"], 0, T, xT)
    gdn_phase(k, c, L, xT, T, 64, mods0, mcol, nseq, tps, G, st, True)
    if getattr(c, "dbg_s", None):
        transpose_out(k, c, lambda kc: xT[:, kc, :], NKC, T, k.dout("d_xmid0", [64, 2048]), 0)
    for pc in range(4):
        transpose_out(k, c, lambda cid: st.convtail[:, pc * 16 + cid, :, :].re("p s j -> p (s j)"), 16, 48,
                      out("conv_s"), 0, col0=pc * 2048)
    ffn_phase(k, c, L, xT, T, mods0, mcol, nseq, tps, 0)
    if getattr(c, "dbg_s", None):
        transpose_out(k, c, lambda kc: xT[:, kc, :], NKC, T, k.dout("d_x0", [64, 2048]), 0)
    k.dma(xs1_d[:, :], xT[:, :, :].re("p k t -> p (k t)"), store=True)
    load_rope(k, c, II, 2052, T)
    KBw = Ctx()
    KBw.ckvT = lambda rc: c.ckvT_new[:, rc, :]
    KBw.kpeT = lambda: c.kpeT_new[:, :]
    KBw.ckv_tm = lambda s: c.ckv_tm_new[0:64, :]
    kv_phase(k, c, L, xT, T, modskv, mcol, nseq, tps, KBw, 0, out("ckv_s"), out("kpe_s"), 0)
    k.release(m)


def sample_layer1(k, c, L, II, out, mods1, modsf, xs1_d):
    T, nseq, tps = 64, 16, 4
    mcol = slice(1, 17)
    m = k.mark()
    xT = k.sb([128, NKC, T], F32, nsplit=NKC, name="xTs")
    k.dma(xT[:, :, :].re("p k t -> p (k t)"), xs1_d[:, :])
    load_rope(k, c, II, 2052, T)
    hT = k.sb([128, NKC, T], BF16, nsplit=NKC, name="hTs")
    k.ts(c.gs[:, :, :], mods1[:, 16:32, :], 1.0, ALU.add)
    k.tt(c.gs[:, :, :], c.gs[:, :, :], c.gmix[:, 16:32].un(2).bc([128, 16, 17]), ALU.mult)
    modnorm(k, c, xT, T, lambda kc: c.gs[:, kc, mcol], lambda kc: mods1[:, kc, mcol], mcol, nseq, tps, hT)
    ws = WS(k, NKC * 512)
    cqT = k.sb([128, 4, T], BF16, nsplit=4, name="cqTs")
    m2 = k.mark()
    cq_compute(k, c, L, ws, hT, T, cqT)
    k.release(m2)
    QLs = k.sb([128, 4, NH, T], BF16, name="QLs")
    QPs = k.sb([64, NH, T], BF16, name="QPs")
    OLs = k.sb([128, 4, NH, T], BF16, name="OLs")
    aoT = k.sb([128, NH, T], BF16, nsplit=NH, name="aoTs")
    Q = Ctx()
    Q.wukf = k.sb([128, 4, 128], F32, name="wukf")
    Q.wukT = k.sb([128, 512], BF16, name="wukT")
    Q.qn = k.sb([128, T], BF16, name="qn")
    Q.rtmp = [k.sb([64, T], F32, name="qrt0"), k.sb([64, T], F32, name="qrt1")]
    Q.qpe_dst = lambda h: QPs[:, h, :]
    Q.qlat_dst = lambda h, rc: QLs[:, rc, h, :]
    for h in range(NH):
        qside_head(k, c, L, ws, h, cqT, T, Q)
    ptb = k.sb([128, 1024], I32, name="ptb")
    k.dma(ptb[:, :], II["page_table"][0:1, :].bc([128, 1024]))
    idxf = k.sb([128, 1024], F32, name="idxf")
    k.copy(idxf[:, :], ptb[:, :])
    iota = load_const(k, II["iota_p"], [128, 1])
    k.ts(idxf[:, :], idxf[:, :], 128.0, ALU.mult, iota[:, 0:1], ALU.add)
    idx = k.sb([128, 1024], I32, name="idx")
    k.copy(idx[:, :], idxf[:, :])
    cm = load_const(k, II["cm"], [128, 64])
    ident_bf = k.sb([128, 128], BF16, name="ident_bf")
    k.copy(ident_bf[:, :], c.ident[:, :])
    Kp = [k.sb([128, 576], BF16, name="Kp") for _ in range(6)]
    KT = [k.sb([128, 5, 128], BF16, name="KT") for _ in range(3)]
    pts = [k.sb([128, 64], BF16, name="pts") for _ in range(3)]
    pe_ = k.sb([64, 64], F32, name="pe_")
    ptm = k.sb([64, 64], BF16, name="ptm")
    linv = k.sb([128, 64], F32, name="linvs")
    ptr_bank = k.psb[7]
    ptr_bf = ptr_bank[:, :].bitcast(BF16)
    po = [k.psb[2 + rc] for rc in range(4)]
    pl = k.psb[6]
    import os
    NPG = int(os.environ.get("NPG", "64"))
    n = 0
    for s in range(nseq):
        qs = slice(s * tps, (s + 1) * tps)
        def qlat_s(rc):
            return QLs[:, rc, :, qs]
        qpe_s = QPs[:, :, qs]
        def stA(pg):
            nn = s * NPG + pg
            col = s * 64 + pg
            kp = Kp[nn % 6]
            k.idma(kp[:, 0:512], II["cache_ckv"][:, :], idx[:, col:col + 1])
            k.idma(kp[:, 512:576], II["cache_kpe"][:, :], idx[:, col:col + 1])

        def stB(pg):
            nn = s * NPG + pg
            kp = Kp[nn % 6]
            kt = KT[nn % 3]
            for rc in range(4):
                k.tr(ptr_bf[:, rc * 128:(rc + 1) * 128], kp[:, rc * 128:(rc + 1) * 128], ident_bf[:, :])
            k.tr(ptr_bf[0:64, 512:640], kp[:, 512:576], ident_bf[:, :])
            k.copy(kt[:, 0:4, :], ptr_bf[:, 0:512].re("p (c t) -> p c t", t=128), e="dve")
            k.copy(kt[0:64, 4, :], ptr_bf[0:64, 512:640], e="dve")

        def stC(pg):
            nn = s * NPG + pg
            kt = KT[nn % 3]
            ps_ = k.psb[nn % 2]
            for rc in range(4):
                k.mm(ps_[:, 0:64], kt[:, rc, :], qlat_s(rc), start=(rc == 0), stop=False)
            k.mm(ps_[:, 0:64], kt[0:64, 4, :], qpe_s, start=False, stop=True)
            k.act(pts[nn % 3][:, :], ps_[:, 0:64], AF.Exp, scale=SM_SCALE)

        def stD(pg):
            nn = s * NPG + pg
            kp = Kp[nn % 6]
            pt = pts[nn % 3]
            for rc in range(4):
                k.mm(po[rc][:, 0:64], kp[:, rc * 128:(rc + 1) * 128], pt[:, :], start=(pg == 0), stop=False)
            k.mm(pl[:, 0:64], c.ones_bf[:, :], pt[:, :], start=(pg == 0), stop=False)

        for it in range(NPG + 3):
            if it < NPG:
                stA(it)
            if 0 <= it - 1 < NPG:
                stB(it - 1)
            if 0 <= it - 2 < NPG:
                stC(it - 2)
            if 0 <= it - 3 < NPG:
                stD(it - 3)
        n = (s + 1) * NPG
        ps_ = k.psb[n % 2]
        n += 1
        for rc in range(4):
            k.mm(ps_[0:64, 0:64], c.ckvT_new[:, rc, :], qlat_s(rc), start=(rc == 0), stop=False)
        k.mm(ps_[0:64, 0:64], c.kpeT_new[0:64, :], qpe_s, start=False, stop=True)
        k.act(pe_[:, :], ps_[0:64, 0:64], AF.Exp, scale=SM_SCALE)
        k.stt(ptm[:, :], pe_[:, :], c.Gs.seqmask[0:64, s:s + 1], cm[0:64, :], ALU.mult, ALU.mult)
        for rc in range(4):
            k.mm(po[rc][:, 0:64], c.ckv_tm_new[0:64, rc * 128:(rc + 1) * 128], ptm[:, :], start=(NPG == 0), stop=True)
        k.mm(pl[:, 0:64], c.ones_bf[0:64, :], ptm[:, :], start=(NPG == 0), stop=True)
        k.recip(linv[:, :], pl[:, 0:64])
        for rc in range(4):
            k.tt(OLs[:, rc, :, qs], po[rc][:, 0:64].re("p (h i) -> p h i", i=tps), linv[:, :].re("p (h i) -> p h i", i=tps),
                 ALU.mult)
    k.ps_rot = [0, 1, 7]
    wuv_t = [k.sb([128, 4, 128], BF16, name="wuvs") for _ in range(2)]
    for h in range(NH):
        wuv = wuv_t[h % 2]
        k.dma(wuv[:, :, :], L.kv_w_uv[:, h * 128:(h + 1) * 128].re("(rc p) v -> p rc v", p=128), q="pool")
        pv = k.ps()
        for rc in range(4):
            k.mm(pv[:, 0:T], wuv[:, rc, :], OLs[:, rc, h, :], start=(rc == 0), stop=(rc == 3))
        k.copy(aoT[:, h, :], pv[:, 0:T], e="act")
    k.ps_rot = list(range(8))
    def cons(oc, p):
        resid_add(k, c, xT, oc, p, T, mods1[:, 32 + oc, mcol], nseq, tps)
    linear_fm(k, ws, L.mla_w_o, 0, NH, 0, D, 512, lambda kc: aoT[:, kc, :], T, cons)
    ffn_phase(k, c, L, xT, T, mods1, mcol, nseq, tps, 1)
    final_phase(k, c, xT, T, modsf, mcol, nseq, tps, out("ys"), 0)
    k.release(m)


def _host_inputs(z, core, cst, needed, shared):
    seq = core % 4
    ins = {n: v for n, v in cst.items() if n in needed}
    sl = slice(core * 16, (core + 1) * 16)
    i = np.arange(128)

    def put(n, fn, share=False):
        if n not in needed:
            return
        if share:
            if n not in shared:
                shared[n] = np.ascontiguousarray(fn())
            ins[n] = shared[n]
        else:
            ins[n] = np.ascontiguousarray(fn())

    def cc():
        a = np.zeros((17, 2048), np.float32)
        a[0] = z["c_prompt"][seq]
        a[1:] = z["c_sample"][sl]
        return a
    put("cc", cc)
    put("xp", lambda: z["x_prompt"][seq])
    put("xs", lambda: z["x_sample"][sl].reshape(64, 2048))
    for l in (0, 1):
        put(f"w_ada{l}", lambda: z["w_ada"][l], True)
        put(f"b_ada{l}", lambda: z["b_ada"][l], True)
        put(f"w_gate_up{l}", lambda: z["w_gate_up"][l], True)
        put(f"w_down{l}", lambda: z["w_down"][l], True)
    put("g_mix", lambda: z["g_mix"].reshape(-1), True)
    put("g_ffn", lambda: z["g_ffn"].reshape(-1), True)
    for n in ("gdn_w_in", "gdn_w_conv", "gdn_a_log", "gdn_dt_bias", "gdn_g_norm", "gdn_w_out", "mla_w_dq", "mla_g_q",
              "mla_w_uq", "mla_w_o"):
        put(n, lambda n=n: z[n][0], True)
    for n in ("kv_w_ada", "kv_b_ada", "kv_g_in", "kv_w_down", "kv_g_norm", "kv_w_uk", "kv_w_uv", "final_w_ada",
              "final_b_ada", "final_g"):
        put(n, lambda n=n: z[n], True)
    if "rope_cos" in needed or "rope_sin" in needed:
        if "rope" not in shared:
            shared["rope"] = rope_tables()
        ins["rope_cos"], ins["rope_sin"] = shared["rope"]
    put("tri", lambda: (i[None, :] >= i[:, None]).astype(np.float32), True)
    put("cache_ckv", lambda: z["cache_ckv"].reshape(-1, 512), True)
    put("cache_kpe", lambda: z["cache_kpe"].reshape(-1, 64), True)
    put("page_table", lambda: z["page_table"][sl].reshape(1, 1024).astype(np.int32))
    put("state_ssm", lambda: z["state_ssm"][0, sl])
    put("state_conv", lambda: z["state_conv"][0, sl].reshape(48, 8192))
    put("iota_p", lambda: i.astype(np.float32).reshape(128, 1), True)
    put("cm", lambda: ((i[:, None] % 4) <= (np.arange(64)[None, :] % 4)).astype(np.float32), True)
    return ins


_PROG = {}


def _get_prog():
    if "k" not in _PROG:
        k, (II, O, cst) = two_pass(lambda kk: build_all(kk))
        _PROG["k"] = k
        _PROG["cst"] = cst
    return _PROG["k"], _PROG["cst"]


def kernel(**inputs):
    from concourse.bass_utils import run_bass_kernel_spmd
    k, cst = _get_prog()
    z = {n: np.asarray(v) for n, v in inputs.items()}
    needed = set(k.dram_in)
    shared = {}
    in_maps = [_host_inputs(z, core, cst, needed, shared) for core in range(8)]
    res = run_bass_kernel_spmd(k.nc, in_maps, core_ids=list(range(8)))
    R = res.results
    f32 = np.float32
    y_prompt = np.stack([R[c]["yp"] for c in range(4)]).astype(f32)
    y_sample = np.concatenate([R[c]["ys"].reshape(16, 4, 2048) for c in range(8)]).astype(f32)
    ssm_prompt = np.stack([R[c]["ssm_p"] for c in range(4)])[None].astype(f32)
    conv_prompt = np.stack([R[c]["conv_p"] for c in range(4)])[None].astype(f32)
    ckv_prompt = np.stack([R[c]["ckv_p"] for c in range(4)]).astype(f32)
    kpe_prompt = np.stack([R[c]["kpe_p"] for c in range(4)]).astype(f32)
    ssm_sample = np.concatenate([R[c]["ssm_s"] for c in range(8)])[None].astype(f32)
    conv_sample = np.concatenate([R[c]["conv_s"].reshape(16, 3, 8192) for c in range(8)])[None].astype(f32)
    ckv_sample = np.concatenate([R[c]["ckv_s"].reshape(16, 4, 512) for c in range(8)]).astype(f32)
    kpe_sample = np.concatenate([R[c]["kpe_s"].reshape(16, 4, 64) for c in range(8)]).astype(f32)
    return (y_prompt, y_sample, ssm_prompt, conv_prompt, ckv_prompt, kpe_prompt,
            ssm_sample, conv_sample, ckv_sample, kpe_sample)
```

```python
import os
import numpy as np
import concourse.bass as bass
import concourse.mybir as mybir

F32 = mybir.dt.float32
BF16 = mybir.dt.bfloat16
I32 = mybir.dt.int32
U32 = mybir.dt.uint32
AF = mybir.ActivationFunctionType
ALU = mybir.AluOpType
AX = mybir.AxisListType
DTSZ = {F32: 4, BF16: 2, I32: 4, U32: 4}


class Buf:
    __slots__ = ("w", "r", "name")

    def __init__(self, name=""):
        self.w = {}
        self.r = {}
        self.name = name


class V:
    __slots__ = ("ap", "bufs")

    def __init__(self, ap, bufs):
        self.ap = ap
        self.bufs = bufs

    def __getitem__(self, key):
        return V(self.ap[key], self.bufs)

    def bc(self, shape):
        return V(self.ap.to_broadcast(list(shape)), self.bufs)

    def un(self, axis):
        return V(self.ap.unsqueeze(axis), self.bufs)

    def re(self, s, **kw):
        return V(self.ap.rearrange(s, **kw), self.bufs)

    def bitcast(self, dt):
        return V(self.ap.bitcast(dt), self.bufs)

    @property
    def shape(self):
        return tuple(self.ap.shape)


class Tens:
    def __init__(self, handle, shape, nsplit=1, name=""):
        self.h = handle
        self.shape = tuple(shape)
        self.nsplit = nsplit
        self.bufs = [Buf(f"{name}.{i}") for i in range(nsplit)]
        self.name = name

    def all(self):
        return V(self.h.ap() if hasattr(self.h, "ap") and callable(getattr(self.h, "ap")) else self.h[:], list(self.bufs))

    def __getitem__(self, key):
        ap = self.h[key]
        if self.nsplit == 1:
            return V(ap, self.bufs)
        k1 = key[1] if isinstance(key, tuple) and len(key) > 1 else slice(None)
        if isinstance(k1, int):
            per = self.shape[1] // self.nsplit
            return V(ap, [self.bufs[k1 // per]])
        if isinstance(k1, slice):
            per = self.shape[1] // self.nsplit
            a = 0 if k1.start is None else k1.start
            b = self.shape[1] if k1.stop is None else k1.stop
            return V(ap, self.bufs[a // per:(b - 1) // per + 1])
        return V(ap, list(self.bufs))


class K:
    def __init__(self, needed=None, same_engine_sync=True):
        import bisect
        self._bisect = bisect
        self.needed = needed
        self.used = {e: set() for e in ("pe", "dve", "act", "pool")}
        self.runid = {e: 0 for e in ("pe", "dve", "act", "pool", "sp")}
        self.run_of = {e: [] for e in ("pe", "dve", "act", "pool")}
        self.nc = bass.Bass("TRN2", target_bir_lowering=False)
        nc = self.nc
        self.E = {"pe": nc.tensor, "dve": nc.vector, "act": nc.scalar, "pool": nc.gpsimd, "sp": nc.sync}
        self.esem = {e: nc.alloc_semaphore(f"es_{e}") for e in ("pe", "dve", "act", "pool")}
        self.ecnt = {e: 0 for e in self.esem}
        self.seen = {e: {} for e in self.E}
        self.same = same_engine_sync
        self.dpool = {}
        for q, n in (("sp", 24), ("pool", 24), ("act", 8)):
            self.dpool[q] = [[nc.alloc_semaphore(f"ds_{q}{i}"), 0] for i in range(n)]
        self.dnext = {q: 0 for q in self.dpool}
        self.stores = []
        self.semname = {}
        self.ninst = 0
        self.nincs = 0
        self.log = []
        self.sem2eng = {id(s): e for e, s in self.esem.items()}
        self.needed_set = {e: set(v) for e, v in needed["reps"].items()} if needed is not None else None
        self.sb_off = 16512
        self.sb_cap = 229376
        self.sb_peak = 0
        self.uid = 0
        self.psb = []
        for i in range(8):
            h = nc.alloc_psum_tensor(f"psb{i}", [128, 512], F32)
            self.psb.append(Tens(h, [128, 512], 1, f"psb{i}"))
        self.psn = 0
        self.ps_rot = list(range(8))
        self.dram_in = {}
        self.dram_out = {}

    def sb(self, shape, dtype=F32, nsplit=1, name=None):
        self.uid += 1
        name = f"{name or 't'}_{self.uid}"
        per = int(np.prod(shape[1:])) * DTSZ[dtype]
        per = (per + 63) // 64 * 64
        off = self.sb_off
        if off + per > self.sb_cap:
            raise RuntimeError(f"SBUF arena overflow allocating {name} {shape}: off={off} per={per}")
        h = self.nc.alloc_sbuf_tensor_at(name, list(shape), dtype, offset=off)
        self.sb_off += per
        self.sb_peak = max(self.sb_peak, self.sb_off)
        return Tens(h, shape, nsplit, name)

    def mark(self):
        return self.sb_off

    def release(self, m):
        self.barrier()
        self.sb_off = m

    def ps(self):
        t = self.psb[self.ps_rot[self.psn % len(self.ps_rot)]]
        self.psn += 1
        return t

    def din(self, name, shape, dtype=F32):
        h = self.nc.dram_tensor(name, list(shape), dtype, kind="ExternalInput")
        self.dram_in[name] = (tuple(shape), dtype)
        return Tens(h, shape, 1, name)

    def dout(self, name, shape, dtype=F32):
        h = self.nc.dram_tensor(name, list(shape), dtype, kind="ExternalOutput")
        self.dram_out[name] = (tuple(shape), dtype)
        return Tens(h, shape, 1, name)

    def dscratch(self, name, shape, dtype=F32, nsplit=1):
        h = self.nc.dram_tensor(name, list(shape), dtype, kind="Internal")
        return Tens(h, shape, nsplit, name)

    def _deps(self, reads, writes):
        d = {}
        for v in reads:
            if isinstance(v, V):
                for b in v.bufs:
                    for s, val in b.w.items():
                        if d.get(s, 0) < val:
                            d[s] = val
        for v in writes:
            for b in v.bufs:
                for dd in (b.w, b.r):
                    for s, val in dd.items():
                        if d.get(s, 0) < val:
                            d[s] = val
        return d

    def _wait(self, e, deps, skip_own=False):
        eng = self.E[e]
        seen = self.seen[e]
        own = self.esem.get(e)
        for s, val in deps.items():
            if s is own and (skip_own or not self.same):
                continue
            if seen.get(s, 0) < val:
                sv = self._semval(s, val)
                eng.wait_ge(s, sv)
                seen[s] = val
                self.runid[e] += 1
                self.log.append((e, "wait", s, sv))

    def _semval(self, s, val):
        en = self.sem2eng.get(id(s))
        if en is None:
            return val
        self.used[en].add(val)
        if self.needed is None:
            return val
        rep = self.needed["rep"][en][val]
        lst = self.needed["reps"][en]
        i = self._bisect.bisect_left(lst, rep)
        assert i < len(lst) and lst[i] == rep, (en, val)
        return i + 1

    def _record(self, reads, writes, ev):
        s, val = ev
        for v in reads:
            if isinstance(v, V):
                for b in v.bufs:
                    if b.r.get(s, 0) < val:
                        b.r[s] = val
        for v in writes:
            for b in v.bufs:
                b.w = {s: val}
                b.r = {}

    def op(self, e, fn, reads, writes, pe_acc=False):
        deps = self._deps(reads, writes)
        if pe_acc:
            self._wait(e, deps, skip_own=True)
        else:
            self._wait(e, deps, skip_own=(e == "pe"))
        ins = fn(self.E[e])
        self.ecnt[e] += 1
        self.run_of[e].append(self.runid[e])
        if self.needed is None or self.ecnt[e] in self.needed_set[e]:
            ins.then_inc(self.esem[e], 1)
            self.log.append((e, "inc", self.esem[e], 1))
            self.nincs += 1
        ev = (self.esem[e], self.ecnt[e])
        self._record(reads, writes, ev)
        self.ninst += 1
        return ev

    def dma(self, out, in_, q="sp", store=False, **kw):
        deps = self._deps([in_], [out])
        pool = self.dpool[q]
        j = self.dnext[q] % len(pool)
        self.dnext[q] += 1
        sem, cnt = pool[j]
        if cnt > 0:
            deps[sem] = max(deps.get(sem, 0), cnt)
        self._wait(q, deps)
        ins = self.E[q].dma_start(out=out.ap, in_=in_.ap, **kw)
        ins.then_inc(sem, 16)
        self.log.append((q, "inc", sem, 16))
        pool[j][1] = cnt + 16
        ev = (sem, cnt + 16)
        self._record([in_], [out], ev)
        if store:
            self.stores.append(ev)
        self.ninst += 1
        return ev

    def idma(self, out, in_full, idx, q="pool"):
        deps = self._deps([idx], [out])
        pool = self.dpool[q]
        j = self.dnext[q] % len(pool)
        self.dnext[q] += 1
        sem, cnt = pool[j]
        if cnt > 0:
            deps[sem] = max(deps.get(sem, 0), cnt)
        self._wait(q, deps)
        ins = self.E[q].indirect_dma_start(out=out.ap, out_offset=None, in_=in_full.ap,
                                           in_offset=bass.IndirectOffsetOnAxis(ap=idx.ap, axis=0))
        ins.then_inc(sem, 16)
        self.log.append((q, "inc", sem, 16))
        pool[j][1] = cnt + 16
        ev = (sem, cnt + 16)
        self._record([idx], [out], ev)
        self.ninst += 1
        return ev

    def barrier(self):
        deps = {self.esem[e]: self.ecnt[e] for e in self.esem if self.ecnt[e] > 0}
        for s, val in self.stores:
            deps[s] = max(deps.get(s, 0), val)
        for q in self.dpool:
            for sem, cnt in self.dpool[q]:
                if cnt > 0:
                    deps[sem] = max(deps.get(sem, 0), cnt)
        self.stores = []
        for e in ("pe", "dve", "act", "pool", "sp"):
            eng = self.E[e]
            seen = self.seen[e]
            for s, val in deps.items():
                if seen.get(s, 0) < val:
                    sv = self._semval(s, val)
                    eng.wait_ge(s, sv)
                    seen[s] = val
                    self.runid[e] += 1
                    self.log.append((e, "wait", s, sv))

    def finish(self):
        self.barrier()

    def mm(self, out, lhsT, rhs, start=True, stop=True):
        return self.op("pe", lambda e: e.matmul(out.ap, lhsT.ap, rhs.ap, start=start, stop=stop),
                       [lhsT, rhs], [out], pe_acc=not start)

    def tr(self, out, in_, ident):
        return self.op("pe", lambda e: e.transpose(out.ap, in_.ap, ident.ap), [in_, ident], [out])

    def act(self, out, in_, func, bias=0.0, scale=1.0, accum=None, e="act"):
        reads = [in_] + [x for x in (bias, scale) if isinstance(x, V)]
        writes = [out] + ([accum] if accum is not None else [])
        b = bias.ap if isinstance(bias, V) else bias
        s = scale.ap if isinstance(scale, V) else scale
        kw = {}
        if accum is not None:
            kw["accum_out"] = accum.ap
        return self.op(e, lambda g: g.activation(out=out.ap, in_=in_.ap, func=func, bias=b, scale=s, **kw),
                       reads, writes)

    def tt(self, out, a, b, op, e="dve"):
        return self.op(e, lambda g: g.tensor_tensor(out.ap, a.ap, b.ap, op), [a, b], [out])

    def ts(self, out, a, s1, op0, s2=None, op1=None, e="dve", accum=None):
        reads = [a] + [x for x in (s1, s2) if isinstance(x, V)]
        x1 = s1.ap if isinstance(s1, V) else s1
        x2 = s2.ap if isinstance(s2, V) else s2
        kw = {}
        if op1 is not None:
            kw["op1"] = op1
        writes = [out]
        if accum is not None:
            kw["accum_out"] = accum.ap
            writes.append(accum)
        return self.op(e, lambda g: g.tensor_scalar(out.ap, a.ap, x1, x2, op0, **kw), reads, writes)

    def stt(self, out, in0, scalar, in1, op0, op1, e="dve"):
        reads = [in0, in1] + ([scalar] if isinstance(scalar, V) else [])
        sc = scalar.ap if isinstance(scalar, V) else scalar
        return self.op("dve", lambda g: g.scalar_tensor_tensor(out.ap, in0.ap, sc, in1.ap, op0, op1), reads, [out])

    def copy(self, out, in_, e="dve"):
        if e == "act":
            return self.op(e, lambda g: g.copy(out.ap, in_.ap), [in_], [out])
        return self.op(e, lambda g: g.tensor_copy(out.ap, in_.ap), [in_], [out])

    def memset(self, out, val, e="dve"):
        return self.op(e, lambda g: g.memset(out.ap, val), [], [out])

    def reduce(self, out, in_, op=ALU.add, axis=AX.X, e="dve"):
        return self.op(e, lambda g: g.tensor_reduce(out.ap, in_.ap, axis, op), [in_], [out])

    def recip(self, out, in_):
        return self.op("dve", lambda g: g.reciprocal(out.ap, in_.ap), [in_], [out])


def simulate_log(log):
    streams = {}
    for it in log:
        streams.setdefault(it[0], []).append(it)
    pos = {e: 0 for e in streams}
    sem = {}
    progress = True
    while progress:
        progress = False
        for e, st in streams.items():
            while pos[e] < len(st):
                _, kind, s, val = st[pos[e]]
                if kind == "wait":
                    if sem.get(id(s), 0) >= val:
                        pos[e] += 1
                        progress = True
                    else:
                        break
                else:
                    sem[id(s)] = sem.get(id(s), 0) + val
                    pos[e] += 1
                    progress = True
    stuck = {e: (pos[e], len(st)) for e, st in streams.items() if pos[e] < len(st)}
    return stuck


def two_pass(build_fn):
    k1 = K()
    build_fn(k1)
    rep, reps = {}, {}
    for e, v in k1.used.items():
        ro = k1.run_of[e]
        last = {}
        for idx in sorted(v):
            last[ro[idx - 1]] = idx
        rep[e] = {idx: last[ro[idx - 1]] for idx in v}
        reps[e] = sorted(set(rep[e].values()))
    k2 = K(needed={"rep": rep, "reps": reps})
    r = build_fn(k2)
    return k2, r

D = 2048
NKC = 16
DFF = 5632
EPS = 1e-6
NHV = 32
NHK = 16
CONV_DIM_ = 8192
GIN = 12352
QL = 512
KVL = 512
ROPE = 64
NH = 16
SM_SCALE = (128 + 64) ** -0.5
NEG = -30000.0


class Ctx:
    pass


def load_const(k, dram, shape, dtype=F32, q="sp"):
    t = k.sb(list(shape), dtype)
    k.dma(t[:], dram[:], q=q)
    return t


class WS:
    def __init__(self, k, nelem):
        self.k = k
        self.n = nelem
        self.b = [k.sb([128, nelem], BF16, name="wbuf") for _ in range(2)]
        self.i = 0

    def get(self, nk, ncols):
        t = self.b[self.i % 2]
        self.i += 1
        assert nk * ncols <= self.n, (nk, ncols, self.n)
        return t[:, 0:nk * ncols].re("p (k n) -> p k n", n=ncols)

    def load(self, wd, r0, nk, colspecs, rows=128):
        tot = sum(n for _, n in colspecs)
        v = self.get(nk, tot)
        o = 0
        for c0, n in colspecs:
            src = wd[r0:r0 + nk * rows, c0:c0 + n].re("(k p) n -> p k n", p=rows)
            self.k.dma(v[0:rows, :, o:o + n], src, q="pool")
            o += n
        return v


def transpose_in(k, c, xd, row0, T, xT):
    ts_ = min(128, T)
    nsub = T // ts_
    m = k.mark()
    stg = [k.sb([128, D], F32, name="xstg") for _ in range(2)]
    for s in range(nsub):
        st = stg[s % 2]
        k.dma(st[0:ts_, :], xd[row0 + s * ts_: row0 + (s + 1) * ts_, :])
        for g in range(4):
            p = k.ps()
            for j in range(4):
                kc = g * 4 + j
                k.tr(p[:, j * ts_:(j + 1) * ts_], st[0:ts_, kc * 128:(kc + 1) * 128], c.ident[0:ts_, 0:ts_])
            src = p[:, 0:4 * ts_].re("p (j t) -> p j t", t=ts_)
            dst = xT[:, g * 4:(g + 1) * 4, s * ts_:(s + 1) * ts_]
            k.copy(dst, src, e="act" if g % 2 else "dve")
    k.release(m)


def transpose_out(k, c, srcfn, nkc, T, dd, row0, col0=0, rows=128):
    ts_ = min(128, T)
    nsub = T // ts_
    m = k.mark()
    stg = [k.sb([128, nkc * rows], F32, name="ostg") for _ in range(2)]
    for s in range(nsub):
        st = stg[s % 2]
        for g0 in range(0, nkc, 4):
            ng = min(4, nkc - g0)
            p = k.ps()
            for j in range(ng):
                src = srcfn(g0 + j)[:, s * ts_:(s + 1) * ts_]
                k.tr(p[0:ts_, j * rows:(j + 1) * rows], src, c.ident[0:rows, 0:rows])
            k.copy(st[0:ts_, g0 * rows:(g0 + ng) * rows], p[0:ts_, 0:ng * rows], e="act" if (g0 // 4) % 2 else "dve")
        k.dma(dd[row0 + s * ts_: row0 + (s + 1) * ts_, col0:col0 + nkc * rows], st[0:ts_, :], store=True)
    k.release(m)


def rstd_from_psum(k, out, ps, n, eps=EPS):
    k.ts(out, ps, 1.0 / n, ALU.mult, eps, ALU.add)
    k.act(out, out, AF.Sqrt)
    k.recip(out, out)


def bc3(v, nseq, tps):
    return v.un(2).bc([128, nseq, tps])


def v3(v, tps):
    return v.re("p (s t) -> p s t", t=tps)


def modnorm(k, c, xT, T, gs, sh, mcol, nseq, tps, hT):
    m = k.mark()
    sq = [k.sb([128, T], F32, name="sq") for _ in range(2)]
    pss = k.ps()
    for kc in range(NKC):
        s_ = sq[kc % 2]
        k.act(s_[:], xT[:, kc, :], AF.Square)
        k.mm(pss[:, 0:T], c.ones[:], s_[:], start=(kc == 0), stop=(kc == NKC - 1))
    rstd = k.sb([128, T], F32, name="rstd")
    rstd_from_psum(k, rstd[:], pss[:, 0:T], D)
    tmp = [k.sb([128, T], F32, name="mtmp") for _ in range(2)]
    for kc in range(NKC):
        t_ = tmp[kc % 2]
        e1 = "dve" if kc % 2 == 0 else "pool"
        k.tt(t_[:], xT[:, kc, :], rstd[:], ALU.mult, e=e1)
        k.tt(v3(t_[:], tps), v3(t_[:], tps), bc3(gs(kc), nseq, tps), ALU.mult, e=e1)
        k.tt(v3(hT[:, kc, :], tps), v3(t_[:], tps), bc3(sh(kc), nseq, tps), ALU.add, e=e1)
    k.release(m)


def ada_phase(k, c, wd, bd, N, name):
    nch = N // 128
    modsT = k.sb([128, nch, 17], F32, name=name)
    m = k.mark()
    bT = k.sb([128, nch], F32, name="bT")
    brow = k.sb([128, 128], F32, name="brow")
    k.dma(brow[0:nch, :], bd[:].re("(c p) -> c p", p=128))
    p = k.ps()
    k.tr(p[:, 0:nch], brow[0:nch, :], c.ident[0:nch, 0:nch])
    k.copy(bT[:], p[:, 0:nch])
    ws = WS(k, NKC * 512)
    nblk = N // 512
    nxt = ws.load(wd, 0, NKC, [(0, 512)])
    for b in range(nblk):
        w = nxt
        if b + 1 < nblk:
            nxt = ws.load(wd, 0, NKC, [((b + 1) * 512, 512)])
        p = k.ps()
        for sub in range(4):
            for kc in range(NKC):
                k.mm(p[:, sub * 17:(sub + 1) * 17], w[:, kc, sub * 128:(sub + 1) * 128], c.scT[:, kc, :],
                     start=(kc == 0), stop=(kc == NKC - 1))
        src = p[:, 0:4 * 17].re("p (j t) -> p j t", t=17)
        k.tt(modsT[:, b * 4:(b + 1) * 4, :], src, bT[:, b * 4:(b + 1) * 4].un(2).bc([128, 4, 17]), ALU.add)
    k.release(m)
    return modsT


def linear_fm(k, ws, wd, r0, nk, c0, ncols, blk, actfn, T, consume):
    nblk = (ncols + blk - 1) // blk
    def spec(b):
        return [(c0 + b * blk, min(blk, ncols - b * blk))]
    nxt = ws.load(wd, r0, nk, spec(0))
    for b in range(nblk):
        w = nxt
        if b + 1 < nblk:
            nxt = ws.load(wd, r0, nk, spec(b + 1))
        nc_ = min(blk, ncols - b * blk)
        for j in range(0, nc_, 128):
            mcols = min(128, nc_ - j)
            p = k.ps()
            for kc in range(nk):
                k.mm(p[0:mcols, 0:T], w[:, kc, j:j + mcols], actfn(kc), start=(kc == 0), stop=(kc == nk - 1))
            consume((b * blk + j) // 128, p)


def ffn_phase(k, c, L, xT, T, mods, mcol, nseq, tps, l):
    m = k.mark()
    hT = k.sb([128, NKC, T], BF16, nsplit=NKC, name="hT")
    k.ts(c.gs[:, :, :], mods[:, 4 * 16:5 * 16, :], 1.0, ALU.add)
    k.tt(c.gs[:, :, :], c.gs[:, :, :], c.gffn[:, l * 16:(l + 1) * 16].un(2).bc([128, 16, 17]), ALU.mult)
    modnorm(k, c, xT, T, lambda kc: c.gs[:, kc, mcol], lambda kc: mods[:, 3 * 16 + kc, mcol], mcol, nseq, tps, hT)
    ws = WS(k, NKC * 512)
    NQ = 2
    CQ = DFF // NQ
    nq = CQ // 128
    actT = k.sb([128, nq, T], BF16, nsplit=nq, name="actT")
    gsil = [k.sb([128, T], F32, name="gsil") for _ in range(2)]
    wgu = L.w_gate_up[l]
    wdn = L.w_down[l]
    for qd in range(NQ):
        nb = CQ // 256
        def ld(bi):
            col = qd * CQ + bi * 256
            return ws.load(wgu, 0, NKC, [(col, 256), (DFF + col, 256)])
        nxt = ld(0)
        for bi in range(nb):
            w = nxt
            if bi + 1 < nb:
                nxt = ld(bi + 1)
            for jj in range(2):
                j = bi * 2 + jj
                pg = k.ps()
                pu = k.ps()
                for kc in range(NKC):
                    k.mm(pg[:, 0:T], w[:, kc, jj * 128:(jj + 1) * 128], hT[:, kc, :], start=(kc == 0), stop=(kc == NKC - 1))
                for kc in range(NKC):
                    k.mm(pu[:, 0:T], w[:, kc, 256 + jj * 128:256 + (jj + 1) * 128], hT[:, kc, :], start=(kc == 0), stop=(kc == NKC - 1))
                g_ = gsil[j % 2]
                k.act(g_[:], pg[:, 0:T], AF.Silu)
                k.tt(actT[:, j, :], g_[:], pu[:, 0:T], ALU.mult)
        def cons(oc, p):
            resid_add(k, c, xT, oc, p, T, mods[:, 5 * 16 + oc, mcol], nseq, tps)
        linear_fm(k, ws, wdn, qd * CQ, nq, 0, D, 256, lambda kc: actT[:, kc, :], T, cons)
    k.release(m)


def resid_add(k, c, xT, oc, p, T, gate, nseq, tps):
    if nseq == 1:
        k.stt(xT[:, oc, :], p[:, 0:T], gate, xT[:, oc, :], ALU.mult, ALU.add)
    else:
        t_ = c.rtmp[c.rti % 2]
        c.rti += 1
        k.tt(v3(t_[:, 0:T], tps), v3(p[:, 0:T], tps), bc3(gate, nseq, tps), ALU.mult)
        k.tt(xT[:, oc, :], xT[:, oc, :], t_[:, 0:T], ALU.add, e="pool")


def gdn_chunks_batched(k, c, G, st, kh, B, kTb, qTb, kn, xc, zs, beta, negg, gcs, edl, bg, ogT):
    NCH = 4
    nlev = getattr(c, "nlev", 6)
    gstop = int(os.environ.get("GSTOP", "99"))
    def c3(v):
        return v.re("p (c j) -> p c j", j=128)
    def cs_(ch):
        return slice(ch * 128, (ch + 1) * 128)
    k.ps_rot = [3, 4, 5, 6, 7]
    pG, pQK, pK = k.psb[0], k.psb[1], k.psb[2]
    for ch in range(NCH):
        cs = cs_(ch)
        k.mm(pG[:, cs], kTb[:, cs], kTb[:, cs])
        k.mm(pQK[:, cs], qTb[:, cs], kTb[:, cs])
        k.tr(pK[:, cs], kn[:, cs], c.ident[:, :])
    X1, X2, R = B.X1, B.X2, B.R
    Ub = G.U[:, :].un(1).bc([128, NCH, 128])
    Mnb = G.Mn[:, :].un(1).bc([128, NCH, 128])
    SLb = G.SL[:, :].un(1).bc([128, NCH, 128])
    Ib = c.ident[:, :].un(1).bc([128, NCH, 128])
    for a in range(2):
        hh = 2 * kh + a
        def bcol(t):
            return t[:, :, hh:hh + 1].bc([128, NCH, 128])
        pV = k.ps()
        for ch in range(NCH):
            k.tr(pV[:, cs_(ch)], xc[2 + a][:, cs_(ch)], c.ident[:, :])
        k.tt(c3(B.vb[:, :]), c3(pV[:, :]), bcol(beta), ALU.mult)
        k.tt(c3(B.kbg[:, :]), c3(pK[:, :]), bcol(bg), ALU.mult)
        k.tt(c3(B.kd[:, :]), c3(pK[:, :]), bcol(edl), ALU.mult)
        if gstop <= 1:
            continue
        k.tt(c3(X1[:, :]), Ub, bcol(negg), ALU.mult, e="pool")
        pE = k.ps()
        k.mm(pE[:, :], c.ones[:, :], X1[:, :])
        k.tt(c3(X2[:, :]), c3(pE[:, :]), Mnb, ALU.add)
        k.tt(c3(X2[:, :]), c3(X2[:, :]), bcol(gcs), ALU.add, e="pool")
        k.act(X2[:, :], X2[:, :], AF.Exp)
        k.act(X1[:, :], pE[:, :], AF.Exp, scale=-1.0)
        if gstop <= 2:
            continue
        k.tt(B.oT[:, :], pG[:, :], X2[:, :], ALU.mult)
        if gstop == 21:
            continue
        k.tt(c3(B.oT[:, :]), c3(B.oT[:, :]), SLb, ALU.mult, e="pool")
        if gstop == 22:
            continue
        A0f = B.Af[0]
        k.tt(c3(A0f[:, :]), c3(B.oT[:, :]), bcol(beta), ALU.mult)
        k.tt(B.qkb[:, :], pQK[:, :], X2[:, :], ALU.mult)
        pBt = k.ps()
        for ch in range(NCH):
            k.tr(pBt[:, cs_(ch)], A0f[:, cs_(ch)], c.ident[:, :])
        ptr = k.ps()
        ptrb = ptr[:, :].bitcast(BF16)
        for ch in range(NCH):
            k.tr(ptrb[:, ch * 128:(ch + 1) * 128], B.qkb[:, cs_(ch)], B.identbf[:, :])
        k.copy(B.Bf[0][:, :], pBt[:, :], e="act")
        k.copy(B.qkT[:, :], ptrb[:, 0:512], e="dve")
        k.tt(c3(R[0][:, :]), Ib, c3(B.Bf[0][:, :]), ALU.subtract, e="pool")
        Ac, Bc, Rc, ri = A0f, B.Bf[0], R[0], 0
        for l in range(nlev):
            An = B.Af[(l + 1) % 2]
            Bn = B.Bf[(l + 1) % 2]
            pA = k.ps()
            for ch in range(NCH):
                k.mm(pA[:, cs_(ch)], Bc[:, cs_(ch)], Ac[:, cs_(ch)])
            if l < nlev - 1:
                pB = k.ps()
                for ch in range(NCH):
                    k.mm(pB[:, cs_(ch)], Ac[:, cs_(ch)], Bc[:, cs_(ch)])
            k.copy(An[:, :], pA[:, :], e="act")
            if l < nlev - 1:
                k.copy(Bn[:, :], pB[:, :], e="dve")
            pR = k.ps()
            for ch in range(NCH):
                k.mm(pR[:, cs_(ch)], An[:, cs_(ch)], Rc[:, cs_(ch)])
            Rn = R[(ri + 1) % 2]
            ri += 1
            k.tt(Rn[:, :], pR[:, :], Rc[:, :], ALU.add)
            Rc = Rn
            Ac, Bc = An, Bn
        k.copy(B.Rb[:, :], Rc[:, :], e="pool")
        if gstop <= 5:
            continue
        pw = k.ps()
        for ch in range(NCH):
            k.mm(pw[:, cs_(ch)], B.kbg[:, cs_(ch)], B.Rb[:, cs_(ch)])
        k.act(B.wTn[:, :], pw[:, :], AF.Copy, scale=-1.0)
        k.tt(B.qg[:, :], qTb[:, :], X1[:, :], ALU.mult, e="pool")
        for ch in range(NCH):
            cs = cs_(ch)
            Sbt = B.Sbh[ch % 2]
            k.copy(Sbt[:, :], st.S[:, hh, :], e="pool")
            pv = k.ps()
            k.mm(pv[:, 0:128], B.Rb[:, cs], B.vb[:, cs], start=True, stop=False)
            k.mm(pv[:, 0:128], B.wTn[:, cs], Sbt[:, :], start=False, stop=True)
            vn = B.vn[ch % 2]
            k.copy(vn[:, :], pv[:, 0:128], e="act")
            po = k.ps()
            k.mm(po[:, 0:128], Sbt[:, :], B.qg[:, cs], start=True, stop=False)
            k.mm(po[:, 0:128], vn[:, :], B.qkT[:, cs], start=False, stop=True)
            k.copy(B.oT[:, cs], po[:, 0:128], e="act")
            pS = k.ps()
            k.mm(pS[:, 0:128], B.kd[:, cs], vn[:, :])
            k.stt(st.S[:, hh, :], st.S[:, hh, :], X1[:, ch * 128 + 127:ch * 128 + 128], pS[:, 0:128], ALU.mult, ALU.add)
        if gstop <= 6:
            continue
        k.tt(X2[:, :], B.oT[:, :], B.oT[:, :], ALU.mult, e="pool")
        pq = k.ps()
        k.mm(pq[:, :], c.ones[:, :], X2[:, :])
        orn = R[0]
        rstd_from_psum(k, orn[:, :], pq[:, :], 128)
        k.tt(B.oT[:, :], B.oT[:, :], orn[:, :], ALU.mult)
        k.stt(ogT[:, hh, :], B.oT[:, :], c.gnorm[:, 0:1], zs[a][:, :], ALU.mult, ALU.mult)
    k.ps_rot = [2, 3, 4, 5, 6, 7]


def gdn_phase(k, c, L, xT, T, C, mods, mcol, nseq, tps, G, st, samp):
    nch = T // C
    nlev = 1 if samp else getattr(c, 'nlev', 6)
    m = k.mark()
    k.ps_rot = [2, 3, 4, 5, 6, 7]
    hT = k.sb([128, NKC, T], BF16, nsplit=NKC, name="hT")
    k.ts(c.gs[:, :, :], mods[:, 16:32, :], 1.0, ALU.add)
    k.tt(c.gs[:, :, :], c.gs[:, :, :], c.gmix[:, 0:16].un(2).bc([128, 16, 17]), ALU.mult)
    modnorm(k, c, xT, T, lambda kc: c.gs[:, kc, mcol], lambda kc: mods[:, kc, mcol], mcol, nseq, tps, hT)
    ogT = k.sb([128, NHV, T], BF16, nsplit=NHV, name="ogT")
    ws = WS(k, NKC * 512)
    wba = ws.load(L.gdn_w_in, 0, NKC, [(12288, 64)])
    beta = k.sb([128, nch, 32], F32, nsplit=nch, name="beta")
    gg = k.sb([128, nch, 32], F32, nsplit=nch, name="gg")
    negg = k.sb([128, nch, 32], F32, nsplit=nch, name="negg")
    gcs = k.sb([128, nch, 32], F32, nsplit=nch, name="gcs")
    edl = k.sb([128, nch, 32], F32, nsplit=nch, name="edl")
    bg = k.sb([128, nch, 32], F32, nsplit=nch, name="bg")
    t1 = k.sb([128, 32], F32, name="t1")
    t2 = k.sb([128, 32], F32, name="t2")
    for ch in range(nch):
        p = k.ps()
        for kc in range(NKC):
            k.mm(p[0:C, 0:64], hT[:, kc, ch * C:(ch + 1) * C], wba[:, kc, 0:64], start=(kc == 0), stop=(kc == NKC - 1))
        k.act(beta[0:C, ch, :], p[0:C, 0:32], AF.Sigmoid)
        k.tt(t1[0:C, :], p[0:C, 32:64], c.dtb[0:C, :], ALU.add)
        k.act(t2[0:C, :], t1[0:C, :], AF.Abs)
        k.act(t2[0:C, :], t2[0:C, :], AF.Exp, scale=-1.0)
        k.act(t2[0:C, :], t2[0:C, :], AF.Ln, bias=1.0)
        k.ts(t1[0:C, :], t1[0:C, :], 0.0, ALU.max)
        k.tt(t1[0:C, :], t1[0:C, :], t2[0:C, :], ALU.add)
        k.tt(gg[0:C, ch, :], t1[0:C, :], c.nega[0:C, :], ALU.mult)
        k.ts(negg[0:C, ch, :], gg[0:C, ch, :], -1.0, ALU.mult)
        p2 = k.ps()
        k.mm(p2[0:C, 0:32], G.U[0:C, 0:C], gg[0:C, ch, :])
        k.mm(p2[0:C, 32:64], G.OB[0:C, 0:C], gg[0:C, ch, :])
        k.copy(gcs[0:C, ch, :], p2[0:C, 0:32])
        k.tt(t1[0:C, :], p2[0:C, 32:64], gcs[0:C, ch, :], ALU.subtract)
        k.act(edl[0:C, ch, :], t1[0:C, :], AF.Exp)
        k.act(t2[0:C, :], gcs[0:C, ch, :], AF.Exp)
        k.tt(bg[0:C, ch, :], t2[0:C, :], beta[0:C, ch, :], ALU.mult)
    if getattr(c, "stage", 99) <= 2:
        c.dbg = [("beta", beta[:, :, :]), ("gg", gg[:, :, :]), ("gcs", gcs[:, :, :]), ("edl", edl[:, :, :]), ("hT0", None)]
        return
    cb = [k.sb([128, nseq, 3 + tps], F32, name="cb") for _ in range(2)]
    xc = [k.sb([128, T], F32, name="xc") for _ in range(4)]
    qTb = k.sb([128, T], BF16, name="qTb")
    kTb = k.sb([128, T], BF16, name="kTb")
    kn = k.sb([128, T], F32, name="kn")
    zs = [k.sb([128, T], BF16, name="zs") for _ in range(2)]
    sqt = k.sb([128, T], F32, name="sqt")
    rn = k.sb([128, T], F32, name="rn")
    def f32t(n="ct"):
        return k.sb([128, 128], F32, name=n)
    if not samp:
        Bt = Ctx()
        def b16(n, w=512):
            return k.sb([128, w], BF16, name=n)
        Bt.qkb = b16("qkb")
        Bt.Af = [k.sb([128, 512], F32, name="Af0"), k.sb([128, 512], F32, name="Af1")]
        Bt.Bf = [k.sb([128, 512], F32, name="Bf0"), k.sb([128, 512], F32, name="Bf1")]
        Bt.Rb = b16("Rb"); Bt.qkT = b16("qkT"); Bt.vb = b16("vb"); Bt.kbg = b16("kbg"); Bt.kd = b16("kd")
        Bt.wTn = b16("wTn"); Bt.qg = b16("qg")
        Bt.vn = [b16("vn0", 128), b16("vn1", 128)]; Bt.Sbh = [b16("Sbh0", 128), b16("Sbh1", 128)]
        Bt.oT = k.sb([128, 512], F32, name="oTall")
        Bt.identbf = b16("identbf", 128)
        k.copy(Bt.identbf[:, :], c.ident[:, :])
        Bt.X1 = xc[0]; Bt.X2 = xc[1]; Bt.R = [sqt, rn]
    else:
        rh = [f32t("rh") for _ in range(2)]
        tmpE = [f32t("tmpE") for _ in range(2)]
        E1 = [f32t("E1") for _ in range(2)]
        egrow = [f32t("egrow") for _ in range(2)]
        Am = [f32t("Am") for _ in range(4)]
        Bm = [f32t("Bm") for _ in range(4)]
        qk = [f32t("qk") for _ in range(2)]
        Rr = [f32t("Rr") for _ in range(4)]
        qkT = [k.sb([128, 128], BF16, name="qkT") for _ in range(2)]
        Rb = [k.sb([128, 128], BF16, name="Rb") for _ in range(2)]
        vb = [k.sb([128, 128], BF16, name="vb") for _ in range(2)]
        kbg = [k.sb([128, 128], BF16, name="kbg") for _ in range(2)]
        kd = [k.sb([128, 128], BF16, name="kd") for _ in range(2)]
        wTn = [k.sb([128, 128], BF16, name="wTn") for _ in range(2)]
        vn = [k.sb([128, 128], BF16, name="vn") for _ in range(2)]
        qg = [k.sb([128, 128], BF16, name="qg") for _ in range(2)]
        Sbh = [k.sb([128, 128], BF16, name="Sbh") for _ in range(2)]
        osb = [f32t("osb") for _ in range(2)]
        osq = [f32t("osq") for _ in range(2)]
        orn = [f32t("orn") for _ in range(2)]
        if samp:
            vnT = [f32t("vnT") for _ in range(2)]
            kds = [k.sb([128, 128], BF16, name="kds") for _ in range(2)]
            Sall = [k.sb([128, 16, 128], F32, name="Sall") for _ in range(2)]
            Sball = [k.sb([128, 16, 128], BF16, name="Sball") for _ in range(2)]
    cnt = 0
    def _ldw(kh_):
        w_ = ws.load(L.gdn_w_in, 0, NKC, [(kh_ * 128, 128), (2048 + kh_ * 128, 128), (4096 + kh_ * 256, 256)])
        wz_ = ws.load(L.gdn_w_in, 0, NKC, [(8192 + kh_ * 256, 256)])
        return w_, wz_
    nxt_w = _ldw(0)
    for kh in range(NHK):
        w, wz = nxt_w
        ids = [kh, 16 + kh, 32 + 2 * kh, 33 + 2 * kh]
        for ci in range(4):
            p = k.ps()
            for kc in range(NKC):
                k.mm(p[:, 0:T], w[:, kc, ci * 128:(ci + 1) * 128], hT[:, kc, :], start=(kc == 0), stop=(kc == NKC - 1))
            cid = ids[ci]
            cb_ = cb[ci % 2]
            k.copy(cb_[:, :, 3:3 + tps], v3(p[:, 0:T], tps), e="act")
            k.copy(cb_[:, :, 0:3], st.convtail[:, cid, :, :], e="pool")
            a_ = v3(xc[ci][:], tps)
            k.ts(a_, cb_[:, :, 0:tps], c.wconv[:, cid, 0:1], ALU.mult, e="pool")
            for j in range(1, 4):
                k.stt(a_, cb_[:, :, j:j + tps], c.wconv[:, cid, j:j + 1], a_, ALU.mult, ALU.add,
                      e="pool" if j % 2 else "dve")
            k.copy(st.convtail[:, cid, :, :], cb_[:, :, tps:tps + 3], e="pool")
            k.act(xc[ci][:], xc[ci][:], AF.Silu)
        for ci in range(2):
            k.tt(sqt[:], xc[ci][:], xc[ci][:], ALU.mult, e="pool")
            p = k.ps()
            k.mm(p[:, 0:T], c.ones[:], sqt[:])
            k.ts(rn[:], p[:, 0:T], EPS, ALU.add)
            k.act(rn[:], rn[:], AF.Sqrt)
            k.recip(rn[:], rn[:])
            if ci == 0:
                k.stt(qTb[:], xc[0][:], 128 ** -0.5, rn[:], ALU.mult, ALU.mult)
            else:
                k.tt(kn[:], xc[1][:], rn[:], ALU.mult)
                k.copy(kTb[:], kn[:], e="pool")
        for a in range(2):
            p = k.ps()
            for kc in range(NKC):
                k.mm(p[:, 0:T], wz[:, kc, a * 128:(a + 1) * 128], hT[:, kc, :], start=(kc == 0), stop=(kc == NKC - 1))
            k.act(zs[a][:], p[:, 0:T], AF.Silu)
        if kh + 1 < NHK:
            nxt_w = _ldw(kh + 1)
        if samp:
            for a in range(2):
                hh = 2 * kh + a
                k.dma(Sall[a][:, :, :], st.ssm_in[:, hh, :, :].re("s p v -> p s v"))
                k.copy(Sball[a][:, :, :], Sall[a][:, :, :], e="pool")
        if not samp:
            gdn_chunks_batched(k, c, G, st, kh, Bt, kTb, qTb, kn, xc, zs, beta, negg, gcs, edl, bg, ogT)
            continue
        for ch in range(nch):
            cs = slice(ch * C, (ch + 1) * C)
            pG = k.psb[0]
            k.mm(pG[0:C, 0:C], kTb[:, cs], kTb[:, cs])
            k.mm(pG[0:C, C:2 * C], qTb[:, cs], kTb[:, cs])
            pT = k.psb[1]
            k.tr(pT[0:C, 0:128], kn[:, cs], c.ident[:])
            for a in range(2):
                k.tr(pT[0:C, 128 * (1 + a):128 * (2 + a)], xc[2 + a][:, cs], c.ident[:])
            for a in range(2):
                hh = 2 * kh + a
                i2 = cnt % 2
                cnt += 1
                hcol = slice(hh, hh + 1)
                k.ts(rh[i2][0:C, 0:C], G.U[0:C, 0:C], negg[0:C, ch, hcol], ALU.mult, e="pool")
                pE = k.ps()
                k.mm(pE[:, 0:C], c.ones[0:C, :], rh[i2][0:C, 0:C])
                k.tt(tmpE[i2][0:C, 0:C], pE[0:C, 0:C], G.Mn[0:C, 0:C], ALU.add)
                k.act(E1[i2][0:C, 0:C], tmpE[i2][0:C, 0:C], AF.Exp, bias=gcs[0:C, ch, hcol])
                k.act(egrow[i2][:, 0:C], pE[:, 0:C], AF.Exp, scale=-1.0)
                if getattr(c, "stage", 99) == 35:
                    c.dbg = [("E1", E1[i2][0:C, 0:C]), ("egrow", egrow[i2][:, 0:C]), ("tmpE", tmpE[i2][0:C, 0:C])]
                    return
                A0 = Am[0]
                k.stt(A0[0:C, 0:C], pG[0:C, 0:C], beta[0:C, ch, hcol], E1[i2][0:C, 0:C], ALU.mult, ALU.mult)
                k.tt(A0[0:C, 0:C], A0[0:C, 0:C], G.SL[0:C, 0:C], ALU.mult, e="pool")
                k.tt(qk[i2][0:C, 0:C], pG[0:C, C:2 * C], E1[i2][0:C, 0:C], ALU.mult)
                pB = k.ps()
                k.tr(pB[0:C, 0:C], A0[0:C, 0:C], c.ident[0:C, 0:C])
                k.tr(pB[0:C, C:2 * C], qk[i2][0:C, 0:C], c.ident[0:C, 0:C])
                B0 = Bm[0]
                k.copy(B0[0:C, 0:C], pB[0:C, 0:C], e="act")
                k.copy(qkT[i2][0:C, 0:C], pB[0:C, C:2 * C], e="act")
                R = Rr[0]
                k.tt(R[0:C, 0:C], c.ident[0:C, 0:C], B0[0:C, 0:C], ALU.subtract, e="pool")
                if getattr(c, "stage", 99) == 36:
                    c.dbg = [("R", R[0:C, 0:C]), ("A0", A0[0:C, 0:C]), ("B0", B0[0:C, 0:C]), ("qk", qk[i2][0:C, 0:C])]
                    return
                Ac, Bc = A0, B0
                ri = 0
                for l in range(nlev):
                    An = Am[(l + 1) % 4]
                    Bn = Bm[(l + 1) % 4]
                    pA = k.ps()
                    k.mm(pA[0:C, 0:C], Bc[0:C, 0:C], Ac[0:C, 0:C])
                    k.copy(An[0:C, 0:C], pA[0:C, 0:C], e="act")
                    if l < nlev - 1:
                        pA2 = k.ps()
                        k.mm(pA2[0:C, 0:C], Ac[0:C, 0:C], Bc[0:C, 0:C])
                        k.copy(Bn[0:C, 0:C], pA2[0:C, 0:C], e="dve")
                    if getattr(c, "skipR", 0) and l >= 1:
                        Ac, Bc = An, Bn
                        continue
                    pR = k.ps()
                    k.mm(pR[0:C, 0:C], An[0:C, 0:C], R[0:C, 0:C])
                    Rn = Rr[(ri + 1) % 4]
                    ri += 1
                    k.tt(Rn[0:C, 0:C], pR[0:C, 0:C], R[0:C, 0:C], ALU.add)
                    R = Rn
                    Ac, Bc = An, Bn
                k.copy(Rb[i2][0:C, 0:C], R[0:C, 0:C], e="pool")
                if getattr(c, "stage", 99) <= 4:
                    c.dbg = [("R", R[0:C, 0:C]), ("E1", E1[i2][0:C, 0:C]), ("A0", Am[0][0:C, 0:C]), ("B0", Bm[0][0:C, 0:C]), ("egrow", egrow[i2][:, 0:C])]
                    return
                k.ts(vb[i2][0:C, :], pT[0:C, 128 * (1 + a):128 * (2 + a)], beta[0:C, ch, hcol], ALU.mult)
                k.act(kbg[i2][0:C, :], pT[0:C, 0:128], AF.Copy, scale=bg[0:C, ch, hcol])
                k.act(kd[i2][0:C, :], pT[0:C, 0:128], AF.Copy, scale=edl[0:C, ch, hcol])
                pw = k.ps()
                k.mm(pw[:, 0:C], kbg[i2][0:C, :], Rb[i2][0:C, 0:C])
                k.act(wTn[i2][:, 0:C], pw[:, 0:C], AF.Copy, scale=-1.0)
                k.tt(qg[i2][:, 0:C], qTb[:, cs], egrow[i2][:, 0:C], ALU.mult, e="pool")
                po = k.ps()
                if not samp:
                    Sbt = Sbh[i2]
                    k.copy(Sbt[:, :], st.S[:, hh, :], e="pool")
                    Sb_h = Sbt[:, :]
                    pv = k.ps()
                    k.mm(pv[0:C, 0:128], Rb[i2][0:C, 0:C], vb[i2][0:C, :], start=True, stop=False)
                    k.mm(pv[0:C, 0:128], wTn[i2][:, 0:C], Sb_h, start=False, stop=True)
                    k.copy(vn[i2][0:C, :], pv[0:C, 0:128], e="act")
                    k.mm(po[:, 0:C], Sb_h, qg[i2][:, 0:C], start=True, stop=False)
                    k.mm(po[:, 0:C], vn[i2][0:C, :], qkT[i2][0:C, 0:C], start=False, stop=True)
                    k.copy(osb[i2][:, 0:C], po[:, 0:C], e="act")
                    pS = k.ps()
                    k.mm(pS[:, 0:128], kd[i2][0:C, :], vn[i2][0:C, :])
                    k.stt(st.S[:, hh, :], st.S[:, hh, :], egrow[i2][:, C - 1:C], pS[:, 0:128], ALU.mult, ALU.add)
                else:
                    pu = k.ps()
                    k.mm(pu[:, 0:C], vb[i2][0:C, :], Rb[i2][0:C, 0:C], start=True, stop=False)
                    for s in range(nseq):
                        k.mm(pu[:, s * tps:(s + 1) * tps], Sball[a][:, s, :], wTn[i2][:, s * tps:(s + 1) * tps],
                             start=False, stop=(s == nseq - 1))
                    k.copy(vnT[i2][:, 0:C], pu[:, 0:C], e="act")
                    pvt = k.ps()
                    k.tr(pvt[0:C, 0:128], vnT[i2][:, 0:C], c.ident[:])
                    k.copy(vn[i2][0:C, :], pvt[0:C, 0:128], e="act")
                    k.mm(po[:, 0:C], vn[i2][0:C, :], qkT[i2][0:C, 0:C], start=True, stop=False)
                    for s in range(nseq):
                        k.mm(po[:, s * tps:(s + 1) * tps], Sball[a][:, s, :], qg[i2][:, s * tps:(s + 1) * tps],
                             start=False, stop=(s == nseq - 1))
                    k.copy(osb[i2][:, 0:C], po[:, 0:C], e="act")
                    for s in range(nseq):
                        j2 = s % 2
                        k.ts(kds[j2][0:C, :], kd[i2][0:C, :], G.seqmask[0:C, s:s + 1], ALU.mult, e="pool")
                        pS = k.ps()
                        k.mm(pS[:, 0:128], kds[j2][0:C, :], vn[i2][0:C, :])
                        k.stt(Sall[a][:, s, :], Sall[a][:, s, :], egrow[i2][:, s * tps + tps - 1:s * tps + tps],
                              pS[:, 0:128], ALU.mult, ALU.add)
                k.tt(osq[i2][:, 0:C], osb[i2][:, 0:C], osb[i2][:, 0:C], ALU.mult, e="pool")
                pq = k.ps()
                k.mm(pq[:, 0:C], c.ones[:], osq[i2][:, 0:C])
                rstd_from_psum(k, orn[i2][:, 0:C], pq[:, 0:C], 128)
                k.tt(osb[i2][:, 0:C], osb[i2][:, 0:C], orn[i2][:, 0:C], ALU.mult)
                k.stt(ogT[:, hh, cs], osb[i2][:, 0:C], c.gnorm[:, 0:1], zs[a][:, cs], ALU.mult, ALU.mult)
                if getattr(c, "stage", 99) <= 5:
                    c.dbg = [("osb", osb[i2][:, 0:C]), ("S0", st.S[:, hh, :])]
                    return
        if samp:
            for a in range(2):
                hh = 2 * kh + a
                k.dma(st.ssm_out[:, hh, :, :].re("s p v -> p s v"), Sall[a][:, :, :], store=True)
    ws2 = ws
    def cons(oc, p):
        resid_add(k, c, xT, oc, p, T, mods[:, 32 + oc, mcol], nseq, tps)
    k.ps_rot = list(range(8))
    linear_fm(k, ws2, L.gdn_w_out, 0, 32, 0, D, 256, lambda kc: ogT[:, kc, :], T, cons)
    k.release(m)


class GC:
    pass


def make_consts():
    cst = {}
    cst["ident"] = np.eye(128, dtype=np.float32)
    cst["ones"] = np.ones((128, 128), np.float32)
    i = np.arange(128)
    cst["U_p"] = (i[:, None] <= i[None, :]).astype(np.float32)
    cst["OB_p"] = np.ones((128, 128), np.float32)
    cst["Mn_p"] = np.where(i[None, :] <= i[:, None], 0.0, NEG).astype(np.float32)
    cst["SL_p"] = (i[None, :] < i[:, None]).astype(np.float32)
    blk = i // 4
    same = blk[:, None] == blk[None, :]
    cst["U_s"] = (same & (i[:, None] <= i[None, :])).astype(np.float32)
    cst["OB_s"] = same.astype(np.float32)
    cst["Mn_s"] = np.where(same & (i[None, :] <= i[:, None]), 0.0, NEG).astype(np.float32)
    cst["SL_s"] = (same & (i[None, :] < i[:, None])).astype(np.float32)
    sm = np.zeros((128, 32), np.float32)
    sm[i, blk] = 1.0
    cst["seqmask"] = sm
    return cst


def setup_common(k, c, I):
    c.ident = load_const(k, I["ident"], [128, 128])
    c.ones = load_const(k, I["ones"], [128, 128])
    c.gs = k.sb([128, 16, 17], F32, name="gs")
    c.rtmp = [k.sb([128, 64], F32, name="rtmp") for _ in range(2)]
    c.rti = 0
    def colvec(dv, n, name):
        t = k.sb([128, n], F32, name=name)
        m = k.mark()
        row = k.sb([128, 128], F32, name="row")
        k.dma(row[0:n, :], dv.re("(c p) -> c p", p=128))
        p = k.ps()
        k.tr(p[:, 0:n], row[0:n, :], c.ident[0:n, 0:n])
        k.copy(t[:], p[:, 0:n])
        k.release(m)
        return t
    c.colvec = colvec
    c.scT = k.sb([128, NKC, 17], BF16, name="scT")
    m = k.mark()
    cc = k.sb([128, D], F32, name="cc")
    k.dma(cc[0:17, :], I["cc"][:, :])
    k.act(cc[0:17, :], cc[0:17, :], AF.Silu)
    p = k.ps()
    for kc in range(NKC):
        k.tr(p[:, kc * 17:(kc + 1) * 17], cc[0:17, kc * 128:(kc + 1) * 128], c.ident[0:17, 0:17])
    k.copy(c.scT[:, :, :], p[:, 0:NKC * 17].re("p (k t) -> p k t", t=17))
    k.release(m)


def setup_gdn(k, c, I):
    c.gmix = c.colvec(I["g_mix"][0:2 * D], 32, "gmix")
    c.gffn = c.colvec(I["g_ffn"][0:2 * D], 32, "gffn")
    c.gnorm = k.sb([128, 1], F32, name="gnorm")
    k.dma(c.gnorm[:, :], I["gdn_g_norm"][0:128].re("(p o) -> p o", o=1))
    c.dtb = k.sb([128, 32], F32, name="dtb")
    k.dma(c.dtb[:, :], I["gdn_dt_bias"][0:32].re("(o n) -> o n", o=1).bc([128, 32]))
    c.nega = k.sb([128, 32], F32, name="nega")
    k.dma(c.nega[:, :], I["gdn_a_log"][0:32].re("(o n) -> o n", o=1).bc([128, 32]))
    k.act(c.nega[:, :], c.nega[:, :], AF.Exp)
    k.ts(c.nega[:, :], c.nega[:, :], -1.0, ALU.mult)
    c.wconv = k.sb([128, 64, 4], F32, name="wconv")
    m = k.mark()
    wr = k.sb([128, 8192], F32, name="wr")
    k.dma(wr[0:4, :], I["gdn_w_conv"][:, :])
    for g in range(2):
        p = k.ps()
        for j in range(32):
            cid = g * 32 + j
            k.tr(p[:, j * 4:(j + 1) * 4], wr[0:4, cid * 128:(cid + 1) * 128], c.ident[0:4, 0:4])
        k.copy(c.wconv[:, g * 32:(g + 1) * 32, :], p[:, 0:128].re("p (c j) -> p c j", j=4))
    k.release(m)


def load_G(k, I, sfx):
    G = GC()
    G.U = load_const(k, I["U_" + sfx], [128, 128])
    G.OB = load_const(k, I["OB_" + sfx], [128, 128])
    G.Mn = load_const(k, I["Mn_" + sfx], [128, 128])
    G.SL = load_const(k, I["SL_" + sfx], [128, 128])
    if sfx == "s":
        G.seqmask = load_const(k, I["seqmask"], [128, 32])
    return G


def rope_tables():
    half = 32
    inv = np.power(np.float32(10000.0), -(np.arange(half, dtype=np.float32) / np.float32(half))).astype(np.float32)
    pos = np.concatenate([np.arange(2048), 8192 + np.arange(4)]).astype(np.float32)
    ang = (pos[None, :] * inv[:, None]).astype(np.float32)
    cs = np.cos(ang.astype(np.float64)).astype(np.float32)
    sn = np.sin(ang.astype(np.float64)).astype(np.float32)
    cos2 = np.concatenate([cs, cs], 0)
    sin_s = np.concatenate([-sn, sn], 0)
    cos2 = np.concatenate([cos2, np.tile(cos2[:, 2048:2052], (1, 16))], 1)
    sin_s = np.concatenate([sin_s, np.tile(sin_s[:, 2048:2052], (1, 16))], 1)
    return np.ascontiguousarray(cos2), np.ascontiguousarray(sin_s)


def load_rope(k, c, I, pos0, T):
    c.cos2 = k.sb([64, T], F32, name="cos2")
    c.sin_s = k.sb([64, T], F32, name="sin_s")
    k.dma(c.cos2[:, :], I["rope_cos"][:, pos0:pos0 + T])
    k.dma(c.sin_s[:, :], I["rope_sin"][:, pos0:pos0 + T])


def rope_apply(k, c, out, p1, p2, T, tmp):
    k.tt(tmp[0][0:64, 0:T], p1, c.cos2[:, 0:T], ALU.mult)
    k.tt(tmp[1][0:64, 0:T], p2, c.sin_s[:, 0:T], ALU.mult)
    k.tt(out, tmp[0][0:64, 0:T], tmp[1][0:64, 0:T], ALU.add, e="pool")


def kv_phase(k, c, L, xT, T, mods, mcol, nseq, tps, KB, tok0, ckv_out, kpe_out, row0):
    m = k.mark()
    hT = k.sb([128, NKC, T], BF16, nsplit=NKC, name="hT")
    k.ts(c.gs[:, :, :], mods[:, 16:32, :], 1.0, ALU.add)
    k.tt(c.gs[:, :, :], c.gs[:, :, :], c.gkv[:, 0:16].un(2).bc([128, 16, 17]), ALU.mult)
    modnorm(k, c, xT, T, lambda kc: c.gs[:, kc, mcol], lambda kc: mods[:, kc, mcol], mcol, nseq, tps, hT)
    ws = WS(k, NKC * 640)
    w = ws.load(L.kv_w_down, 0, NKC, [(0, 576), (544, 32), (512, 32)])
    ckf = k.sb([128, 4, T], F32, nsplit=4, name="ckf")
    sq = k.sb([128, T], F32, name="ksq")
    pss = k.psb[0]
    k.ps_rot = [1, 2, 3, 4, 5, 6, 7]
    for rc in range(4):
        p = k.ps()
        for kc in range(NKC):
            k.mm(p[:, 0:T], w[:, kc, rc * 128:(rc + 1) * 128], hT[:, kc, :], start=(kc == 0), stop=(kc == NKC - 1))
        k.copy(ckf[:, rc, :], p[:, 0:T], e="act")
        k.tt(sq[:], ckf[:, rc, :], ckf[:, rc, :], ALU.mult)
        k.mm(pss[:, 0:T], c.ones[:], sq[:], start=(rc == 0), stop=(rc == 3))
    rstd = k.sb([128, T], F32, name="krstd")
    rstd_from_psum(k, rstd[:], pss[:, 0:T], KVL)
    k.ps_rot = list(range(8))
    for rc in range(4):
        k.stt(ckf[:, rc, :], ckf[:, rc, :], c.gkvn[:, rc:rc + 1], rstd[:], ALU.mult, ALU.mult)
        k.copy(KB.ckvT(rc), ckf[:, rc, :], e="pool")
    import os
    kvstop = int(os.environ.get("KVSTOP", "9"))
    if kvstop <= 1:
        k.release(m)
        return
    p1 = k.ps()
    p2 = k.ps()
    for kc in range(NKC):
        k.mm(p1[0:64, 0:T], w[:, kc, 512:576], hT[:, kc, :], start=(kc == 0), stop=(kc == NKC - 1))
    for kc in range(NKC):
        k.mm(p2[0:64, 0:T], w[:, kc, 576:640], hT[:, kc, :], start=(kc == 0), stop=(kc == NKC - 1))
    kpf = k.sb([64, T], F32, name="kpf")
    tmp = [k.sb([64, T], F32, name="rtmpa"), k.sb([64, T], F32, name="rtmpb")]
    rope_apply(k, c, kpf[:, :], p1[0:64, 0:T], p2[0:64, 0:T], T, tmp)
    k.copy(KB.kpeT(), kpf[:, :], e="pool")
    if kvstop <= 2:
        k.release(m)
        return
    ts_ = min(128, T)
    nsub = T // ts_
    stg = [k.sb([128, 576], F32, name="kvstg") for _ in range(2)]
    for s in range(nsub):
        st = stg[s % 2]
        p = k.ps()
        for rc in range(4):
            k.tr(p[0:ts_, rc * 128:(rc + 1) * 128], ckf[:, rc, s * ts_:(s + 1) * ts_], c.ident[:, :])
        k.copy(st[0:ts_, 0:512], p[0:ts_, 0:512], e="act")
        if KB.ckv_tm is not None and kvstop != 3:
            k.copy(KB.ckv_tm(s), st[0:ts_, 0:512], e="pool")
        if kvstop >= 4:
            pk = k.ps()
            k.tr(pk[0:ts_, 0:64], kpf[:, s * ts_:(s + 1) * ts_], c.ident[0:64, 0:64])
            k.copy(st[0:ts_, 512:576], pk[0:ts_, 0:64], e="act")
        k.dma(ckv_out[row0 + s * ts_:row0 + (s + 1) * ts_, :], st[0:ts_, 0:512], store=True)
        if kvstop >= 5:
            k.dma(kpe_out[row0 + s * ts_:row0 + (s + 1) * ts_, :], st[0:ts_, 512:576], store=True)
    k.release(m)


def qside_head(k, c, L, ws, h, cqT, T, Q):
    wq = ws.load(L.mla_w_uq, 0, 4, [(h * 192, 192), (h * 192 + 160, 32), (h * 192 + 128, 32)])
    k.dma(Q.wukf[:, :, :], L.kv_w_uk[:, h * 128:(h + 1) * 128].re("(rc p) d -> p rc d", p=128))
    p = k.ps()
    for rc in range(4):
        k.tr(p[:, rc * 128:(rc + 1) * 128], Q.wukf[:, rc, :], c.ident[:, :])
    k.copy(Q.wukT[:, :], p[:, 0:512], e="act")
    p = k.ps()
    for kc in range(4):
        k.mm(p[:, 0:T], wq[:, kc, 0:128], cqT[:, kc, :], start=(kc == 0), stop=(kc == 3))
    k.copy(Q.qn[:, 0:T], p[:, 0:T], e="act")
    p1 = k.ps()
    for kc in range(4):
        k.mm(p1[0:64, 0:T], wq[:, kc, 128:192], cqT[:, kc, :], start=(kc == 0), stop=(kc == 3))
    p2 = k.ps()
    for kc in range(4):
        k.mm(p2[0:64, 0:T], wq[:, kc, 192:256], cqT[:, kc, :], start=(kc == 0), stop=(kc == 3))
    rope_apply(k, c, Q.qpe_dst(h), p1[0:64, 0:T], p2[0:64, 0:T], T, Q.rtmp)
    for rc in range(4):
        p = k.ps()
        k.mm(p[:, 0:T], Q.wukT[:, rc * 128:(rc + 1) * 128], Q.qn[:, 0:T])
        k.copy(Q.qlat_dst(h, rc), p[:, 0:T], e="act" if rc % 2 else "dve")


def cq_compute(k, c, L, ws, hT, T, cqT):
    cqf = k.sb([128, 4, T], F32, nsplit=4, name="cqf")
    sq = k.sb([128, T], F32, name="cqsq")
    def cons(oc, p):
        k.copy(cqf[:, oc, :], p[:, 0:T], e="act")
    linear_fm(k, ws, L.mla_w_dq, 0, NKC, 0, QL, 512, lambda kc: hT[:, kc, :], T, cons)
    pss = k.ps()
    for rc in range(4):
        k.tt(sq[:], cqf[:, rc, :], cqf[:, rc, :], ALU.mult)
        k.mm(pss[:, 0:T], c.ones[:], sq[:], start=(rc == 0), stop=(rc == 3))
    rstd = k.sb([128, T], F32, name="cqrstd")
    rstd_from_psum(k, rstd[:], pss[:, 0:T], QL)
    for rc in range(4):
        k.stt(cqT[:, rc, :], cqf[:, rc, :], c.gq[:, rc:rc + 1], rstd[:], ALU.mult, ALU.mult)


def attn_phase_prompt(k, c, L, xT, T, mods, KB, tile):
    mcol = slice(0, 1)
    m = k.mark()
    hT = k.sb([128, NKC, T], BF16, nsplit=NKC, name="hT")
    k.ts(c.gs[:, :, :], mods[:, 16:32, :], 1.0, ALU.add)
    k.tt(c.gs[:, :, :], c.gs[:, :, :], c.gmix[:, 16:32].un(2).bc([128, 16, 17]), ALU.mult)
    modnorm(k, c, xT, T, lambda kc: c.gs[:, kc, mcol], lambda kc: mods[:, kc, mcol], mcol, 1, T, hT)
    ws = WS(k, NKC * 512)
    cqT = k.sb([128, 4, T], BF16, nsplit=4, name="cqT")
    m2 = k.mark()
    cq_compute(k, c, L, ws, hT, T, cqT)
    k.release(m2)
    aoT = k.sb([128, NH, T], BF16, nsplit=NH, name="aoT")
    Q = Ctx()
    Q.wukf = k.sb([128, 4, 128], F32, name="wukf")
    Q.wukT = k.sb([128, 512], BF16, name="wukT")
    Q.qn = k.sb([128, T], BF16, name="qn")
    Q.rtmp = [k.sb([64, T], F32, name="qrt0"), k.sb([64, T], F32, name="qrt1")]
    qlat = k.sb([128, 4, T], BF16, nsplit=4, name="qlat")
    qpe = k.sb([64, T], BF16, name="qpe")
    Q.qpe_dst = lambda h: qpe[:, 0:T]
    Q.qlat_dst = lambda h, rc: qlat[:, rc, :]
    pT = [k.sb([128, T], BF16, name="pT") for _ in range(2)]
    linv = k.sb([128, T], F32, name="linv")
    olat = k.sb([128, 4, T], BF16, nsplit=4, name="olat")
    wuv_t = [k.sb([128, 4, 128], BF16, name="wuv") for _ in range(2)]
    nkb = 4 * tile + 4
    for h in range(NH):
        k.ps_rot = [6, 7]
        qside_head(k, c, L, ws, h, cqT, T, Q)
        wuv = wuv_t[h % 2]
        k.dma(wuv[:, :, :], L.kv_w_uv[:, h * 128:(h + 1) * 128].re("(rc p) v -> p rc v", p=128), q="pool")
        k.ps_rot = [7]
        pl = k.psb[6]
        po = [k.psb[2 + rc] for rc in range(4)]

        def scores(kb):
            ps_ = k.psb[kb % 2]
            q0 = max(0, kb - 4 * tile) * 128
            ks = slice(kb * 128, (kb + 1) * 128)
            for rc in range(4):
                k.mm(ps_[:, q0:T], KB.ckvT_bf[:, rc, ks], qlat[:, rc, q0:T], start=(rc == 0), stop=False)
            k.mm(ps_[:, q0:T], KB.kpeT_bf[0:64, ks], qpe[0:64, q0:T], start=False, stop=True)
            return ps_, q0

        nxt = scores(0)
        for kb in range(nkb):
            ps_, q0 = nxt
            if kb + 1 < nkb:
                nxt = scores(kb + 1)
            pt = pT[kb % 2]
            k.act(pt[:, q0:T], ps_[:, q0:T], AF.Exp, scale=SM_SCALE)
            if kb >= 4 * tile:
                k.tt(pt[:, q0:q0 + 128], pt[:, q0:q0 + 128], c.tri[:, :], ALU.mult, e="pool")
            last = (kb == nkb - 1)
            for rc in range(4):
                k.mm(po[rc][:, q0:T], KB.ckv_tm_bf[:, kb, rc * 128:(rc + 1) * 128], pt[:, q0:T], start=(kb == 0), stop=last)
            k.mm(pl[:, q0:T], c.ones_bf[:, :], pt[:, q0:T], start=(kb == 0), stop=last)
        k.recip(linv[:, :], pl[:, 0:T])
        for rc in range(4):
            k.copy(olat[:, rc, :], po[rc][:, 0:T], e="act" if rc % 2 else "dve")
        pv = k.ps()
        for rc in range(4):
            k.mm(pv[:, 0:T], wuv[:, rc, :], olat[:, rc, :], start=(rc == 0), stop=(rc == 3))
        k.tt(aoT[:, h, :], pv[:, 0:T], linv[:, :], ALU.mult)
    k.ps_rot = list(range(8))
    def cons(oc, p):
        resid_add(k, c, xT, oc, p, T, mods[:, 32 + oc, mcol], 1, T)
    linear_fm(k, ws, L.mla_w_o, 0, NH, 0, D, 512, lambda kc: aoT[:, kc, :], T, cons)
    k.release(m)


def final_phase(k, c, xT, T, mods, mcol, nseq, tps, y_out, row0):
    m = k.mark()
    yT = k.sb([128, NKC, T], F32, nsplit=NKC, name="yT")
    k.ts(c.gs[:, :, :], mods[:, 16:32, :], 1.0, ALU.add)
    k.tt(c.gs[:, :, :], c.gs[:, :, :], c.gfin[:, 0:16].un(2).bc([128, 16, 17]), ALU.mult)
    modnorm(k, c, xT, T, lambda kc: c.gs[:, kc, mcol], lambda kc: mods[:, kc, mcol], mcol, nseq, tps, yT)
    transpose_out(k, c, lambda kc: yT[:, kc, :], NKC, T, y_out, row0)
    k.release(m)


INPUT_SPECS = {
    "cc": ([17, 2048], F32), "xp": ([2048, 2048], F32), "xs": ([64, 2048], F32),
    "w_ada0": ([2048, 12288], F32), "b_ada0": ([12288], F32), "w_ada1": ([2048, 12288], F32), "b_ada1": ([12288], F32),
    "g_mix": ([4096], F32), "g_ffn": ([4096], F32),
    "w_gate_up0": ([2048, 11264], F32), "w_gate_up1": ([2048, 11264], F32),
    "w_down0": ([5632, 2048], F32), "w_down1": ([5632, 2048], F32),
    "gdn_w_in": ([2048, 12352], F32), "gdn_w_conv": ([4, 8192], F32), "gdn_a_log": ([32], F32), "gdn_dt_bias": ([32], F32),
    "gdn_g_norm": ([128], F32), "gdn_w_out": ([4096, 2048], F32),
    "kv_w_ada": ([2048, 4096], F32), "kv_b_ada": ([4096], F32), "kv_g_in": ([2048], F32), "kv_w_down": ([2048, 576], F32),
    "kv_g_norm": ([512], F32), "kv_w_uk": ([512, 2048], F32), "kv_w_uv": ([512, 2048], F32),
    "mla_w_dq": ([2048, 512], F32), "mla_g_q": ([512], F32), "mla_w_uq": ([512, 3072], F32), "mla_w_o": ([2048, 2048], F32),
    "final_w_ada": ([2048, 4096], F32), "final_b_ada": ([4096], F32), "final_g": ([2048], F32),
    "rope_cos": ([64, 2116], F32), "rope_sin": ([64, 2116], F32), "tri": ([128, 128], F32),
    "cache_ckv": ([10240 * 128, 512], F32), "cache_kpe": ([10240 * 128, 64], F32), "page_table": ([1, 1024], I32),
    "state_ssm": ([16, 32, 128, 128], F32), "state_conv": ([48, 8192], F32),
    "iota_p": ([128, 1], F32), "cm": ([128, 64], F32),
}
OUTPUT_SPECS = {
    "yp": [2048, 2048], "ys": [64, 2048], "ssm_p": [32, 128, 128], "conv_p": [3, 8192], "ckv_p": [2048, 512], "kpe_p": [2048, 64],
    "ssm_s": [16, 32, 128, 128], "conv_s": [48, 8192], "ckv_s": [64, 512], "kpe_s": [64, 64],
}


def build_all(k, do_prompt=True, do_sample=True, ntiles=4, dbg=None, stop=None, dbg_s=False):
    c = Ctx()
    c.dbg_s = dbg_s
    L = Ctx()
    I = {}
    cst = make_consts()
    for n, v in cst.items():
        I[n] = k.din(n, list(v.shape))

    class LazyIn(dict):
        def __missing__(self, key):
            shp, dt = INPUT_SPECS[key]
            t = k.din(key, shp, dt)
            self[key] = t
            return t
    II = LazyIn(I)
    O = {}

    def out(name):
        if name not in O:
            O[name] = k.dout(name, OUTPUT_SPECS[name])
        return O[name]
    L.gdn_w_in = II["gdn_w_in"]
    L.gdn_w_out = II["gdn_w_out"]
    L.w_gate_up = [II["w_gate_up0"], II["w_gate_up1"]]
    L.w_down = [II["w_down0"], II["w_down1"]]
    L.kv_w_down = II["kv_w_down"]
    L.kv_w_uk = II["kv_w_uk"]
    L.kv_w_uv = II["kv_w_uv"]
    L.mla_w_dq = II["mla_w_dq"]
    L.mla_w_uq = II["mla_w_uq"]
    L.mla_w_o = II["mla_w_o"]
    setup_common(k, c, II)
    setup_gdn(k, c, II)
    c.gkv = c.colvec(II["kv_g_in"][0:D], 16, "gkv")
    c.gkvn = c.colvec(II["kv_g_norm"][0:512], 4, "gkvn")
    c.gq = c.colvec(II["mla_g_q"][0:512], 4, "gq")
    c.gfin = c.colvec(II["final_g"][0:D], 16, "gfin")
    c.tri = load_const(k, II["tri"], [128, 128], BF16, q="pool")
    c.ones_bf = k.sb([128, 128], BF16, name="ones_bf")
    k.copy(c.ones_bf[:, :], c.ones[:, :])
    T = 512
    x1_d = [k.dscratch(f"x1_{t}", [128, NKC * T]) for t in range(4)]
    KT_d = k.dscratch("KT_d", [128, 4 * 2048], BF16)
    KP_d = k.dscratch("KP_d", [64, 2048], BF16)
    KV_d = k.dscratch("KV_d", [128, 16 * 512], BF16)
    xs1_d = k.dscratch("xs1_d", [128, NKC * 64])
    c.ckvT_new = k.sb([128, 4, 64], BF16, nsplit=4, name="ckvT_new")
    c.kpeT_new = k.sb([64, 64], BF16, name="kpeT_new")
    c.ckv_tm_new = k.sb([128, 512], BF16, name="ckv_tm_new")
    m0 = k.mark()
    mods0 = ada_phase(k, c, II["w_ada0"], II["b_ada0"], 12288, "mods0")
    modskv = ada_phase(k, c, II["kv_w_ada"], II["kv_b_ada"], 4096, "modskv")
    if do_prompt:
        mp = k.mark()
        G = load_G(k, II, "p")
        st = Ctx()
        st.S = k.sb([128, 32, 128], F32, nsplit=32, name="S")
        st.convtail = k.sb([128, 64, 1, 3], F32, nsplit=64, name="ctail")
        k.memset(st.S[:, :, :], 0.0)
        k.memset(st.convtail[:, :, :, :], 0.0)
        xT = k.sb([128, NKC, T], F32, nsplit=NKC, name="xT")
        mcol = slice(0, 1)
        for t in range(ntiles):
            transpose_in(k, c, II["xp"], t * T, T, xT)
            gdn_phase(k, c, L, xT, T, 128, mods0, mcol, 1, T, G, st, False)
            ffn_phase(k, c, L, xT, T, mods0, mcol, 1, T, 0)
            k.dma(x1_d[t][:, :], xT[:, :, :].re("p k t -> p (k t)"), store=True)
            mt = k.mark()
            load_rope(k, c, II, t * T, T)
            KBw = Ctx()
            ckvT_st = k.sb([128, 4, T], BF16, nsplit=4, name="ckvT_st")
            kpeT_st = k.sb([64, T], BF16, name="kpeT_st")
            ckvtm_st = k.sb([128, 4, 512], BF16, nsplit=4, name="ckvtm_st")
            KBw.ckvT = lambda rc: ckvT_st[:, rc, :]
            KBw.kpeT = lambda: kpeT_st[:, :]
            KBw.ckv_tm = lambda s: ckvtm_st[:, s, :]
            if stop != "nokv":
                kv_phase(k, c, L, xT, T, modskv, mcol, 1, T, KBw, t * T, out("ckv_p"), out("kpe_p"), t * T)
            for rc in range(4):
                k.dma(KT_d[:, rc * 2048 + t * T: rc * 2048 + (t + 1) * T], ckvT_st[:, rc, :], store=True)
            k.dma(KP_d[:, t * T:(t + 1) * T], kpeT_st[:, :], store=True)
            k.dma(KV_d[:, t * 4 * 512:(t + 1) * 4 * 512], ckvtm_st[:, :, :].re("p s r -> p (s r)"), store=True)
            k.release(mt)
        k.dma(out("ssm_p")[:, :, :].re("h p v -> p h v"), st.S[:, :, :], store=True)
        transpose_out(k, c, lambda cid: st.convtail[:, cid, 0, :], 64, 3, out("conv_p"), 0)
        k.release(mp)
    if do_sample:
        sample_layer0(k, c, L, II, out, mods0, modskv, xs1_d)
    k.release(m0)
    if stop == "l0":
        k.finish()
        return II, O, cst
    mods1 = ada_phase(k, c, II["w_ada1"], II["b_ada1"], 12288, "mods1")
    modsf = ada_phase(k, c, II["final_w_ada"], II["final_b_ada"], 4096, "modsf")
    if do_prompt:
        mp = k.mark()
        KB = Ctx()
        KB.ckvT_bf = k.sb([128, 4, 2048], BF16, nsplit=4, name="ckvT_bf")
        KB.kpeT_bf = k.sb([64, 2048], BF16, name="kpeT_bf")
        KB.ckv_tm_bf = k.sb([128, 16, 512], BF16, nsplit=16, name="ckv_tm_bf")
        k.dma(KB.ckvT_bf[:, :, :].re("p r t -> p (r t)"), KT_d[:, :])
        k.dma(KB.kpeT_bf[:, :], KP_d[:, :])
        k.dma(KB.ckv_tm_bf[:, :, :].re("p s r -> p (s r)"), KV_d[:, :])
        xT = k.sb([128, NKC, T], F32, nsplit=NKC, name="xT")
        for t in range(ntiles):
            k.dma(xT[:, :, :].re("p k t -> p (k t)"), x1_d[t][:, :])
            mt = k.mark()
            load_rope(k, c, II, t * T, T)
            attn_phase_prompt(k, c, L, xT, T, mods1, KB, t)
            k.release(mt)
            if dbg is not None and "xmid1" in dbg:
                transpose_out(k, c, lambda kc: xT[:, kc, :], NKC, T, dbg["xmid1"], t * T)
            ffn_phase(k, c, L, xT, T, mods1, slice(0, 1), 1, T, 1)
            final_phase(k, c, xT, T, modsf, slice(0, 1), 1, T, out("yp"), t * T)
        k.release(mp)
    if do_sample:
        sample_layer1(k, c, L, II, out, mods1, modsf, xs1_d)
    k.finish()
    return II, O, cst


def sample_layer0(k, c, L, II, out, mods0, modskv, xs1_d):
    T, nseq, tps = 64, 16, 4
    mcol = slice(1, 17)
    m = k.mark()
    G = load_G(k, II, "s")
    c.Gs = G
    st = Ctx()
    st.convtail = k.sb([128, 64, nseq, 3], F32, nsplit=64, name="ctail_s")
    st.ssm_in = II["state_ssm"]
    st.ssm_out = out("ssm_s")
    m2 = k.mark()
    stg = [k.sb([128, 2048], F32, name="cstg") for _ in range(2)]
    for pc in range(4):
        sg = stg[pc % 2]
        k.dma(sg[0:48, :], II["state_conv"][:, pc * 2048:(pc + 1) * 2048])
        for g in range(2):
            p = k.ps()
            for j in range(8):
                cl = g * 8 + j
                k.tr(p[:, j * 48:(j + 1) * 48], sg[0:48, cl * 128:(cl + 1) * 128], c.ident[0:48, 0:48])
            cid0 = pc * 16 + g * 8
            k.copy(st.convtail[:, cid0:cid0 + 8, :, :].re("p c s j -> p c (s j)"),
                   p[:, 0:8 * 48].re("p (c x) -> p c x", x=48), e="act" if g else "dve")
    k.release(m2)
    xT = k.sb([128, NKC, T], F32, nsplit=NKC, name="xTs")
    transpose_in(k, c, II["xs"], 0, T, xT)
    gdn_phase(k, c, L, xT, T, 64, mods0, mcol, nseq, tps, G, st, True)
    if getattr(c, "dbg_s", None):
        transpose_out(k, c, lambda kc: xT[:, kc, :], NKC, T, k.dout("d_xmid0", [64, 2048]), 0)
    for pc in range(4):
        transpose_out(k, c, lambda cid: st.convtail[:, pc * 16 + cid, :, :].re("p s j -> p (s j)"), 16, 48,
                      out("conv_s"), 0, col0=pc * 2048)
    ffn_phase(k, c, L, xT, T, mods0, mcol, nseq, tps, 0)
    if getattr(c, "dbg_s", None):
        transpose_out(k, c, lambda kc: xT[:, kc, :], NKC, T, k.dout("d_x0", [64, 2048]), 0)
    k.dma(xs1_d[:, :], xT[:, :, :].re("p k t -> p (k t)"), store=True)
    load_rope(k, c, II, 2052, T)
    KBw = Ctx()
    KBw.ckvT = lambda rc: c.ckvT_new[:, rc, :]
    KBw.kpeT = lambda: c.kpeT_new[:, :]
    KBw.ckv_tm = lambda s: c.ckv_tm_new[0:64, :]
    kv_phase(k, c, L, xT, T, modskv, mcol, nseq, tps, KBw, 0, out("ckv_s"), out("kpe_s"), 0)
    k.release(m)


def sample_layer1(k, c, L, II, out, mods1, modsf, xs1_d):
    T, nseq, tps = 64, 16, 4
    mcol = slice(1, 17)
    m = k.mark()
    xT = k.sb([128, NKC, T], F32, nsplit=NKC, name="xTs")
    k.dma(xT[:, :, :].re("p k t -> p (k t)"), xs1_d[:, :])
    load_rope(k, c, II, 2052, T)
    hT = k.sb([128, NKC, T], BF16, nsplit=NKC, name="hTs")
    k.ts(c.gs[:, :, :], mods1[:, 16:32, :], 1.0, ALU.add)
    k.tt(c.gs[:, :, :], c.gs[:, :, :], c.gmix[:, 16:32].un(2).bc([128, 16, 17]), ALU.mult)
    modnorm(k, c, xT, T, lambda kc: c.gs[:, kc, mcol], lambda kc: mods1[:, kc, mcol], mcol, nseq, tps, hT)
    ws = WS(k, NKC * 512)
    cqT = k.sb([128, 4, T], BF16, nsplit=4, name="cqTs")
    m2 = k.mark()
    cq_compute(k, c, L, ws, hT, T, cqT)
    k.release(m2)
    QLs = k.sb([128, 4, NH, T], BF16, name="QLs")
    QPs = k.sb([64, NH, T], BF16, name="QPs")
    OLs = k.sb([128, 4, NH, T], BF16, name="OLs")
    aoT = k.sb([128, NH, T], BF16, nsplit=NH, name="aoTs")
    Q = Ctx()
    Q.wukf = k.sb([128, 4, 128], F32, name="wukf")
    Q.wukT = k.sb([128, 512], BF16, name="wukT")
    Q.qn = k.sb([128, T], BF16, name="qn")
    Q.rtmp = [k.sb([64, T], F32, name="qrt0"), k.sb([64, T], F32, name="qrt1")]
    Q.qpe_dst = lambda h: QPs[:, h, :]
    Q.qlat_dst = lambda h, rc: QLs[:, rc, h, :]
    for h in range(NH):
        qside_head(k, c, L, ws, h, cqT, T, Q)
    ptb = k.sb([128, 1024], I32, name="ptb")
    k.dma(ptb[:, :], II["page_table"][0:1, :].bc([128, 1024]))
    idxf = k.sb([128, 1024], F32, name="idxf")
    k.copy(idxf[:, :], ptb[:, :])
    iota = load_const(k, II["iota_p"], [128, 1])
    k.ts(idxf[:, :], idxf[:, :], 128.0, ALU.mult, iota[:, 0:1], ALU.add)
    idx = k.sb([128, 1024], I32, name="idx")
    k.copy(idx[:, :], idxf[:, :])
    cm = load_const(k, II["cm"], [128, 64])
    ident_bf = k.sb([128, 128], BF16, name="ident_bf")
    k.copy(ident_bf[:, :], c.ident[:, :])
    Kp = [k.sb([128, 576], BF16, name="Kp") for _ in range(6)]
    KT = [k.sb([128, 5, 128], BF16, name="KT") for _ in range(3)]
    pts = [k.sb([128, 64], BF16, name="pts") for _ in range(3)]
    pe_ = k.sb([64, 64], F32, name="pe_")
    ptm = k.sb([64, 64], BF16, name="ptm")
    linv = k.sb([128, 64], F32, name="linvs")
    ptr_bank = k.psb[7]
    ptr_bf = ptr_bank[:, :].bitcast(BF16)
    po = [k.psb[2 + rc] for rc in range(4)]
    pl = k.psb[6]
    import os
    NPG = int(os.environ.get("NPG", "64"))
    n = 0
    for s in range(nseq):
        qs = slice(s * tps, (s + 1) * tps)
        def qlat_s(rc):
            return QLs[:, rc, :, qs]
        qpe_s = QPs[:, :, qs]
        def stA(pg):
            nn = s * NPG + pg
            col = s * 64 + pg
            kp = Kp[nn % 6]
            k.idma(kp[:, 0:512], II["cache_ckv"][:, :], idx[:, col:col + 1])
            k.idma(kp[:, 512:576], II["cache_kpe"][:, :], idx[:, col:col + 1])

        def stB(pg):
            nn = s * NPG + pg
            kp = Kp[nn % 6]
            kt = KT[nn % 3]
            for rc in range(4):
                k.tr(ptr_bf[:, rc * 128:(rc + 1) * 128], kp[:, rc * 128:(rc + 1) * 128], ident_bf[:, :])
            k.tr(ptr_bf[0:64, 512:640], kp[:, 512:576], ident_bf[:, :])
            k.copy(kt[:, 0:4, :], ptr_bf[:, 0:512].re("p (c t) -> p c t", t=128), e="dve")
            k.copy(kt[0:64, 4, :], ptr_bf[0:64, 512:640], e="dve")

        def stC(pg):
            nn = s * NPG + pg
            kt = KT[nn % 3]
            ps_ = k.psb[nn % 2]
            for rc in range(4):
                k.mm(ps_[:, 0:64], kt[:, rc, :], qlat_s(rc), start=(rc == 0), stop=False)
            k.mm(ps_[:, 0:64], kt[0:64, 4, :], qpe_s, start=False, stop=True)
            k.act(pts[nn % 3][:, :], ps_[:, 0:64], AF.Exp, scale=SM_SCALE)

        def stD(pg):
            nn = s * NPG + pg
            kp = Kp[nn % 6]
            pt = pts[nn % 3]
            for rc in range(4):
                k.mm(po[rc][:, 0:64], kp[:, rc * 128:(rc + 1) * 128], pt[:, :], start=(pg == 0), stop=False)
            k.mm(pl[:, 0:64], c.ones_bf[:, :], pt[:, :], start=(pg == 0), stop=False)

        for it in range(NPG + 3):
            if it < NPG:
                stA(it)
            if 0 <= it - 1 < NPG:
                stB(it - 1)
            if 0 <= it - 2 < NPG:
                stC(it - 2)
            if 0 <= it - 3 < NPG:
                stD(it - 3)
        n = (s + 1) * NPG
        ps_ = k.psb[n % 2]
        n += 1
        for rc in range(4):
            k.mm(ps_[0:64, 0:64], c.ckvT_new[:, rc, :], qlat_s(rc), start=(rc == 0), stop=False)
        k.mm(ps_[0:64, 0:64], c.kpeT_new[0:64, :], qpe_s, start=False, stop=True)
        k.act(pe_[:, :], ps_[0:64, 0:64], AF.Exp, scale=SM_SCALE)
        k.stt(ptm[:, :], pe_[:, :], c.Gs.seqmask[0:64, s:s + 1], cm[0:64, :], ALU.mult, ALU.mult)
        for rc in range(4):
            k.mm(po[rc][:, 0:64], c.ckv_tm_new[0:64, rc * 128:(rc + 1) * 128], ptm[:, :], start=(NPG == 0), stop=True)
        k.mm(pl[:, 0:64], c.ones_bf[0:64, :], ptm[:, :], start=(NPG == 0), stop=True)
        k.recip(linv[:, :], pl[:, 0:64])
        for rc in range(4):
            k.tt(OLs[:, rc, :, qs], po[rc][:, 0:64].re("p (h i) -> p h i", i=tps), linv[:, :].re("p (h i) -> p h i", i=tps),
                 ALU.mult)
    k.ps_rot = [0, 1, 7]
    wuv_t = [k.sb([128, 4, 128], BF16, name="wuvs") for _ in range(2)]
    for h in range(NH):
        wuv = wuv_t[h % 2]
        k.dma(wuv[:, :, :], L.kv_w_uv[:, h * 128:(h + 1) * 128].re("(rc p) v -> p rc v", p=128), q="pool")
        pv = k.ps()
        for rc in range(4):
            k.mm(pv[:, 0:T], wuv[:, rc, :], OLs[:, rc, h, :], start=(rc == 0), stop=(rc == 3))
        k.copy(aoT[:, h, :], pv[:, 0:T], e="act")
    k.ps_rot = list(range(8))
    def cons(oc, p):
        resid_add(k, c, xT, oc, p, T, mods1[:, 32 + oc, mcol], nseq, tps)
    linear_fm(k, ws, L.mla_w_o, 0, NH, 0, D, 512, lambda kc: aoT[:, kc, :], T, cons)
    ffn_phase(k, c, L, xT, T, mods1, mcol, nseq, tps, 1)
    final_phase(k, c, xT, T, modsf, mcol, nseq, tps, out("ys"), 0)
    k.release(m)


def _host_inputs(z, core, cst, needed, shared):
    seq = core % 4
    ins = {n: v for n, v in cst.items() if n in needed}
    sl = slice(core * 16, (core + 1) * 16)
    i = np.arange(128)

    def put(n, fn, share=False):
        if n not in needed:
            return
        if share:
            if n not in shared:
                shared[n] = np.ascontiguousarray(fn())
            ins[n] = shared[n]
        else:
            ins[n] = np.ascontiguousarray(fn())

    def cc():
        a = np.zeros((17, 2048), np.float32)
        a[0] = z["c_prompt"][seq]
        a[1:] = z["c_sample"][sl]
        return a
    put("cc", cc)
    put("xp", lambda: z["x_prompt"][seq])
    put("xs", lambda: z["x_sample"][sl].reshape(64, 2048))
    for l in (0, 1):
        put(f"w_ada{l}", lambda: z["w_ada"][l], True)
        put(f"b_ada{l}", lambda: z["b_ada"][l], True)
        put(f"w_gate_up{l}", lambda: z["w_gate_up"][l], True)
        put(f"w_down{l}", lambda: z["w_down"][l], True)
    put("g_mix", lambda: z["g_mix"].reshape(-1), True)
    put("g_ffn", lambda: z["g_ffn"].reshape(-1), True)
    for n in ("gdn_w_in", "gdn_w_conv", "gdn_a_log", "gdn_dt_bias", "gdn_g_norm", "gdn_w_out", "mla_w_dq", "mla_g_q",
              "mla_w_uq", "mla_w_o"):
        put(n, lambda n=n: z[n][0], True)
    for n in ("kv_w_ada", "kv_b_ada", "kv_g_in", "kv_w_down", "kv_g_norm", "kv_w_uk", "kv_w_uv", "final_w_ada",
              "final_b_ada", "final_g"):
        put(n, lambda n=n: z[n], True)
    if "rope_cos" in needed or "rope_sin" in needed:
        if "rope" not in shared:
            shared["rope"] = rope_tables()
        ins["rope_cos"], ins["rope_sin"] = shared["rope"]
    put("tri", lambda: (i[None, :] >= i[:, None]).astype(np.float32), True)
    put("cache_ckv", lambda: z["cache_ckv"].reshape(-1, 512), True)
    put("cache_kpe", lambda: z["cache_kpe"].reshape(-1, 64), True)
    put("page_table", lambda: z["page_table"][sl].reshape(1, 1024).astype(np.int32))
    put("state_ssm", lambda: z["state_ssm"][0, sl])
    put("state_conv", lambda: z["state_conv"][0, sl].reshape(48, 8192))
    put("iota_p", lambda: i.astype(np.float32).reshape(128, 1), True)
    put("cm", lambda: ((i[:, None] % 4) <= (np.arange(64)[None, :] % 4)).astype(np.float32), True)
    return ins


_PROG = {}


def _get_prog():
    if "k" not in _PROG:
        k, (II, O, cst) = two_pass(lambda kk: build_all(kk))
        _PROG["k"] = k
        _PROG["cst"] = cst
    return _PROG["k"], _PROG["cst"]


def kernel(**inputs):
    from concourse.bass_utils import run_bass_kernel_spmd
    k, cst = _get_prog()
    z = {n: np.asarray(v) for n, v in inputs.items()}
    needed = set(k.dram_in)
    shared = {}
    in_maps = [_host_inputs(z, core, cst, needed, shared) for core in range(8)]
    res = run_bass_kernel_spmd(k.nc, in_maps, core_ids=list(range(8)))
    R = res.results
    f32 = np.float32
    y_prompt = np.stack([R[c]["yp"] for c in range(4)]).astype(f32)
    y_sample = np.concatenate([R[c]["ys"].reshape(16, 4, 2048) for c in range(8)]).astype(f32)
    ssm_prompt = np.stack([R[c]["ssm_p"] for c in range(4)])[None].astype(f32)
    conv_prompt = np.stack([R[c]["conv_p"] for c in range(4)])[None].astype(f32)
    ckv_prompt = np.stack([R[c]["ckv_p"] for c in range(4)]).astype(f32)
    kpe_prompt = np.stack([R[c]["kpe_p"] for c in range(4)]).astype(f32)
    ssm_sample = np.concatenate([R[c]["ssm_s"] for c in range(8)])[None].astype(f32)
    conv_sample = np.concatenate([R[c]["conv_s"].reshape(16, 3, 8192) for c in range(8)])[None].astype(f32)
    ckv_sample = np.concatenate([R[c]["ckv_s"].reshape(16, 4, 512) for c in range(8)]).astype(f32)
    kpe_sample = np.concatenate([R[c]["kpe_s"].reshape(16, 4, 64) for c in range(8)]).astype(f32)
    return (y_prompt, y_sample, ssm_prompt, conv_prompt, ckv_prompt, kpe_prompt,
            ssm_sample, conv_sample, ckv_sample, kpe_sample)
```
